# Optimizing a Trainium2 kernel written in Bass

```python
import jax, jax.numpy as jnp
from jax import lax
import numpy as np

D_MODEL = 2048
BATCH = 8
SEQ = 2048
DEPTH = 2

D_MIX = D_MODEL
RWKV_HEAD_DIM = 64
RWKV_WIDTH = D_MODEL // 2
RWKV_HEADS = RWKV_WIDTH // RWKV_HEAD_DIM
RWKV_DECAY_RANK = 64
RWKV_ICLR_RANK = 64
RWKV_GN_EPS = 64e-5
MOBA_HEAD_DIM = 128
MOBA_WIDTH = D_MODEL // 4
MOBA_HEADS = MOBA_WIDTH // MOBA_HEAD_DIM
MOBA_BLOCK = 256
MOBA_TOPK = 3
MOBA_Q_CHUNK = 16
GLA_WIDTH = D_MODEL // 4
GLA_KEY_WIDTH = GLA_WIDTH // 2
GLA_HEADS = 4
GLA_KEY_DIM = GLA_KEY_WIDTH // GLA_HEADS
GLA_VALUE_DIM = GLA_WIDTH // GLA_HEADS
GLA_GATE_RANK = 16
GLA_GATE_TEMP = 16.0
GLA_CHUNK = 64
NORM_EPS = 1e-6

RWKV_SHIFT_SIZES = (RWKV_WIDTH, RWKV_WIDTH, RWKV_WIDTH, RWKV_DECAY_RANK, RWKV_ICLR_RANK)
RWKV_SHIFT_COLS = 3 * RWKV_WIDTH + RWKV_DECAY_RANK + RWKV_ICLR_RANK
IN_SIZES = (RWKV_SHIFT_COLS, RWKV_WIDTH,
            MOBA_WIDTH, MOBA_WIDTH, MOBA_WIDTH, MOBA_WIDTH,
            GLA_KEY_WIDTH, GLA_KEY_WIDTH, GLA_WIDTH, GLA_WIDTH, GLA_GATE_RANK)
D_IN = RWKV_SHIFT_COLS + RWKV_WIDTH + 4 * MOBA_WIDTH + 2 * GLA_KEY_WIDTH + 2 * GLA_WIDTH + GLA_GATE_RANK

kernel_name = 'hybrid_rwkv7_moba_gla_parallel_heads'


def _split(t, sizes):
    offsets = np.cumsum(np.array(sizes))[:-1].tolist()
    return jnp.split(t, offsets, axis=-1)


def _rmsnorm(x, w):
    xf = x.astype(jnp.float32)
    y = xf * lax.rsqrt(jnp.mean(xf * xf, axis=-1, keepdims=True) + NORM_EPS)
    return (y * w.astype(jnp.float32)).astype(x.dtype)


def _rwkv7_mix(p_shift, g, mu, w0, w2, a0, a2, k_k, k_a, r_k, ln_w, ln_b):
    B, S, _ = p_shift.shape
    H, N = RWKV_HEADS, RWKV_HEAD_DIM
    p = p_shift.astype(jnp.float32)
    prev = jnp.pad(p, ((0, 0), (1, 0), (0, 0)))[:, :-1]
    p = p + mu * (prev - p)
    r, k, v, wd, ad = _split(p, RWKV_SHIFT_SIZES)
    w = w0 + jnp.tanh(wd) @ w2
    w = -jax.nn.softplus(-w) - 0.5
    decay = jnp.exp(-jnp.exp(w))
    a = jax.nn.sigmoid(a0 + ad @ a2)
    r, k, v, decay, a = (t.reshape(B, S, H, N) for t in (r, k, v, decay, a))
    kk = k * k_k.reshape(H, N)
    kk = kk * lax.rsqrt(jnp.maximum(jnp.sum(kk * kk, axis=-1, keepdims=True), 1e-24))
    k = k * (1.0 + (a - 1.0) * k_a.reshape(H, N))
    a_vec = -kk
    b_vec = kk * a

    def step(state, inp):
        r_t, w_t, k_t, v_t, a_t, b_t = inp
        sa = jnp.einsum('bhvk,bhk->bhv', state, a_t)
        state = (state * w_t[:, :, None, :] + sa[..., None] * b_t[:, :, None, :]
                 + v_t[..., None] * k_t[:, :, None, :])
        return state, jnp.einsum('bhvk,bhk->bhv', state, r_t)

    xs = (jnp.moveaxis(r, 1, 0), jnp.moveaxis(decay, 1, 0), jnp.moveaxis(k, 1, 0),
          jnp.moveaxis(v, 1, 0), jnp.moveaxis(a_vec, 1, 0), jnp.moveaxis(b_vec, 1, 0))
    _, y = lax.scan(step, jnp.zeros((B, H, N, N), jnp.float32), xs)
    y = jnp.moveaxis(y, 0, 1)
    mean = jnp.mean(y, axis=-1, keepdims=True)
    var = jnp.mean(jnp.square(y - mean), axis=-1, keepdims=True)
    y = (y - mean) * lax.rsqrt(var + RWKV_GN_EPS) * ln_w.reshape(H, N) + ln_b.reshape(H, N)
    y = y + jnp.sum(r * k * r_k, axis=-1, keepdims=True) * v
    return y.reshape(B, S, RWKV_WIDTH) * jax.nn.silu(g.astype(jnp.float32))


def _moba_mix(q, k, v, g):
    B, S, _ = q.shape
    H, D = MOBA_HEADS, MOBA_HEAD_DIM
    n_blocks = -(-S // MOBA_BLOCK)
    s_pad = n_blocks * MOBA_BLOCK

    def heads(t):
        t = t.astype(jnp.float32).reshape(B, S, H, D).transpose(0, 2, 1, 3)
        return jnp.pad(t, ((0, 0), (0, 0), (0, s_pad - S), (0, 0)))

    qh, kh, vh = heads(q), heads(k), heads(v)
    kb = kh.reshape(B, H, n_blocks, MOBA_BLOCK, D)
    vb = vh.reshape(B, H, n_blocks, MOBA_BLOCK, D)
    k_mean = jnp.mean(kb, axis=3)
    q_block = jnp.arange(s_pad) // MOBA_BLOCK
    past = jnp.arange(n_blocks)[None, :] < q_block[:, None]
    gate = jnp.where(past, jnp.einsum('bhsd,bhnd->bhsn', qh, k_mean), -jnp.inf)
    topk = min(MOBA_TOPK, n_blocks)
    _, sel = lax.top_k(gate, topk)
    sel_valid = sel < q_block[None, None, :, None]
    scale = D ** -0.5
    bi = jnp.arange(B)[:, None, None, None]
    hi = jnp.arange(H)[None, :, None, None]

    def chunk(i):
        start = i * MOBA_Q_CHUNK
        q_c = lax.dynamic_slice_in_dim(qh, start, MOBA_Q_CHUNK, axis=2)
        sel_c = lax.dynamic_slice_in_dim(sel, start, MOBA_Q_CHUNK, axis=2)
        val_c = lax.dynamic_slice_in_dim(sel_valid, start, MOBA_Q_CHUNK, axis=2)
        blk = start // MOBA_BLOCK
        k_own = lax.dynamic_slice_in_dim(kh, blk * MOBA_BLOCK, MOBA_BLOCK, axis=2)
        v_own = lax.dynamic_slice_in_dim(vh, blk * MOBA_BLOCK, MOBA_BLOCK, axis=2)
        k_sel = kb[bi, hi, sel_c]
        v_sel = vb[bi, hi, sel_c]
        s_sel = jnp.einsum('bhqd,bhqnjd->bhqnj', q_c, k_sel) * scale
        s_sel = jnp.where(val_c[..., None], s_sel, -jnp.inf)
        s_sel = s_sel.reshape(B, H, MOBA_Q_CHUNK, topk * MOBA_BLOCK)
        q_pos = start + jnp.arange(MOBA_Q_CHUNK)
        k_pos = blk * MOBA_BLOCK + jnp.arange(MOBA_BLOCK)
        s_own = jnp.einsum('bhqd,bhjd->bhqj', q_c, k_own) * scale
        s_own = jnp.where(k_pos[None, :] <= q_pos[:, None], s_own, -jnp.inf)
        p = jax.nn.softmax(jnp.concatenate([s_sel, s_own], axis=-1), axis=-1)
        p_sel = p[..., :topk * MOBA_BLOCK].reshape(B, H, MOBA_Q_CHUNK, topk, MOBA_BLOCK)
        p_own = p[..., topk * MOBA_BLOCK:]
        return (jnp.einsum('bhqnj,bhqnjd->bhqd', p_sel, v_sel)
                + jnp.einsum('bhqj,bhjd->bhqd', p_own, v_own))

    o = lax.map(chunk, jnp.arange(s_pad // MOBA_Q_CHUNK))
    o = o.transpose(1, 2, 0, 3, 4).reshape(B, H, s_pad, D)[:, :, :S]
    o = o.transpose(0, 2, 1, 3).reshape(B, S, MOBA_WIDTH)
    return o * jax.nn.silu(g.astype(jnp.float32))


def _gla_mix(q, k, v, g, gate_low, a_up, a_b, norm_w):
    B, S, _ = q.shape
    H, DK, DV, C = GLA_HEADS, GLA_KEY_DIM, GLA_VALUE_DIM, GLA_CHUNK
    n = S // C
    log_a = jax.nn.log_sigmoid(gate_low.astype(jnp.float32) @ a_up + a_b) / GLA_GATE_TEMP

    def chunks(t, d):
        return t.astype(jnp.float32).reshape(B, n, C, H, d).transpose(0, 3, 1, 2, 4)

    qc = chunks(q, DK) * (DK ** -0.5)
    kc, vc, la = chunks(k, DK), chunks(v, DV), chunks(log_a, DK)
    b = jnp.cumsum(la, axis=3)
    b_last = b[:, :, :, -1:, :]
    qe = qc * jnp.exp(b)
    ke = kc * jnp.exp(-b)
    kd = kc * jnp.exp(b_last - b)
    causal = jnp.tril(jnp.ones((C, C), dtype=bool))
    att = jnp.where(causal, jnp.einsum('bhncd,bhnjd->bhncj', qe, ke), 0.0)
    o_intra = jnp.einsum('bhncj,bhnje->bhnce', att, vc)
    ds = jnp.einsum('bhncd,bhnce->bhnde', kd, vc)
    dec = jnp.exp(b_last[:, :, :, 0, :])

    def step(state, inp):
        ds_n, dec_n = inp
        return dec_n[..., None] * state + ds_n, state

    _, s_prev = lax.scan(step, jnp.zeros((B, H, DK, DV), jnp.float32),
                         (jnp.moveaxis(ds, 2, 0), jnp.moveaxis(dec, 2, 0)))
    s_prev = jnp.moveaxis(s_prev, 0, 2)
    o = o_intra + jnp.einsum('bhncd,bhnde->bhnce', qe, s_prev)
    o = o.transpose(0, 2, 3, 1, 4).reshape(B, S, H, DV)
    o = o * lax.rsqrt(jnp.mean(o * o, axis=-1, keepdims=True) + NORM_EPS) * norm_w
    return o.reshape(B, S, GLA_WIDTH) * jax.nn.silu(g.astype(jnp.float32))


def setup_inputs(seed: int = 0) -> dict:
    key = jax.random.key(seed)
    ks = jax.random.split(key, 20)
    f32 = jnp.float32
    nrm = lambda k_, shape, s: jax.random.normal(k_, shape, f32) * s
    return {
        'x': nrm(ks[0], (BATCH, SEQ, D_MODEL), 1.0),
        'norm_w': 1.0 + nrm(ks[1], (DEPTH, D_MODEL), 0.02),
        'w_in': nrm(ks[2], (DEPTH, D_MODEL, D_IN), D_MODEL ** -0.5),
        'w_out': nrm(ks[3], (DEPTH, D_MIX, D_MODEL), 0.5 * D_MIX ** -0.5),
        'rwkv_mu': jax.random.uniform(ks[4], (DEPTH, RWKV_SHIFT_COLS), f32),
        'rwkv_w0': jax.random.uniform(ks[5], (DEPTH, RWKV_WIDTH), f32, -4.0, 0.0),
        'rwkv_w2': nrm(ks[6], (DEPTH, RWKV_DECAY_RANK, RWKV_WIDTH), 0.1),
        'rwkv_a0': nrm(ks[7], (DEPTH, RWKV_WIDTH), 0.1),
        'rwkv_a2': nrm(ks[8], (DEPTH, RWKV_ICLR_RANK, RWKV_WIDTH), RWKV_ICLR_RANK ** -0.5),
        'rwkv_k_k': 0.85 + nrm(ks[9], (DEPTH, RWKV_WIDTH), 0.05),
        'rwkv_k_a': 1.0 + nrm(ks[10], (DEPTH, RWKV_WIDTH), 0.05),
        'rwkv_r_k': nrm(ks[11], (DEPTH, RWKV_HEADS, RWKV_HEAD_DIM), 0.1),
        'rwkv_ln_w': 1.0 + nrm(ks[12], (DEPTH, RWKV_WIDTH), 0.02),
        'rwkv_ln_b': nrm(ks[13], (DEPTH, RWKV_WIDTH), 0.02),
        'gla_a_up': nrm(ks[14], (DEPTH, GLA_GATE_RANK, GLA_KEY_WIDTH), GLA_GATE_RANK ** -0.5),
        'gla_a_b': nrm(ks[15], (DEPTH, GLA_KEY_WIDTH), 0.1),
        'gla_norm_w': 1.0 + nrm(ks[16], (DEPTH, GLA_VALUE_DIM), 0.02),
        'final_norm_w': 1.0 + nrm(ks[17], (D_MODEL,), 0.02),
    }


def reference(x, norm_w, w_in, w_out, rwkv_mu, rwkv_w0, rwkv_w2, rwkv_a0, rwkv_a2,
              rwkv_k_k, rwkv_k_a, rwkv_r_k, rwkv_ln_w, rwkv_ln_b,
              gla_a_up, gla_a_b, gla_norm_w, final_norm_w):
    for l in range(DEPTH):
        h = _rmsnorm(x, norm_w[l])
        p = h @ w_in[l]
        (p_rwkv, g_a, q_b, k_b, v_b, g_b,
         q_c, k_c, v_c, g_c, low_c) = _split(p, IN_SIZES)
        y_a = _rwkv7_mix(p_rwkv, g_a, rwkv_mu[l], rwkv_w0[l], rwkv_w2[l], rwkv_a0[l], rwkv_a2[l],
                         rwkv_k_k[l], rwkv_k_a[l], rwkv_r_k[l], rwkv_ln_w[l], rwkv_ln_b[l])
        y_b = _moba_mix(q_b, k_b, v_b, g_b)
        y_c = _gla_mix(q_c, k_c, v_c, g_c, low_c, gla_a_up[l], gla_a_b[l], gla_norm_w[l])
        mix = jnp.concatenate([y_a, y_b, y_c], axis=-1).astype(x.dtype)
        x = x + mix @ w_out[l]
    return _rmsnorm(x, final_norm_w)
```

```python
import os
import numpy as np
import ml_dtypes
from contextlib import ExitStack
import concourse.bass as bass
import concourse.mybir as mybir
from concourse.bass_utils import run_bass_kernel_spmd

F32 = mybir.dt.float32
BF16 = mybir.dt.bfloat16
ALU = mybir.AluOpType
AF = mybir.ActivationFunctionType
AX = mybir.AxisListType

NDSEM = 6
FINE_P = int(os.environ.get("FINE_P", "16"))
FINE_N = int(os.environ.get("FINE_N", "0"))
SW = int(os.environ.get("SW", "1"))
MERGE = int(os.environ.get("MERGE", "1"))
S_ = 2048
D_ = 2048
DIN = 7824
C0 = float(np.exp(-0.5))
NEG = -1.0e30

MU, W0, A0, KK, KA, RK, GAB, GNW, LNW, LNB, NPC = 0, 25, 33, 41, 49, 57, 65, 67, 68, 76, 84


class _Rec:
    def __getattr__(self, name):
        def f(*a, **k):
            self.call = (name, a, k)
            return self
        return f


class Sched:
    ENGS = ("tensor", "vector", "scalar", "gpsimd", "sync")

    def __init__(self, nc, es):
        self.nc = nc
        self.ops = {e: [] for e in self.ENGS}
        self.sem = {e: es.enter_context(nc.semaphore("s_" + e)) for e in self.ENGS}
        self.cnt = {e: 0 for e in self.ENGS}
        self.dsem = {e: [es.enter_context(nc.semaphore("d_%s%d" % (e, i))) for i in range(NDSEM)]
                     for e in ("sync", "gpsimd")}
        self.dcnt = {e: 0 for e in self.dsem}
        self.clock = {e: {} for e in self.ENGS}
        self.lastw = {}
        self.readers = {}
        self.semobj = {}
        self.bar = []
        self.bar_pending = set()
        self.collect = None
        self.vt = {}

    def barrier(self):
        toks = [("c_" + e, self.cnt[e], e) for e in self.ENGS if self.cnt[e] > 0]
        for q in self.dsem:
            n = self.dcnt[q]
            for j in range(NDSEM):
                k = (n - j + NDSEM - 1) // NDSEM if n > j else 0
                if k > 0:
                    toks.append(("d_%s%d" % (q, j), 16 * k, None))
        self.bar = toks
        self.bar_pending = set(self.ENGS)

    def _bar(self, eng, deps):
        if eng in self.bar_pending:
            self.bar_pending.discard(eng)
            deps.extend(self.bar)
        return deps

    def _deps(self, reads, writes):
        deps = []
        for k in reads:
            t = self.lastw.get(k)
            if t is not None:
                deps.append(t)
        for k in writes:
            t = self.lastw.get(k)
            if t is not None:
                deps.append(t)
            deps.extend(self.readers.get(k, ()))
        return deps

    def _commit(self, tok, reads, writes):
        for k in reads:
            self.readers.setdefault(k, []).append(tok)
        for k in writes:
            self.lastw[k] = tok
            self.readers[k] = []

    def _waits(self, eng, deps):
        need = {}
        for (sk, val, src) in deps:
            if src == eng and eng == "tensor":
                continue
            if self.clock[eng].get(sk, 0) >= val:
                continue
            if need.get(sk, 0) < val:
                need[sk] = val
        out = []
        for sk, val in need.items():
            self.clock[eng][sk] = val
            out.append((self.semobj[sk], val))
        return out

    def op(self, eng, fn, reads=(), writes=()):
        if self.collect is not None:
            rec = _Rec()
            fn(rec)
            self.collect.append(("op", eng, rec.call, tuple(reads), tuple(writes)))
            return None
        return self._op(eng, fn, reads, writes)

    def _op(self, eng, fn, reads=(), writes=(), call=None):
        pk = [k for k in reads if k.startswith(("psb", "pproj", "ptr", "pseq"))]
        if pk:
            reads = [k for k in reads if k not in pk]
            writes = list(writes) + pk
        deps = self._bar(eng, self._deps(reads, writes))
        waits = self._waits(eng, deps)
        self.cnt[eng] += 1
        n = self.cnt[eng]
        sem = self.sem[eng]
        sk = "c_" + eng
        self.semobj[sk] = sem
        tok = (sk, n, eng)
        self._commit(tok, reads, writes)
        if call is None:
            rec = _Rec()
            fn(rec)
            call = rec.call

        def emit(e, call=call, waits=waits, sem=sem):
            for (s, v) in waits:
                e.wait_ge(s, v)
            getattr(e, call[0])(*call[1], **call[2]).then_inc(sem, 1)
        self.ops[eng].append(emit)
        return tok

    def dma(self, q, out, in_, reads=(), writes=()):
        if self.collect is not None:
            self.collect.append(("dma", q, (out, in_), tuple(reads), tuple(writes)))
            return None
        deps = self._bar(q, self._deps(reads, writes))
        i = self.dcnt[q]
        self.dcnt[q] += 1
        j = i % NDSEM
        m = i // NDSEM
        sem = self.dsem[q][j]
        sk = "d_%s%d" % (q, j)
        self.semobj[sk] = sem
        if m > 0:
            deps.append((sk, 16 * m, None))
        waits = self._waits(q, deps)
        tok = (sk, 16 * (m + 1), None)
        self._commit(tok, reads, writes)

        def emit(e, waits=waits, sem=sem, out=out, in_=in_):
            for (s, v) in waits:
                e.wait_ge(s, v)
            e.dma_start(out=out, in_=in_).then_inc(sem, 16)
        self.ops[q].append(emit)
        return tok

    @staticmethod
    def _cost(kind, eng, call):
        if kind == "dma":
            return 0.3 if eng == "gpsimd" else 0.08
        args = call[1]
        out = call[2].get("out", args[0] if args else None)
        try:
            n = int(np.prod(out.shape[1:]))
        except Exception:
            n = 128
        if eng == "tensor":
            return 0.14 + n / 2400.0
        if eng == "gpsimd":
            return 0.25 + n / 700.0
        return 0.13 + n / 1000.0

    def merge(self, gens, lat=float(os.environ.get("LAT", "1.5"))):
        streams = []
        for g in gens:
            self.collect = []
            for _ in g:
                pass
            streams.append(self.collect)
            self.collect = None
        ptr = [0] * len(streams)
        ready = [0.0] * len(streams)
        prev_eng = [None] * len(streams)
        free = {e: self.vt.get(e, 0.0) for e in self.ENGS}
        while True:
            best, bt = None, None
            for i, st in enumerate(streams):
                if ptr[i] >= len(st):
                    continue
                kind, eng, call, rd, wr = st[ptr[i]]
                r = ready[i] + (lat if (prev_eng[i] is not None and prev_eng[i] != eng) else 0.0)
                t = max(free[eng], r)
                if bt is None or t < bt:
                    best, bt = i, t
            if best is None:
                break
            kind, eng, call, rd, wr = streams[best][ptr[best]]
            ptr[best] += 1
            c = self._cost(kind, eng, call)
            free[eng] = bt + c
            ready[best] = bt + c
            prev_eng[best] = eng
            if kind == "op":
                self._op(eng, None, rd, wr, call=call)
            else:
                self.dma(eng, call[0], call[1], rd, wr)
        base = max(free.values())
        self.vt = {e: base for e in self.ENGS}

    def finish(self, tokens, eng="sync"):
        waits = self._waits(eng, list(tokens))

        def emit(e, waits=waits):
            for (s, v) in waits:
                e.wait_ge(s, v)
        self.ops[eng].append(emit)

    def run(self):
        nc = self.nc
        with nc.Block() as block:
            @block.tensor
            def _(e):
                for f in self.ops["tensor"]:
                    f(e)

            @block.vector
            def _(e):
                for f in self.ops["vector"]:
                    f(e)

            @block.scalar
            def _(e):
                for f in self.ops["scalar"]:
                    f(e)

            @block.gpsimd
            def _(e):
                for f in self.ops["gpsimd"]:
                    f(e)

            @block.sync
            def _(e):
                for f in self.ops["sync"]:
                    f(e)


def host_consts():
    bf = ml_dtypes.bfloat16
    c = {}
    c["ident"] = np.eye(128, dtype=np.float32).astype(bf)
    i = np.arange(128)
    same = (i[:, None] // 64) == (i[None, :] // 64)
    mU = (same & (i[:, None] < i[None, :])).astype(np.float32)
    mL = (same & (i[:, None] > i[None, :])).astype(np.float32)
    c["mskUL"] = np.concatenate([mU, mL], axis=1).astype(bf)
    j = np.arange(64)
    inc = (j[:, None] <= j[None, :]).astype(np.float32)
    st = (j[:, None] < j[None, :]).astype(np.float32)
    c["mskC"] = np.concatenate([inc, st, inc], axis=1).astype(bf)
    rm = np.ones((128, 512), np.float32)
    rm[:, ::64] = 0.0
    c["rmask"] = rm.astype(bf)
    c["onesBD"] = same.astype(np.float32).astype(bf)
    c["ones64"] = np.full((64, 64), 1.0 / 64, np.float32).astype(bf)
    c["ones64f"] = np.full((128, 64), 1.0 / 64, np.float32).astype(bf)
    c["ones128"] = np.full((128, 128), 1.0 / 128, np.float32).astype(bf)
    cb = np.zeros((128, 2, 256), np.float32)
    q = np.arange(128)
    kpos = np.arange(256)
    cb[:, 0, :] = np.where(kpos[None, :] <= q[:, None], 0.0, NEG)
    cb[:, 1, :] = np.where(kpos[None, :] <= q[:, None] + 128, 0.0, NEG)
    c["cbias"] = cb.astype(bf)
    pb = np.zeros((128, 8, 8), np.float32)
    for b in range(8):
        pb[:, b, b:] = NEG
    c["pbias"] = pb
    return c


def ilv(*gens):
    gens = list(gens)
    while gens:
        for g in list(gens):
            if g not in gens:
                continue
            try:
                next(g)
            except StopIteration:
                gens = [x for x in gens if x is not g]
        yield


def interleave(*gens):
    for _ in ilv(*gens):
        pass


class Arena:
    def __init__(self, base, nwords):
        self.base = base
        self.n = nwords
        self.off = 0
        self.bufs = {}

    def reset(self):
        self.off = 0
        self.bufs = {}

    def get(self, name, shape, dt=F32):
        if name in self.bufs:
            return self.bufs[name]
        nel = int(np.prod(shape[1:]))
        nby = nel * (4 if dt == F32 else 2)
        words = ((nby + 31) // 32) * 8
        assert self.off + words <= self.n, ("arena overflow", name, self.off, words, self.n)
        ap = self.base[0:shape[0], self.off:self.off + words]
        self.off += words
        if dt != F32:
            ap = ap.bitcast(dt)
        ap = ap[:, 0:nel]
        if len(shape) == 3:
            ap = ap.rearrange("p (a b) -> p a b", b=shape[2])
        self.bufs[name] = ap
        return ap


def build(nlayers=2, dbg=False, phases="ARMGD", rl=5, lim=(8, 4, 4)):
    nc = bass.Bass("TRN2", target_bir_lowering=False)
    DI = lambda n, s, dt=F32: nc.dram_tensor(n, s, dt, kind="ExternalInput").ap()
    x_d = DI("x", [S_, D_])
    win_d = DI("w_in", [2, D_, DIN])
    wout_d = DI("w_out", [2, D_, D_])
    pc_d = DI("pc", [2, 128, NPC])
    ph_d = DI("ph", [2, 64, 32])
    nw_d = DI("norm_w", [2, D_])
    fnw_d = DI("final_norm_w", [1, D_])
    w2_d = DI("rwkv_w2", [2, 64, 1024])
    a2_d = DI("rwkv_a2", [2, 64, 1024])
    aup_d = DI("gla_a_up", [2, 16, 256])
    cd = {}
    for n, a in host_consts().items():
        cd[n] = DI("c_" + n, list(a.shape), BF16 if a.dtype == ml_dtypes.bfloat16 else F32)
    out_d = nc.dram_tensor("out", [S_, D_], F32, kind="ExternalOutput").ap()
    xs_d = nc.dram_tensor("xs", [S_, D_], F32, kind="Internal").ap()
    mix_d = nc.dram_tensor("mixd", [16, 128, 16, 128], BF16, kind="Internal").ap()
    dbg_d = nc.dram_tensor("dbgmix", [128, 16, S_], BF16, kind="ExternalOutput").ap() if dbg else None

    with ExitStack() as es:
        S = Sched(nc, es)
        sb = lambda name, shape, dt=F32: es.enter_context(nc.sbuf_tensor("sb_" + name, shape, dt))
        ps = lambda name, shape, dt=F32: es.enter_context(nc.psum_tensor("ps_" + name, shape, dt))
        V = lambda fn, r=(), w=(): S.op("vector", fn, r, w)
        A = lambda fn, r=(), w=(): S.op("scalar", fn, r, w)
        G = lambda fn, r=(), w=(): S.op("gpsimd", fn, r, w)
        T = lambda fn, r=(), w=(): S.op("tensor", fn, r, w)

        hT = sb("hT", [128, 16, S_], BF16)
        ident = sb("ident", [128, 128], BF16)
        mskUL = sb("mskUL", [128, 256], BF16)
        mskC = sb("mskC", [64, 192], BF16)
        rmask = sb("rmask", [128, 512], BF16)
        onesBD = sb("onesBD", [128, 128], BF16)
        ones64 = sb("ones64", [64, 64], BF16)
        ones64f = sb("ones64f", [128, 64], BF16)
        ones128 = sb("ones128", [128, 128], BF16)
        cbias = sb("cbias", [128, 2, 256], BF16)
        pbias = sb("pbias", [128, 8, 8], F32)
        pc = sb("pc", [128, NPC], F32)
        omu = sb("omu", [128, 25], F32)
        ph = sb("ph", [64, 32], F32)
        w2a2 = sb("w2a2", [128, 1024], BF16)
        aupb = sb("aupb", [16, 256], BF16)
        for n, t in (("ident", ident), ("mskUL", mskUL), ("mskC", mskC), ("rmask", rmask), ("onesBD", onesBD),
                     ("ones64", ones64), ("ones64f", ones64f), ("ones128", ones128), ("cbias", cbias), ("pbias", pbias)):
            S.dma("sync", t[:], cd[n], writes=[n])
        epsc = sb("epsc", [128, 2], F32)
        G(lambda e: e.memset(epsc[:, 0:1], 64e-5), [], ["epsc"])
        G(lambda e: e.memset(epsc[:, 1:2], 1e-6), ["epsc"], ["epsc"])
        NW = 10
        wbf = [sb("wbf%d" % i, [128, 16, 128], BF16) for i in range(NW)]
        wctr = [0]
        ARW = ((nc.sbuf_bytes_remaining - 256) // 32) * 8
        print("arena words", ARW, "bytes", ARW * 4)
        arena_t = sb("arena", [128, ARW], F32)
        AR = Arena(arena_t, ARW)

        def B_(name, shape, dt=F32):
            return AR.get(name, shape, dt)

        def phase():
            S.barrier()
            AR.reset()

        def mix_store(src, skey, fc, q):
            S.dma("sync", mix_d[q * 4:(q + 1) * 4, :, fc, :].rearrange("tt p t -> p tt t"),
                  src[:, :].rearrange("p (tt t) -> p tt t", t=128), reads=[skey], writes=["mixd"])

        psb = ps("psb", [128, 2048], F32)
        pproj = [ps("pproj%d" % i, [128, 512], F32) for i in range(2)]
        ptr = ps("ptr", [128, 1024], BF16)
        pseq = ps("pseq", [128, 512], F32)
        pjc = [0]

        def load_w(src_ap, ncols):
            wi = wctr[0] % NW
            wctr[0] += 1
            S.dma("gpsimd", wbf[wi][:, :, 0:ncols], src_ap.rearrange("(c p) n -> p c n", p=128),
                  writes=["wbf%d" % wi])
            return wi

        def load_w_at(src_ap, ncols, wi):
            S.dma("gpsimd", wbf[wi][:, :, 0:ncols], src_ap.rearrange("(c p) n -> p c n", p=128),
                  writes=["wbf%d" % wi])
            return wi

        def proj_fm(wi, ncols, q):
            pi = pjc[0] % 2
            pjc[0] += 1
            pt = pproj[pi]
            for c in range(16):
                T(lambda e, c=c: e.matmul(pt[0:ncols, :], lhsT=wbf[wi][:, c, 0:ncols],
                                          rhs=hT[:, c, q * 512:(q + 1) * 512], start=(c == 0), stop=(c == 15)),
                  ["wbf%d" % wi, "hT"], ["pproj%d" % pi])
            return pt, "pproj%d" % pi

        def proj_gen(wi, ncols, q, out):
            pi = pjc[0] % 2
            pjc[0] += 1
            pt = pproj[pi]
            for c in range(16):
                T(lambda e, c=c: e.matmul(pt[0:ncols, :], lhsT=wbf[wi][:, c, 0:ncols],
                                          rhs=hT[:, c, q * 512:(q + 1) * 512], start=(c == 0), stop=(c == 15)),
                  ["wbf%d" % wi, "hT"], ["pproj%d" % pi])
                if c % FINE_P == FINE_P - 1:
                    yield
            out.append((pt, "pproj%d" % pi))

        def rms_stats(xin, h16, st4, xk="xin", hk="h16", sk="st4"):
            A(lambda e: e.activation(h16[:], xin[:], AF.Square, accum_out=st4[:, 0:1]), [xk], [hk, sk])
            V(lambda e: e.tensor_scalar(st4[:, 1:2], st4[:, 0:1], 1.0 / D_, 1e-6, op0=ALU.mult, op1=ALU.add),
              [sk], [sk])
            A(lambda e: e.activation(st4[:, 2:3], st4[:, 1:2], AF.Sqrt), [sk], [sk])
            V(lambda e: e.reciprocal(st4[:, 3:4], st4[:, 2:3]), [sk], [sk])

        def shift_evac(pt, pk_, mi, q, n, dst, dkey):
            Bb = B_("Bb", [128, 513])
            carry = B_("carry", [128, 4])
            if q == 0:
                G(lambda e: e.memset(Bb[:, 0:1], 0.0), [], ["Bb"])
            else:
                G(lambda e: e.tensor_copy(Bb[:, 0:1], carry[:, n:n + 1]), ["carry"], ["Bb"])
            A(lambda e: e.mul(Bb[:, 1:513], pt[:], pc[:, MU + mi:MU + mi + 1]), [pk_, "pc"], ["Bb"])
            G(lambda e: e.tensor_copy(carry[:, n:n + 1], Bb[:, 512:513]), ["Bb"], ["carry"])
            V(lambda e: e.scalar_tensor_tensor(dst, pt[:], omu[:, mi:mi + 1], Bb[:, 0:512], op0=ALU.mult, op1=ALU.add),
              [pk_, "omu", "Bb"], [dkey])

        sN = lambda i: B_("s%d" % i, [128, 512])

        for l in range(nlayers):
            phase()
            S.dma("sync", pc[:], pc_d[l], writes=["pc"])
            S.dma("sync", ph[:], ph_d[l], writes=["ph"])
            V(lambda e: e.tensor_scalar(omu[:], pc[:, MU:MU + 25], -1.0, 1.0, op0=ALU.mult, op1=ALU.add),
              ["pc"], ["omu"])
            S.dma("gpsimd", w2a2[0:64, :], w2_d[l], writes=["w2a2"])
            S.dma("gpsimd", w2a2[64:128, :], a2_d[l], writes=["w2a2"])
            S.dma("gpsimd", aupb[:], aup_d[l], writes=["aupb"])
            nwb = B_("nwb", [128, D_])
            xin = B_("xin", [128, D_])
            h16 = B_("h16", [128, D_], BF16)
            st4 = B_("st4", [128, 8])
            S.dma("sync", nwb[:], nw_d[l:l + 1, :].partition_broadcast(128), writes=["nwb"])
            xsrc = x_d if l == 0 else xs_d
            for tt in range(16 if "A" in phases else 0):
                xk_, hk_, sk_ = "xin%d" % (tt % 2), "h16%d" % (tt % 2), "st4%d" % (tt % 2)
                xin = B_(xk_, [128, D_])
                h16 = B_(hk_, [128, D_], BF16)
                st4 = B_(sk_, [128, 8])
                S.dma("sync", xin[:], xsrc[tt * 128:(tt + 1) * 128, :], reads=["xs%d" % tt], writes=[xk_])
                rms_stats(xin, h16, st4, xk_, hk_, sk_)
                V(lambda e: e.scalar_tensor_tensor(h16[:], xin[:], st4[:, 3:4], nwb[:], op0=ALU.mult, op1=ALU.mult),
                  [xk_, sk_, "nwb", hk_], [hk_])
                for half in range(2):
                    for c8 in range(8):
                        c = half * 8 + c8
                        T(lambda e, c=c, c8=c8: e.transpose(ptr[:, c8 * 128:(c8 + 1) * 128], h16[:, c * 128:(c + 1) * 128],
                                                            ident[:]),
                          [hk_, "ident"], ["ptr"])
                    dst = hT[:, half * 8:(half + 1) * 8, tt * 128:(tt + 1) * 128]
                    src = ptr[:, :].rearrange("p (c t) -> p c t", t=128)
                    if half == 0:
                        A(lambda e, dst=dst, src=src: e.copy(dst, src), ["ptr"], ["hT"])
                    else:
                        V(lambda e, dst=dst, src=src: e.tensor_copy(dst, src), ["ptr"], ["hT"])

            wl = win_d[l]

            phase()
            if "R" in phases:
                twad = B_("twad", [128, S_], BF16)
                wi = load_w(wl[:, 3072:3200], 128)
                for q in range(4):
                    sl = slice(q * 512, (q + 1) * 512)
                    pt, pk_ = proj_fm(wi, 128, q)
                    xw = B_("s1", [128, 512])
                    shift_evac(pt, pk_, 24, q, 3, xw, "s1")
                    A(lambda e, sl=sl: e.activation(twad[0:64, sl], xw[0:64, :], AF.Tanh), ["s1"], ["twad"])
                    G(lambda e, sl=sl: e.tensor_copy(twad[64:128, sl], xw[64:128, :]), ["s1"], ["twad"])

                QB = []
                for par in range(2):
                    d = {}
                    for n in ("AT", "RT", "BT", "KT", "BH", "KH", "V16", "SG", "BON", "MST"):
                        d[n] = (B_("%s_%d" % (n, par), [128, 512], BF16), "%s_%d" % (n, par))
                    for n in ("ATo", "RTo", "BTo", "KTo"):
                        d[n] = (B_("%s_%d" % (n, par), [64, 512], BF16), "%s_%d" % (n, par))
                    d["YS"] = (B_("YS_%d" % par, [64, 2, 512], BF16), "YS_%d" % par)
                    d["GAM"] = (B_("GAM_%d" % par, [128, 8]), "GAM_%d" % par)
                    d["GAMo"] = (B_("GAMo_%d" % par, [64, 8]), "GAMo_%d" % par)
                    QB.append(d)
                Tst = [B_("Tst%d" % j, [64, 64]) for j in range(2)]
                Tb = [B_("Tb%d" % j, [64, 64], BF16) for j in range(2)]
                PQg = [B_("PQg%d" % i, [128, 4, 256], BF16) for i in range(2)]
                Wtg = [B_("Wtg%d" % i, [128, 4, 128], BF16) for i in range(2)]
                Q5g = B_("Q5g", [128, 4, 128], BF16)
                Wfg = [B_("Wfg%d" % gs, [64, 4, 128], BF16) for gs in range(2)]
                AMg = [[B_("AMg%d_%d" % (gs, e_), [64, 4, 192], BF16) for e_ in range(2)] for gs in range(2)]
                TMg = [B_("TMg%d" % gs, [64, 4, 384], BF16) for gs in range(2)]
                zxs = [B_("zx%d" % j, [64, 64], BF16) for j in range(2)]
                uus = [B_("uu%d" % j, [64, 64], BF16) for j in range(2)]
                eN = lambda i: B_("e%d" % i, [64, 512])
                wsl = {}

                def prep_w(hp, q):
                    if hp == 0 and q == 0:
                        wsl[0] = [load_w(wl[:, o + 0 * 128:o + 1 * 128], 128) for o in (0, 1024, 2048, 3200)]
                    if q == 0 and hp + 1 < lim[0]:
                        h1 = hp + 1
                        wsl[h1] = [load_w(wl[:, o + h1 * 128:o + (h1 + 1) * 128], 128) for o in (0, 1024, 2048, 3200)]

                def prepA1_gen(hp, q, par):
                    prep_w(hp, q)
                    wr, wk, wv, wg = wsl[hp]
                    col = lambda base: pc[:, base + hp:base + hp + 1]
                    o_ = []
                    yield from proj_gen(wk, 128, q, o_)
                    pt, pk_ = o_.pop()
                    xk = sN(2)
                    shift_evac(pt, pk_, 8 + hp, q, 1, xk[:], "s2")
                    yield
                    kk = sN(10)
                    A(lambda e: e.mul(kk[:], xk[:], col(KK)), ["s2", "pc"], ["s10"])
                    s11 = sN(11)
                    kk2b = B_("kk2b", [128, 512], BF16)
                    A(lambda e: e.activation(kk2b[:], kk[:], AF.Square), ["s10"], ["kk2b"])
                    pi_ = pjc[0] % 2
                    pjc[0] += 1
                    pss = pproj[pi_][:, :]
                    T(lambda e: e.matmul(pss, lhsT=onesBD[:], rhs=kk2b[:], start=True, stop=True), ["onesBD", "kk2b"], ["pproj%d" % pi_])
                    V(lambda e: e.tensor_scalar_max(s11[:], pss, 1e-19), ["pproj%d" % pi_], ["s11"])
                    yield
                    yield from proj_gen(wr, 128, q, o_)
                    pt, pk_ = o_.pop()
                    xr = sN(1)
                    shift_evac(pt, pk_, hp, q, 0, xr[:], "s1")
                    yield
                    A(lambda e: e.activation(s11[:], s11[:], AF.Ln), ["s11"], ["s11"])
                    A(lambda e: e.activation(s11[:], s11[:], AF.Exp, scale=-0.5), ["s11"], ["s11"])
                    kkn = sN(12)
                    V(lambda e: e.tensor_tensor(kkn[:], kk[:], s11[:], ALU.mult), ["s10", "s11"], ["s12"])
                    yield

                def prepA2_gen(hp, q, par):
                    col = lambda base: pc[:, base + hp:base + hp + 1]
                    gsl = slice(q * 512, (q + 1) * 512)
                    pw = pseq[:, 0:512]
                    T(lambda e: e.matmul(pw, lhsT=w2a2[0:64, hp * 128:(hp + 1) * 128], rhs=twad[0:64, gsl],
                                         start=True, stop=True), ["w2a2", "twad"], ["pseq"])
                    sig = sN(3)
                    A(lambda e: e.activation(sig[:], pw, AF.Sigmoid, bias=col(W0)), ["pseq", "pc"], ["s3"])
                    yield
                    pa = pseq[:, 0:512]
                    T(lambda e: e.matmul(pa, lhsT=w2a2[64:128, hp * 128:(hp + 1) * 128], rhs=twad[64:128, gsl],
                                         start=True, stop=True), ["w2a2", "twad"], ["pseq"])
                    ai = sN(7)
                    A(lambda e: e.activation(ai[:], pa, AF.Sigmoid, bias=col(A0)), ["pseq", "pc"], ["s7"])
                    yield
                    cs = sN(4)
                    V(lambda e: e.tensor_tensor_scan(cs[:], rmask[:], sig[:], 0.0, ALU.mult, ALU.add), ["rmask", "s3"], ["s4"])
                    cs3 = cs[:].rearrange("p (c t) -> p c t", t=64)
                    eM = sN(6)
                    A(lambda e: e.activation(eM[:], cs[:], AF.Exp, scale=C0), ["s4"], ["s6"])
                    yield
                    s5 = sN(5)
                    V(lambda e: e.tensor_tensor(s5[:].rearrange("p (c t) -> p c t", t=64),
                                                cs3[:, :, 63:64].to_broadcast([128, 8, 64]), cs3, ALU.subtract), ["s4"], ["s5"])
                    eC = sN(8)
                    A(lambda e: e.activation(eC[:], s5[:], AF.Exp, scale=-C0), ["s5"], ["s8"])
                    yield
                    V(lambda e: e.tensor_tensor(s5[:], cs[:], sig[:], ALU.subtract), ["s4", "s3"], ["s5"])
                    A(lambda e: e.activation(s5[:], s5[:], AF.Exp, scale=-C0), ["s5"], ["s5"])
                    yield

                def prepB1_gen(hp, q, par):
                    qb = QB[par]
                    AT, ATk = qb["AT"]; RT, RTk = qb["RT"]; BT, BTk = qb["BT"]; KT, KTk = qb["KT"]
                    BH, BHk = qb["BH"]; KH, KHk = qb["KH"]
                    BON, BONk = qb["BON"]; ATo, ATok = qb["ATo"]; RTo, RTok = qb["RTo"]
                    BTo, BTok = qb["BTo"]; KTo, KTok = qb["KTo"]; GAM, GAMk = qb["GAM"]; GAMo, GAMok = qb["GAMo"]
                    col = lambda base: pc[:, base + hp:base + hp + 1]
                    xr, xk, cs, ePp, eM, eC, kkn, ai = sN(1), sN(2), sN(4), sN(5), sN(6), sN(8), sN(12), sN(7)
                    cs3 = cs[:].rearrange("p (c t) -> p c t", t=64)
                    bv = sN(10)
                    V(lambda e: e.tensor_tensor(bv[:], kkn[:], ai[:], ALU.mult), ["s12", "s7"], ["s10"])
                    t1 = sN(3)
                    V(lambda e: e.tensor_scalar(t1[:], ai[:], -1.0, col(KA), op0=ALU.add, op1=ALU.mult), ["s7", "pc"], ["s3"])
                    k2 = sN(9)
                    V(lambda e: e.scalar_tensor_tensor(k2[:], t1[:], 1.0, xk[:], op0=ALU.add, op1=ALU.mult), ["s3", "s2"], ["s9"])
                    yield
                    ex = sN(13)
                    A(lambda e: e.activation(ex[:], cs[:], AF.Exp, scale=-C0), ["s4"], ["s13"])
                    A(lambda e: e.activation(GAM[:, :], cs3[:, :, 63], AF.Exp, scale=-C0), ["s4"], [GAMk])
                    G(lambda e: e.tensor_copy(GAMo[:, :], GAM[64:128, :]), [GAMk], [GAMok])
                    yield
                    V(lambda e: e.tensor_tensor(RT[:], xr[:], ex[:], ALU.mult), ["s13", "s1"], [RTk])
                    G(lambda e: e.tensor_copy(RTo[:], RT[64:128, :]), [RTk], [RTok])
                    yield
                    V(lambda e: e.scalar_tensor_tensor(AT[:], kkn[:], -1.0, ePp[:], op0=ALU.mult, op1=ALU.mult), ["s12", "s5"], [ATk])
                    G(lambda e: e.tensor_copy(ATo[:], AT[64:128, :]), [ATk], [ATok])
                    yield
                    V(lambda e: e.tensor_tensor(BT[:], bv[:], eM[:], ALU.mult), ["s10", "s6"], [BTk])
                    G(lambda e: e.tensor_copy(BTo[:], BT[64:128, :]), [BTk], [BTok])
                    yield
                    V(lambda e: e.tensor_tensor(BH[:], bv[:], eC[:], ALU.mult), ["s10", "s8"], [BHk])
                    yield
                    V(lambda e: e.tensor_tensor(KT[:], k2[:], eM[:], ALU.mult), ["s9", "s6"], [KTk])
                    G(lambda e: e.tensor_copy(KTo[:], KT[64:128, :]), [KTk], [KTok])
                    yield
                    V(lambda e: e.tensor_tensor(KH[:], k2[:], eC[:], ALU.mult), ["s9", "s8"], [KHk])
                    rk = B_("rkb", [128, 512], BF16)
                    V(lambda e: e.scalar_tensor_tensor(rk[:], xr[:], col(RK), k2[:], op0=ALU.mult, op1=ALU.mult), ["s1", "s9", "pc"], ["rkb"])
                    yield
                    pbs = pseq[:, 0:512]
                    T(lambda e: e.matmul(pbs, lhsT=onesBD[:], rhs=rk[:], start=True, stop=True), ["onesBD", "rkb"], ["pseq"])
                    A(lambda e: e.copy(BON[:], pbs), ["pseq"], [BONk])
                    yield

                def prepB2_gen(hp, q, par):
                    qb = QB[par]
                    V16, V16k = qb["V16"]; SG, SGk = qb["SG"]
                    wr, wk, wv, wg = wsl[hp]
                    o_ = []
                    yield from proj_gen(wv, 128, q, o_)
                    pt, pk_ = o_.pop()
                    shift_evac(pt, pk_, 16 + hp, q, 2, V16[:], V16k)
                    yield
                    yield from proj_gen(wg, 128, q, o_)
                    pt, pk_ = o_.pop()
                    A(lambda e: e.activation(SG[:], pt[:], AF.Silu), [pk_], [SGk])
                    yield


                def ngroup_gen(g, gs, qb):
                    bc4 = lambda ap, n: ap.rearrange("p (o n) -> p o n", o=1).to_broadcast([ap.shape[0], 4 if n is None else n, ap.shape[1]])
                    TM = TMg[gs]
                    tmk = "TMg%d" % gs
                    for half in range(2):
                        for ee in range(2):
                            ci = half * 2 + ee
                            c0 = (2 * g) * 128 + ci * 64
                            for k_, nm in enumerate(("V16", "BH", "KH")):
                                src, kn_ = qb[nm]
                                T(lambda e: e.transpose(ptr[0:64, ee * 384 + k_ * 128: ee * 384 + (k_ + 1) * 128],
                                                        src[:, c0:c0 + 64], ident[:]), [kn_, "ident"], ["ptr"])
                        srcp = ptr[0:64, 0:768].rearrange("p (c n) -> p c n", n=384)
                        if half == 0:
                            A(lambda e: e.copy(TM[:, 0:2, :], srcp), ["ptr"], [tmk])
                        else:
                            V(lambda e: e.tensor_copy(TM[:, 2:4, :], srcp), ["ptr"], [tmk])
                        yield
                    ops = []
                    for c in range(4):
                        p, j = 2 * g + c // 2, c % 2
                        Ab, Abk = qb["AT"] if j == 0 else qb["ATo"]
                        Rb, Rbk = qb["RT"] if j == 0 else qb["RTo"]
                        Bb_, Bbk = qb["BT"] if j == 0 else qb["BTo"]
                        Kb, Kbk = qb["KT"] if j == 0 else qb["KTo"]
                        ops.append((p * 128, Ab, Abk, Rb, Rbk, Bb_, Bbk, Kb, Kbk))
                    bkc = lambda c: "psb%d" % (c // 2)
                    pn = psb[:, 0:1024].rearrange("p (c n) -> p c n", n=256)
                    pq4 = ptr[:, :].bitcast(F32).rearrange("p (c n) -> p c n", n=128)
                    id4 = ident[:, :].rearrange("p (o n) -> p o n", o=1).to_broadcast([128, 4, 128])
                    for c in range(4):
                        t0, Ab, Abk, Rb, Rbk, Bb_, Bbk, Kb, Kbk = ops[c]
                        T(lambda e: e.matmul(pn[:, c, 0:128], lhsT=Bb_[0:64, t0:t0 + 128], rhs=Ab[0:64, t0:t0 + 128],
                                             start=True, stop=True), [Bbk, Abk], [bkc(c)])
                        T(lambda e: e.matmul(pn[:, c, 128:256], lhsT=Ab[0:64, t0:t0 + 128], rhs=Bb_[0:64, t0:t0 + 128],
                                             start=True, stop=True), [Bbk, Abk], [bkc(c)])
                    yield
                    mk2 = mskUL[:, :].rearrange("p (o n) -> p o n", o=1).to_broadcast([128, 2, 256])
                    V(lambda e: e.tensor_tensor(PQg[0][:, 0:2, :], pn[:, 0:2, :], mk2, ALU.mult), ["psb0", "mskUL"], ["PWa0"])
                    V(lambda e: e.tensor_tensor(PQg[0][:, 2:4, :], pn[:, 2:4, :], mk2, ALU.mult), ["psb1", "mskUL"], ["PWb0"])
                    A(lambda e: e.copy(Wtg[0][:], PQg[0][:, :, 128:256]), ["PWa0", "PWb0"], ["Wtg0"])
                    V(lambda e: e.tensor_copy(PQg[0][:, :, 128:256], id4), ["Wtg0", "ident", "PWa0", "PWb0"], ["PWa0", "PWb0"])
                    yield
                    cur = 0
                    for lev in range(5):
                        nxt = 1 - cur
                        PW, Qc = PQg[cur], Wtg[cur]
                        ka, kb, kq = "PWa%d" % cur, "PWb%d" % cur, "Wtg%d" % cur
                        na, nb, nq = "PWa%d" % nxt, "PWb%d" % nxt, "Wtg%d" % nxt
                        for c in range(4):
                            kk_ = ka if c < 2 else kb
                            T(lambda e: e.matmul(pn[:, c, :], lhsT=Qc[:, c, :], rhs=PW[:, c, :], start=True, stop=True),
                              [kk_, kq], [bkc(c)])
                            T(lambda e: e.matmul(pq4[:, c, :], lhsT=PW[:, c, 0:128], rhs=Qc[:, c, :], start=True, stop=True),
                              [kk_, kq], ["ptr"])
                        yield
                        A(lambda e: e.copy(Wtg[nxt][:], pq4), ["ptr"], [nq])
                        V(lambda e: e.tensor_tensor(PQg[nxt][:, 0:2, 128:256], pn[:, 0:2, 128:256], PW[:, 0:2, 128:256], ALU.add),
                          ["psb0", ka], [na])
                        V(lambda e: e.tensor_tensor(PQg[nxt][:, 2:4, 128:256], pn[:, 2:4, 128:256], PW[:, 2:4, 128:256], ALU.add),
                          ["psb1", kb], [nb])
                        A(lambda e: e.copy(PQg[nxt][:, 0:2, 0:128], pn[:, 0:2, 0:128]), ["psb0", na], [na])
                        A(lambda e: e.copy(PQg[nxt][:, 2:4, 0:128], pn[:, 2:4, 0:128]), ["psb1", nb], [nb])
                        yield
                        cur = nxt
                    PW, Qc = PQg[cur], Wtg[cur]
                    ka, kb, kq = "PWa%d" % cur, "PWb%d" % cur, "Wtg%d" % cur
                    pw4 = psb[:, 0:512].rearrange("p (c n) -> p c n", n=128)
                    for c in range(4):
                        T(lambda e: e.matmul(pw4[:, c, :], lhsT=Qc[:, c, :], rhs=PW[:, c, 128:256], start=True, stop=True),
                          [ka, kb, kq], ["psb0"])
                    yield
                    Wf = Wfg[gs]
                    wfk = "Wfg%d" % gs
                    V(lambda e: e.tensor_tensor(Wf[:, :, 0:64], pw4[0:64, :, 0:64], PW[0:64, :, 128:192], ALU.add),
                      ["psb0", ka, kb], [wfk])
                    V(lambda e: e.tensor_tensor(Wf[:, :, 64:128], pw4[64:128, :, 64:128], PW[64:128, :, 192:256], ALU.add),
                      ["psb0", ka, kb, wfk], [wfk])
                    yield

                    mc2 = mskC[:, :].rearrange("p (o n) -> p o n", o=1).to_broadcast([64, 2, 192])
                    for e_ in range(2):
                        for c in range(4):
                            t0, Ab, Abk, Rb, Rbk, Bb_, Bbk, Kb, Kbk = ops[c]
                            c0 = t0 + e_ * 64
                            bnk = 1 if c < 2 else 0
                            pam = psb[0:64, bnk * 512 + (c % 2) * 192: bnk * 512 + (c % 2 + 1) * 192]
                            T(lambda e: e.matmul(pam[:, 0:64], lhsT=Bb_[0:64, c0:c0 + 64], rhs=Rb[0:64, c0:c0 + 64],
                                                 start=True, stop=True), [Bbk, Rbk], ["psb%d" % bnk])
                            T(lambda e: e.matmul(pam[:, 64:128], lhsT=Kb[0:64, c0:c0 + 64], rhs=Ab[0:64, c0:c0 + 64],
                                                 start=True, stop=True), [Kbk, Abk], ["psb%d" % bnk])
                            T(lambda e: e.matmul(pam[:, 128:192], lhsT=Kb[0:64, c0:c0 + 64], rhs=Rb[0:64, c0:c0 + 64],
                                                 start=True, stop=True), [Kbk, Rbk], ["psb%d" % bnk])
                            if FINE_N:
                                yield
                        if not FINE_N:
                            yield
                        am = AMg[gs][e_]
                        amk = "AMg%d_%d" % (gs, e_)
                        V(lambda e: e.tensor_tensor(am[:, 0:2, :], psb[0:64, 512:896].rearrange("p (c n) -> p c n", n=192), mc2, ALU.mult),
                          ["psb1", "mskC"], [amk])
                        V(lambda e: e.tensor_tensor(am[:, 2:4, :], psb[0:64, 0:384].rearrange("p (c n) -> p c n", n=192), mc2, ALU.mult),
                          ["psb0", "mskC", amk], [amk])
                        yield

                def seq_gen(j, p, gs, qb):
                    c = (p % 2) * 2 + j
                    Ab, Abk = qb["AT"] if j == 0 else qb["ATo"]
                    Rb, Rbk = qb["RT"] if j == 0 else qb["RTo"]
                    Gm, Gmk = qb["GAM"] if j == 0 else qb["GAMo"]
                    YS, YSk = qb["YS"]
                    sbank = psb[:, (2 + j) * 512:(3 + j) * 512]
                    sk_ = "psb%d" % (2 + j)
                    zx, uu = zxs[j], uus[j]
                    Wf = Wfg[gs][:, c, :]
                    wfk = "Wfg%d" % gs
                    tmk = "TMg%d" % gs
                    for e_ in range(2):
                        cl = 2 * p + e_
                        c0 = cl * 64
                        am = AMg[gs][e_][:, c, :]
                        amk = "AMg%d_%d" % (gs, e_)
                        tm = TMg[gs][:, (p % 2) * 2 + e_, :]
                        vj = tm[:, 64 * j:64 * j + 64]
                        bhj = tm[:, 128 + 64 * j:128 + 64 * j + 64]
                        khj = tm[:, 256 + 64 * j:256 + 64 * j + 64]
                        pzx = sbank[0:64, 0:64]
                        T(lambda e: e.matmul(pzx, lhsT=am[:, 64:128], rhs=vj, start=True, stop=False), [amk, tmk], [sk_])
                        T(lambda e: e.matmul(pzx, lhsT=Ab[0:64, c0:c0 + 64], rhs=Tb[j][:], start=False, stop=True),
                          [Abk, "Tb%d" % j], [sk_])
                        yield
                        A(lambda e: e.copy(zx[:], pzx), [sk_], ["zx%d" % j])
                        yield
                        pu = sbank[0:64, 64:128]
                        T(lambda e: e.matmul(pu, lhsT=Wf[:, e_ * 64:(e_ + 1) * 64], rhs=zx[:], start=True, stop=True),
                          [wfk, "zx%d" % j], [sk_])
                        yield
                        V(lambda e: e.tensor_copy(uu[:], pu), [sk_], ["uu%d" % j])
                        yield
                        ptn = sbank[0:64, 192:256]
                        T(lambda e: e.matmul(ptn, lhsT=bhj, rhs=uu[:], start=True, stop=False), [tmk, "uu%d" % j], [sk_])
                        T(lambda e: e.matmul(ptn, lhsT=khj, rhs=vj, start=False, stop=True), [tmk], [sk_])
                        py = sbank[0:64, 128:192]
                        T(lambda e: e.matmul(py, lhsT=Tb[j][:], rhs=Rb[0:64, c0:c0 + 64], start=True, stop=False),
                          ["Tb%d" % j, Rbk], [sk_])
                        T(lambda e: e.matmul(py, lhsT=uu[:], rhs=am[:, 0:64], start=False, stop=False), ["uu%d" % j, amk], [sk_])
                        T(lambda e: e.matmul(py, lhsT=vj, rhs=am[:, 128:192], start=False, stop=True), [tmk, amk], [sk_])
                        yield
                        V(lambda e: e.scalar_tensor_tensor(Tb[j][:], Tst[j][:], Gm[0:64, cl:cl + 1], ptn,
                                                           op0=ALU.mult, op1=ALU.add),
                          ["Tst%d" % j, Gmk, sk_], ["Tb%d" % j])
                        V(lambda e: e.scalar_tensor_tensor(Tst[j][:], Tst[j][:], Gm[0:64, cl:cl + 1], ptn,
                                                           op0=ALU.mult, op1=ALU.add),
                          ["Tst%d" % j, Gmk, sk_], ["Tst%d" % j])
                        A(lambda e: e.copy(YS[:, j, c0:c0 + 64], py), [sk_], [YSk])
                        yield


                def epi_gen(hp, q, j, qb):
                    h = 2 * hp + j
                    js = slice(64 * j, 64 * j + 64)
                    YS, YSk = qb["YS"]; BON, BONk = qb["BON"]; V16, V16k = qb["V16"]; SG, SGk = qb["SG"]; MST, MSTk = qb["MST"]
                    pe_ = psb[0:64, 1536 + 256:2048]
                    dd = eN(0)
                    d2 = B_("d2b", [64, 512], BF16)
                    sd = eN(2)
                    for hh in range(2):
                        hs = slice(hh * 256, (hh + 1) * 256)
                        T(lambda e: e.matmul(pe_, lhsT=ones64[:], rhs=YS[:, j, hs], start=True, stop=True), ["ones64", YSk], ["psb3"])
                        V(lambda e: e.scalar_tensor_tensor(dd[:, hs], pe_, -1.0, YS[:, j, hs], op0=ALU.mult, op1=ALU.add), ["psb3", YSk], ["e0"])
                        yield
                    A(lambda e: e.activation(d2[:], dd[:], AF.Square), ["e0"], ["d2b"])
                    for hh in range(2):
                        hs = slice(hh * 256, (hh + 1) * 256)
                        T(lambda e: e.matmul(pe_, lhsT=ones64f[0:64, :], rhs=d2[:, hs], start=True, stop=True), ["ones64f", "d2b"], ["psb3"])
                        A(lambda e: e.activation(sd[:, hs], pe_, AF.Ln, bias=epsc[0:64, 0:1]), ["psb3", "epsc"], ["e2"])
                        yield
                    A(lambda e: e.activation(sd[:], sd[:], AF.Exp, scale=-0.5), ["e2"], ["e2"])
                    V(lambda e: e.tensor_tensor(dd[:], dd[:], sd[:], ALU.mult), ["e0", "e2"], ["e0"])
                    yield
                    A(lambda e: e.activation(dd[:], dd[:], AF.Identity, bias=ph[:, 16 + h:17 + h], scale=ph[:, h:h + 1]),
                      ["e0", "ph"], ["e0"])
                    bvv = eN(1)
                    V(lambda e: e.tensor_tensor(bvv[:], BON[js, :], V16[js, :], ALU.mult), [BONk, V16k], ["e1"])
                    V(lambda e: e.tensor_tensor(dd[:], dd[:], bvv[:], ALU.add), ["e0", "e1"], ["e0"])
                    yield
                    sgs = eN(2)
                    G(lambda e: e.tensor_copy(sgs[:], SG[js, :]), [SGk], ["e2"])
                    V(lambda e: e.tensor_tensor(MST[js, :], dd[:], sgs[:], ALU.mult), ["e0", "e2"], [MSTk])
                    yield
                    if j == 1:
                        mix_store(MST, MSTk, hp, q)


                def sgroup_gen(hp, q, g, gs, par):
                    qb = QB[par]
                    if q == 0 and g == 0:
                        for j in range(2):
                            G(lambda e: e.memset(Tst[j][:], 0.0), [], ["Tst%d" % j])
                            G(lambda e: e.memset(Tb[j][:], 0.0), [], ["Tb%d" % j])
                    for p in (2 * g, 2 * g + 1):
                        yield from ilv(seq_gen(0, p, gs, qb), seq_gen(1, p, gs, qb))
                    if g == 1 and rl >= 5:
                        yield from epi_gen(hp, q, 0, qb)
                        yield from epi_gen(hp, q, 1, qb)

                quarters = [(hp, q) for hp in range(lim[0] if rl >= 2 else 0) for q in range(lim[1])]
                items = [(k, g) for k in range(len(quarters)) for g in range(2)]
                if quarters:
                    S.merge([prepA1_gen(quarters[0][0], quarters[0][1], 0), prepA2_gen(quarters[0][0], quarters[0][1], 0)])
                    S.merge([prepB1_gen(quarters[0][0], quarters[0][1], 0), prepB2_gen(quarters[0][0], quarters[0][1], 0)])
                for i in range(len(items) + 1):
                    gl = []
                    if i < len(items):
                        k, g = items[i]
                        if not os.environ.get("NO_N"):
                            gl.append(ngroup_gen(g, i % 2, QB[k % 2]))
                        if k + 1 < len(quarters):
                            for pg in ((prepA1_gen, prepA2_gen) if g == 0 else (prepB1_gen, prepB2_gen)):
                                gl.append(pg(quarters[k + 1][0], quarters[k + 1][1], (k + 1) % 2))
                    if i >= 1:
                        k, g = items[i - 1]
                        if not os.environ.get("NO_S"):
                            sg = sgroup_gen(quarters[k][0], quarters[k][1], g, (i - 1) % 2, k % 2)
                            gl.append(sg)
                    if MERGE:
                        S.merge(gl)
                    else:
                        interleave(*gl)

                print("rwkv arena used words", AR.off)


            phase()
            def moba_gen():
                if "M" not in phases:
                    return
                yield
                vtok = B_("vtok", [128, 16, 512], BF16)
                wvs = [load_w_at(wl[:, 5248 + i * 128:5248 + (i + 1) * 128], 128, i) for i in range(4)]
                mslots = {0: (4, 5, 0), 1: (1, 2, 3), 2: (4, 5, 0), 3: (1, 2, 3)}
                msrc = lambda h: (wl[:, 4224 + h * 128:4224 + (h + 1) * 128], wl[:, 4736 + h * 128:4736 + (h + 1) * 128],
                                  wl[:, 5760 + h * 128:5760 + (h + 1) * 128])
                load_w_at(msrc(0)[0], 128, 4)
                load_w_at(msrc(0)[1], 128, 5)
                for tt in range(16):
                    pi = pjc[0] % 2
                    pjc[0] += 1
                    for i in range(4):
                        for c in range(16):
                            T(lambda e, c=c, i=i, pi=pi, tt=tt: e.matmul(pproj[pi][:, i * 128:(i + 1) * 128],
                                                                       lhsT=hT[:, c, tt * 128:(tt + 1) * 128],
                                                                       rhs=wbf[wvs[i]][:, c, :], start=(c == 0), stop=(c == 15)),
                              ["hT", "wbf%d" % wvs[i]], ["pproj%d" % pi])
                    A(lambda e, pi=pi, tt=tt: e.copy(vtok[:, tt, :], pproj[pi][:]), ["pproj%d" % pi], ["vtok"])
                    yield
                QT = B_("QT", [128, S_], BF16)
                KTm = B_("KTm", [128, S_], BF16)
                SGm = B_("SGm", [128, S_], BF16)
                ksum = B_("ksum", [128, 8])
                ksb = B_("ksb", [128, 8], BF16)
                Sm = B_("Sm", [128, S_])
                Ex = B_("Ex", [128, S_], BF16)
                ET = B_("ET", [128, 16, 128], BF16)
                gm = B_("gm", [128, 8])
                t8 = B_("t8", [128, 8])
                mx = B_("mx", [128, 4])
                on = B_("on", [128, 128], BF16)
                sc = float(128 ** -0.5)
                load_w_at(msrc(0)[2], 128, 0)
                for k_ in range(3):
                    load_w_at(msrc(1)[k_], 128, mslots[1][k_])
                for h in range(4):
                    wq, wk, wg = mslots[h]
                    for q in range(4):
                        sl = slice(q * 512, (q + 1) * 512)
                        pt, pk_ = proj_fm(wq, 128, q)
                        A(lambda e, pt=pt, sl=sl: e.copy(QT[:, sl], pt[:]), [pk_], ["QT"])
                        pt, pk_ = proj_fm(wk, 128, q)
                        for hb in range(2):
                            n = q * 2 + hb
                            A(lambda e, pt=pt, n=n, hb=hb: e.activation(KTm[:, n * 256:(n + 1) * 256], pt[:, hb * 256:(hb + 1) * 256],
                                                                        AF.Copy, accum_out=ksum[:, n:n + 1]),
                              [pk_], ["KTm", "ksum"])
                        pt, pk_ = proj_fm(wg, 128, q)
                        A(lambda e, pt=pt, sl=sl: e.activation(SGm[:, sl], pt[:], AF.Silu), [pk_], ["SGm"])
                        yield
                    V(lambda e: e.tensor_copy(ksb[:], ksum[:]), ["ksum"], ["ksb"])
                    if h + 2 < 4:
                        for k_ in range(3):
                            load_w_at(msrc(h + 2)[k_], 128, mslots[h + 2][k_])
                    for i in range(16):
                        b = i // 2
                        nk = (b + 1) * 256
                        qs = slice(i * 128, (i + 1) * 128)
                        nkb = (nk + 511) // 512
                        for kb in range(nkb):
                            w_ = min(512, nk - kb * 512)
                            T(lambda e, kb=kb, w_=w_, qs=qs: e.matmul(psb[:, kb * 512:kb * 512 + w_], lhsT=QT[:, qs],
                                                                     rhs=KTm[:, kb * 512:kb * 512 + w_], start=True, stop=True),
                              ["QT", "KTm"], ["psb%d" % kb])
                        allb = ["psb%d" % kb for kb in range(nkb)]
                        if b > 0:
                            pg_ = pseq[:, 384:392]
                            T(lambda e, qs=qs: e.matmul(pg_, lhsT=QT[:, qs], rhs=ksb[:], start=True, stop=True),
                              ["QT", "ksb"], ["pseq"])
                            V(lambda e, b=b: e.tensor_tensor(gm[:], pg_, pbias[:, b, :], ALU.add), ["pseq", "pbias"], ["gm"])
                            V(lambda e: e.max(t8[:], gm[:]), ["gm"], ["t8"])
                            V(lambda e: e.tensor_scalar_max(t8[:, 2:3], t8[:, 2:3], -1e29), ["t8"], ["t8"])
                            V(lambda e: e.tensor_scalar(gm[:], gm[:], t8[:, 2:3], None, op0=ALU.is_ge), ["gm", "t8"], ["gm"])
                            V(lambda e: e.tensor_scalar(gm[:], gm[:], -1.0, 1.0e30, op0=ALU.add, op1=ALU.mult), ["gm"], ["gm"])
                            V(lambda e, b=b: e.tensor_tensor(
                                Sm[:, 0:b * 256].rearrange("p (n t) -> p n t", t=256),
                                psb[:, 0:b * 256].rearrange("p (n t) -> p n t", t=256),
                                gm[:, 0:b].rearrange("p (n o) -> p n o", o=1).to_broadcast([128, b, 256]), ALU.add),
                              allb + ["gm"], ["Sm"])
                        V(lambda e, b=b, i=i: e.tensor_tensor(Sm[:, b * 256:(b + 1) * 256], psb[:, b * 256:(b + 1) * 256],
                                                              cbias[:, i % 2, :], ALU.add), allb + ["cbias"], ["Sm"])
                        yield
                        V(lambda e, nk=nk: e.reduce_max(mx[:, 0:1], Sm[:, 0:nk], AX.X), ["Sm"], ["mx"])
                        V(lambda e: e.tensor_scalar(mx[:, 1:2], mx[:, 0:1], -sc, None, op0=ALU.mult), ["mx"], ["mx"])
                        A(lambda e, nk=nk: e.activation(Ex[:, 0:nk], Sm[:, 0:nk], AF.Exp, bias=mx[:, 1:2], scale=sc,
                                                        accum_out=mx[:, 2:3]), ["Sm", "mx"], ["Ex", "mx"])
                        V(lambda e: e.reciprocal(mx[:, 3:4], mx[:, 2:3]), ["mx"], ["mx"])
                        njt = nk // 128
                        yield
                        for g0 in range(0, njt, 8):
                            g1 = min(njt, g0 + 8)
                            for jt in range(g0, g1):
                                T(lambda e, jt=jt, g0=g0: e.transpose(ptr[:, (jt - g0) * 128:(jt - g0 + 1) * 128],
                                                                      Ex[:, jt * 128:(jt + 1) * 128], ident[:]),
                                  ["Ex", "ident"], ["ptr"])
                            srcp = ptr[:, 0:(g1 - g0) * 128].rearrange("p (j q) -> p j q", q=128)
                            if (g0 // 8) % 2 == 0:
                                V(lambda e, g0=g0, g1=g1, srcp=srcp: e.tensor_copy(ET[:, g0:g1, :], srcp), ["ptr"], ["ET"])
                            else:
                                A(lambda e, g0=g0, g1=g1, srcp=srcp: e.copy(ET[:, g0:g1, :], srcp), ["ptr"], ["ET"])
                            yield
                        po = pseq[:, 128:256]
                        for jt in range(njt):
                            T(lambda e, jt=jt, h=h, njt=njt: e.matmul(po, lhsT=ET[:, jt, :], rhs=vtok[:, jt, h * 128:(h + 1) * 128],
                                                                      start=(jt == 0), stop=(jt == njt - 1)),
                              ["ET", "vtok"], ["pseq"])
                        A(lambda e: e.mul(on[:], po, mx[:, 3:4]), ["pseq", "mx"], ["on"])
                        yield
                        T(lambda e: e.transpose(ptr[:, 0:128], on[:], ident[:]), ["on", "ident"], ["ptr"])
                        mst = B_("mst%d" % (i % 2), [128, 128], BF16)
                        V(lambda e, qs=qs, mst=mst: e.tensor_tensor(mst[:], ptr[:, 0:128], SGm[:, qs], ALU.mult),
                          ["ptr", "SGm"], ["mst%d" % (i % 2)])
                        S.dma("sync", mix_d[i, :, 8 + h, :], mst[:], reads=["mst%d" % (i % 2)], writes=["mixd"])
                        yield


            def gla_gen():
                if "G" not in phases:
                    return
                yield
                gq_, gk_ = {0: 7, 1: 9}, {0: 8, 1: 6}
                gv_, gg_ = {0: 9, 1: 7, 2: 7, 3: 9}, {0: 6, 1: 8, 2: 8, 3: 6}
                srcq = lambda ft: wl[:, 6272 + ft * 128:6272 + (ft + 1) * 128]
                srck = lambda ft: wl[:, 6528 + ft * 128:6528 + (ft + 1) * 128]
                srcv = lambda h: wl[:, 6784 + h * 128:6784 + (h + 1) * 128]
                srcg = lambda h: wl[:, 7296 + h * 128:7296 + (h + 1) * 128]
                wlow = load_w_at(wl[:, 7808:7824], 16, 6)
                load_w_at(srcq(0), 128, gq_[0])
                load_w_at(srck(0), 128, gk_[0])
                load_w_at(srcv(0), 128, gv_[0])
                lowT = B_("lowT", [16, S_], BF16)
                for q in range(4):
                    pt, pk_ = proj_fm(wlow, 16, q)
                    A(lambda e, pt=pt, q=q: e.copy(lowT[:, q * 512:(q + 1) * 512], pt[0:16, :]), [pk_], ["lowT"])
                load_w_at(srcg(0), 128, gg_[0])
                GQ = B_("GQ", [128, S_], BF16)
                GQo = B_("GQo", [64, S_], BF16)
                GKe = B_("GKe", [128, S_], BF16)
                GKd = B_("GKd", [128, S_], BF16)
                gdec = B_("gdec", [128, 32])
                gdeco = B_("gdeco", [64, 32])
                VT = B_("VT", [128, S_], BF16)
                OT = B_("OT", [128, S_])
                SGg = B_("SGg", [128, S_], BF16)
                Sst = B_("Sst", [64, 128])
                Sbf = B_("Sbf", [64, 128], BF16)
                tmg = B_("tmg", [64, 256], BF16)
                att = B_("att", [64, 64], BF16)
                for ft in range(2):
                    wq, wk = gq_[ft], gk_[ft]
                    for q in range(4):
                        sl = slice(q * 512, (q + 1) * 512)
                        pi_ = pjc[0] % 2
                        pjc[0] += 1
                        pla = pproj[pi_][:, :]
                        T(lambda e, sl=sl, ft=ft: e.matmul(pla, lhsT=aupb[:, ft * 128:(ft + 1) * 128], rhs=lowT[:, sl],
                                                           start=True, stop=True), ["aupb", "lowT"], ["pproj%d" % pi_])
                        la = sN(3)
                        A(lambda e, ft=ft: e.activation(la[:], pla, AF.Sigmoid, bias=pc[:, GAB + ft:GAB + ft + 1]),
                          ["pproj%d" % pi_, "pc"], ["s3"])
                        A(lambda e: e.activation(la[:], la[:], AF.Ln), ["s3"], ["s3"])
                        cs = sN(4)
                        V(lambda e: e.tensor_tensor_scan(cs[:], rmask[:], la[:], 0.0, ALU.mult, ALU.add),
                          ["rmask", "s3"], ["s4"])
                        cs3 = cs[:].rearrange("p (c t) -> p c t", t=64)
                        ex = sN(5)
                        A(lambda e: e.activation(ex[:], cs[:], AF.Exp, scale=1.0 / 16), ["s4"], ["s5"])
                        pt, pk_ = proj_fm(wq, 128, q)
                        V(lambda e, pt=pt, sl=sl: e.scalar_tensor_tensor(GQ[:, sl], pt[:], 0.125, ex[:], op0=ALU.mult, op1=ALU.mult),
                          [pk_, "s5"], ["GQ"])
                        G(lambda e, sl=sl: e.tensor_copy(GQo[:, sl], GQ[64:128, sl]), ["GQ"], ["GQo"])
                        eM = sN(6)
                        A(lambda e: e.activation(eM[:], cs[:], AF.Exp, scale=-1.0 / 16), ["s4"], ["s6"])
                        dC = sN(5)
                        V(lambda e: e.tensor_tensor(dC[:].rearrange("p (c t) -> p c t", t=64),
                                                    cs3[:, :, 63:64].to_broadcast([128, 8, 64]), cs3, ALU.subtract),
                          ["s4"], ["s5"])
                        eC = sN(5)
                        A(lambda e: e.activation(eC[:], dC[:], AF.Exp, scale=1.0 / 16), ["s5"], ["s5"])
                        A(lambda e, q=q: e.activation(gdec[:, q * 8:(q + 1) * 8], cs3[:, :, 63], AF.Exp, scale=1.0 / 16),
                          ["s4"], ["gdec"])
                        pt, pk_ = proj_fm(wk, 128, q)
                        kf = sN(9)
                        A(lambda e, pt=pt: e.copy(kf[:], pt[:]), [pk_], ["s9"])
                        V(lambda e, sl=sl: e.tensor_tensor(GKe[:, sl], kf[:], eM[:], ALU.mult), ["s9", "s6"], ["GKe"])
                        V(lambda e, sl=sl: e.tensor_tensor(GKd[:, sl], kf[:], eC[:], ALU.mult), ["s9", "s5"], ["GKd"])
                        yield
                    G(lambda e: e.tensor_copy(gdeco[:, :], gdec[64:128, :]), ["gdec"], ["gdeco"])
                    load_w_at(srcv(2 * ft + 1), 128, gv_[2 * ft + 1])
                    load_w_at(srcg(2 * ft + 1), 128, gg_[2 * ft + 1])
                    for j in range(2):
                        h = ft * 2 + j
                        js = slice(64 * j, 64 * j + 64)
                        wv, wg = gv_[h], gg_[h]
                        for q in range(4):
                            sl = slice(q * 512, (q + 1) * 512)
                            pt, pk_ = proj_fm(wv, 128, q)
                            A(lambda e, pt=pt, sl=sl: e.copy(VT[:, sl], pt[:]), [pk_], ["VT"])
                            pt, pk_ = proj_fm(wg, 128, q)
                            A(lambda e, pt=pt, sl=sl: e.activation(SGg[:, sl], pt[:], AF.Silu), [pk_], ["SGg"])
                            yield
                        if h == 0:
                            load_w_at(srcq(1), 128, gq_[1])
                            load_w_at(srck(1), 128, gk_[1])
                        elif h == 1:
                            load_w_at(srcv(2), 128, gv_[2])
                            load_w_at(srcg(2), 128, gg_[2])
                        G(lambda e: e.memset(Sst[:], 0.0), [], ["Sst"])
                        G(lambda e: e.memset(Sbf[:], 0.0), [], ["Sbf"])
                        Qb, Qbk = (GQ, "GQ") if j == 0 else (GQo, "GQo")
                        Gd, Gdk = (gdec, "gdec") if j == 0 else (gdeco, "gdeco")
                        for c in range(32):
                            c0 = c * 64
                            T(lambda e, c0=c0: e.transpose(ptr[0:64, 0:128], VT[:, c0:c0 + 64], ident[:]), ["VT", "ident"], ["ptr"])
                            T(lambda e, c0=c0: e.transpose(ptr[0:64, 128:256], GKd[:, c0:c0 + 64], ident[:]), ["GKd", "ident"], ["ptr"])
                            A(lambda e: e.copy(tmg[:], ptr[0:64, 0:256]), ["ptr"], ["tmg"])
                            yield
                            pat = pseq[0:64, 0:64]
                            T(lambda e, js=js, c0=c0: e.matmul(pat, lhsT=GKe[js, c0:c0 + 64], rhs=GQ[js, c0:c0 + 64], start=True, stop=True),
                              ["GKe", "GQ"], ["pseq"])
                            V(lambda e: e.tensor_tensor(att[:], pat, mskC[:, 0:64], ALU.mult), ["pseq", "mskC"], ["att"])
                            po = pseq[:, 64:128]
                            yield
                            T(lambda e: e.matmul(po, lhsT=tmg[:, 0:128], rhs=att[:], start=True, stop=False),
                              ["tmg", "att"], ["pseq"])
                            T(lambda e, Qb=Qb, c0=c0: e.matmul(po, lhsT=Sbf[:], rhs=Qb[0:64, c0:c0 + 64], start=False, stop=True),
                              ["Sbf", Qbk], ["pseq"])
                            A(lambda e, c0=c0: e.copy(OT[:, c0:c0 + 64], po), ["pseq"], ["OT"])
                            pst = pseq[0:64, 256:384]
                            T(lambda e, j=j: e.matmul(pst, lhsT=tmg[:, 128 + 64 * j:128 + 64 * j + 64], rhs=tmg[:, 0:128],
                                                      start=True, stop=True), ["tmg"], ["pseq"])
                            V(lambda e, Gd=Gd, c=c: e.scalar_tensor_tensor(Sst[:], Sst[:], Gd[0:64, c:c + 1], pst, op0=ALU.mult,
                                                                           op1=ALU.add), ["Sst", Gdk, "pseq"], ["Sst"])
                            A(lambda e: e.copy(Sbf[:], Sst[:]), ["Sst"], ["Sbf"])
                            yield
                        for q in range(4):
                            sl = slice(q * 512, (q + 1) * 512)
                            o2 = B_("o2b", [128, 512], BF16)
                            A(lambda e, sl=sl: e.activation(o2[:], OT[:, sl], AF.Square), ["OT"], ["o2b"])
                            pi_ = pjc[0] % 2
                            pjc[0] += 1
                            pms = pproj[pi_][:, :]
                            T(lambda e: e.matmul(pms, lhsT=ones128[:], rhs=o2[:], start=True, stop=True),
                              ["ones128", "o2b"], ["pproj%d" % pi_])
                            sd = sN(4)
                            A(lambda e: e.activation(sd[:], pms, AF.Ln, bias=epsc[:, 1:2]), ["pproj%d" % pi_, "epsc"], ["s4"])
                            A(lambda e: e.activation(sd[:], sd[:], AF.Exp, scale=-0.5), ["s4"], ["s4"])
                            V(lambda e, sl=sl: e.scalar_tensor_tensor(sd[:], OT[:, sl], pc[:, GNW:GNW + 1], sd[:], op0=ALU.mult,
                                                                     op1=ALU.mult), ["OT", "pc", "s4"], ["s4"])
                            gms = B_("gms%d" % (q % 2), [128, 512], BF16)
                            V(lambda e, sl=sl, gms=gms: e.tensor_tensor(gms[:], sd[:], SGg[:, sl], ALU.mult),
                              ["s4", "SGg"], ["gms%d" % (q % 2)])
                            mix_store(gms, "gms%d" % (q % 2), 12 + h, q)
                            yield


            interleave(moba_gen(), gla_gen())

            if dbg and l == 0:
                for tt in range(16):
                    S.dma("sync", dbg_d[:, :, tt * 128:(tt + 1) * 128], mix_d[tt], reads=["mixd"], writes=["dbgout"])

            phase()
            wo = wout_d[l]
            for fb in range(16 if "D" in phases else 0):
                S.dma("gpsimd", hT[:, :, fb * 128:(fb + 1) * 128],
                      wo[:, fb * 128:(fb + 1) * 128].rearrange("(c p) n -> p c n", p=128), writes=["hT"])
            last = (l == nlayers - 1)
            nwb = B_("nwb", [128, D_])
            xin = B_("xin", [128, D_])
            h16 = B_("h16", [128, D_], BF16)
            st4 = B_("st4", [128, 8])
            if last:
                S.dma("sync", nwb[:], fnw_d[0:1, :].partition_broadcast(128), writes=["nwb"])
            for tt in range(16 if "D" in phases else 0):
                ts_ = slice(tt * 128, (tt + 1) * 128)
                xk_, hk_, sk_ = "xin%d" % (tt % 2), "h16%d" % (tt % 2), "st4%d" % (tt % 2)
                xin = B_(xk_, [128, D_])
                h16 = B_(hk_, [128, D_], BF16)
                st4 = B_(sk_, [128, 8])
                S.dma("sync", xin[:], xsrc[ts_, :], reads=["xs%d" % tt], writes=[xk_])
                mxt = B_("mxt%d" % (tt % 2), [128, 16, 128], BF16)
                mxk = "mxt%d" % (tt % 2)
                S.dma("sync", mxt[:], mix_d[tt], reads=["mixd"], writes=[mxk])
                for dq in range(4):
                    pi = pjc[0] % 2
                    pjc[0] += 1
                    for fc in range(16):
                        T(lambda e, fc=fc, pi=pi, dq=dq, ts_=ts_: e.matmul(pproj[pi][:], lhsT=mxt[:, fc, :],
                                                                         rhs=hT[:, fc, dq * 512:(dq + 1) * 512],
                                                                         start=(fc == 0), stop=(fc == 15)),
                          [mxk, "hT"], ["pproj%d" % pi])
                    V(lambda e, pi=pi, dq=dq: e.tensor_tensor(xin[:, dq * 512:(dq + 1) * 512], pproj[pi][:],
                                                              xin[:, dq * 512:(dq + 1) * 512], ALU.add),
                      ["pproj%d" % pi, xk_], [xk_])
                if not last:
                    S.dma("sync", xs_d[ts_, :], xin[:], reads=[xk_], writes=["xs%d" % tt])
                else:
                    rms_stats(xin, h16, st4, xk_, hk_, sk_)
                    V(lambda e: e.scalar_tensor_tensor(xin[:], xin[:], st4[:, 3:4], nwb[:], op0=ALU.mult, op1=ALU.mult),
                      [xk_, sk_, "nwb"], [xk_])
                    S.dma("sync", out_d[ts_, :], xin[:], reads=[xk_], writes=["out%d" % tt])

        fin = [S.lastw[k] for k in S.lastw if k.startswith("out") or k == "dbgout"]
        S.finish(fin, "sync")
        S.run()
    return nc


def host_inputs(inp):
    f = lambda a: np.ascontiguousarray(np.asarray(a, dtype=np.float32))
    pcs = np.zeros((2, 128, NPC), np.float32)
    phs = np.zeros((2, 64, 32), np.float32)
    for l in range(2):
        tp = lambda v: f(v).reshape(-1, 128).T
        pcs[l, :, MU:MU + 25] = tp(inp["rwkv_mu"][l])
        pcs[l, :, W0:W0 + 8] = tp(inp["rwkv_w0"][l])
        pcs[l, :, A0:A0 + 8] = tp(inp["rwkv_a0"][l])
        pcs[l, :, KK:KK + 8] = tp(inp["rwkv_k_k"][l])
        pcs[l, :, KA:KA + 8] = tp(inp["rwkv_k_a"][l])
        pcs[l, :, RK:RK + 8] = tp(f(inp["rwkv_r_k"][l]).reshape(-1))
        pcs[l, :, GAB:GAB + 2] = tp(inp["gla_a_b"][l])
        pcs[l, :, GNW:GNW + 1] = tp(inp["gla_norm_w"][l])
        pcs[l, :, LNW:LNW + 8] = tp(inp["rwkv_ln_w"][l])
        pcs[l, :, LNB:LNB + 8] = tp(inp["rwkv_ln_b"][l])
        phs[l, :, 0:16] = f(inp["rwkv_ln_w"][l]).reshape(16, 64).T
        phs[l, :, 16:32] = f(inp["rwkv_ln_b"][l]).reshape(16, 64).T
    common = {
        "w_in": f(inp["w_in"]), "w_out": f(inp["w_out"]), "pc": pcs, "ph": phs,
        "norm_w": f(inp["norm_w"]), "final_norm_w": f(inp["final_norm_w"]).reshape(1, D_),
        "rwkv_w2": f(inp["rwkv_w2"]), "rwkv_a2": f(inp["rwkv_a2"]), "gla_a_up": f(inp["gla_a_up"]),
    }
    for n, a in host_consts().items():
        common["c_" + n] = a
    x = f(inp["x"])
    return [dict(common, x=x[b]) for b in range(x.shape[0])]


_NC = {}


def kernel(**inputs):
    in_maps = host_inputs(inputs)
    if "nc" not in _NC:
        _NC["nc"] = build()
    res = run_bass_kernel_spmd(_NC["nc"], in_maps, core_ids=list(range(8)))
    return np.stack([np.asarray(r["out"], dtype=np.float32) for r in res.results], axis=0)
```

```python
import os
import numpy as np
import ml_dtypes
from contextlib import ExitStack
import concourse.bass as bass
import concourse.mybir as mybir
from concourse.bass_utils import run_bass_kernel_spmd

F32 = mybir.dt.float32
BF16 = mybir.dt.bfloat16
ALU = mybir.AluOpType
AF = mybir.ActivationFunctionType
AX = mybir.AxisListType

NDSEM = 6
FINE_P = int(os.environ.get("FINE_P", "16"))
FINE_N = int(os.environ.get("FINE_N", "0"))
SW = int(os.environ.get("SW", "1"))
MERGE = int(os.environ.get("MERGE", "1"))
S_ = 2048
D_ = 2048
DIN = 7824
C0 = float(np.exp(-0.5))
NEG = -1.0e30

MU, W0, A0, KK, KA, RK, GAB, GNW, LNW, LNB, NPC = 0, 25, 33, 41, 49, 57, 65, 67, 68, 76, 84


class _Rec:
    def __getattr__(self, name):
        def f(*a, **k):
            self.call = (name, a, k)
            return self
        return f


class Sched:
    ENGS = ("tensor", "vector", "scalar", "gpsimd", "sync")

    def __init__(self, nc, es):
        self.nc = nc
        self.ops = {e: [] for e in self.ENGS}
        self.sem = {e: es.enter_context(nc.semaphore("s_" + e)) for e in self.ENGS}
        self.cnt = {e: 0 for e in self.ENGS}
        self.dsem = {e: [es.enter_context(nc.semaphore("d_%s%d" % (e, i))) for i in range(NDSEM)]
                     for e in ("sync", "gpsimd")}
        self.dcnt = {e: 0 for e in self.dsem}
        self.clock = {e: {} for e in self.ENGS}
        self.lastw = {}
        self.readers = {}
        self.semobj = {}
        self.bar = []
        self.bar_pending = set()
        self.collect = None
        self.vt = {}

    def barrier(self):
        toks = [("c_" + e, self.cnt[e], e) for e in self.ENGS if self.cnt[e] > 0]
        for q in self.dsem:
            n = self.dcnt[q]
            for j in range(NDSEM):
                k = (n - j + NDSEM - 1) // NDSEM if n > j else 0
                if k > 0:
                    toks.append(("d_%s%d" % (q, j), 16 * k, None))
        self.bar = toks
        self.bar_pending = set(self.ENGS)

    def _bar(self, eng, deps):
        if eng in self.bar_pending:
            self.bar_pending.discard(eng)
            deps.extend(self.bar)
        return deps

    def _deps(self, reads, writes):
        deps = []
        for k in reads:
            t = self.lastw.get(k)
            if t is not None:
                deps.append(t)
        for k in writes:
            t = self.lastw.get(k)
            if t is not None:
                deps.append(t)
            deps.extend(self.readers.get(k, ()))
        return deps

    def _commit(self, tok, reads, writes):
        for k in reads:
            self.readers.setdefault(k, []).append(tok)
        for k in writes:
            self.lastw[k] = tok
            self.readers[k] = []

    def _waits(self, eng, deps):
        need = {}
        for (sk, val, src) in deps:
            if src == eng and eng == "tensor":
                continue
            if self.clock[eng].get(sk, 0) >= val:
                continue
            if need.get(sk, 0) < val:
                need[sk] = val
        out = []
        for sk, val in need.items():
            self.clock[eng][sk] = val
            out.append((self.semobj[sk], val))
        return out

    def op(self, eng, fn, reads=(), writes=()):
        if self.collect is not None:
            rec = _Rec()
            fn(rec)
            self.collect.append(("op", eng, rec.call, tuple(reads), tuple(writes)))
            return None
        return self._op(eng, fn, reads, writes)

    def _op(self, eng, fn, reads=(), writes=(), call=None):
        pk = [k for k in reads if k.startswith(("psb", "pproj", "ptr", "pseq"))]
        if pk:
            reads = [k for k in reads if k not in pk]
            writes = list(writes) + pk
        deps = self._bar(eng, self._deps(reads, writes))
        waits = self._waits(eng, deps)
        self.cnt[eng] += 1
        n = self.cnt[eng]
        sem = self.sem[eng]
        sk = "c_" + eng
        self.semobj[sk] = sem
        tok = (sk, n, eng)
        self._commit(tok, reads, writes)
        if call is None:
            rec = _Rec()
            fn(rec)
            call = rec.call

        def emit(e, call=call, waits=waits, sem=sem):
            for (s, v) in waits:
                e.wait_ge(s, v)
            getattr(e, call[0])(*call[1], **call[2]).then_inc(sem, 1)
        self.ops[eng].append(emit)
        return tok

    def dma(self, q, out, in_, reads=(), writes=()):
        if self.collect is not None:
            self.collect.append(("dma", q, (out, in_), tuple(reads), tuple(writes)))
            return None
        deps = self._bar(q, self._deps(reads, writes))
        i = self.dcnt[q]
        self.dcnt[q] += 1
        j = i % NDSEM
        m = i // NDSEM
        sem = self.dsem[q][j]
        sk = "d_%s%d" % (q, j)
        self.semobj[sk] = sem
        if m > 0:
            deps.append((sk, 16 * m, None))
        waits = self._waits(q, deps)
        tok = (sk, 16 * (m + 1), None)
        self._commit(tok, reads, writes)

        def emit(e, waits=waits, sem=sem, out=out, in_=in_):
            for (s, v) in waits:
                e.wait_ge(s, v)
            e.dma_start(out=out, in_=in_).then_inc(sem, 16)
        self.ops[q].append(emit)
        return tok

    @staticmethod
    def _cost(kind, eng, call):
        if kind == "dma":
            return 0.3 if eng == "gpsimd" else 0.08
        args = call[1]
        out = call[2].get("out", args[0] if args else None)
        try:
            n = int(np.prod(out.shape[1:]))
        except Exception:
            n = 128
        if eng == "tensor":
            return 0.14 + n / 2400.0
        if eng == "gpsimd":
            return 0.25 + n / 700.0
        return 0.13 + n / 1000.0

    def merge(self, gens, lat=float(os.environ.get("LAT", "1.5"))):
        streams = []
        for g in gens:
            self.collect = []
            for _ in g:
                pass
            streams.append(self.collect)
            self.collect = None
        ptr = [0] * len(streams)
        ready = [0.0] * len(streams)
        prev_eng = [None] * len(streams)
        free = {e: self.vt.get(e, 0.0) for e in self.ENGS}
        while True:
            best, bt = None, None
            for i, st in enumerate(streams):
                if ptr[i] >= len(st):
                    continue
                kind, eng, call, rd, wr = st[ptr[i]]
                r = ready[i] + (lat if (prev_eng[i] is not None and prev_eng[i] != eng) else 0.0)
                t = max(free[eng], r)
                if bt is None or t < bt:
                    best, bt = i, t
            if best is None:
                break
            kind, eng, call, rd, wr = streams[best][ptr[best]]
            ptr[best] += 1
            c = self._cost(kind, eng, call)
            free[eng] = bt + c
            ready[best] = bt + c
            prev_eng[best] = eng
            if kind == "op":
                self._op(eng, None, rd, wr, call=call)
            else:
                self.dma(eng, call[0], call[1], rd, wr)
        base = max(free.values())
        self.vt = {e: base for e in self.ENGS}

    def finish(self, tokens, eng="sync"):
        waits = self._waits(eng, list(tokens))

        def emit(e, waits=waits):
            for (s, v) in waits:
                e.wait_ge(s, v)
        self.ops[eng].append(emit)

    def run(self):
        nc = self.nc
        with nc.Block() as block:
            @block.tensor
            def _(e):
                for f in self.ops["tensor"]:
                    f(e)

            @block.vector
            def _(e):
                for f in self.ops["vector"]:
                    f(e)

            @block.scalar
            def _(e):
                for f in self.ops["scalar"]:
                    f(e)

            @block.gpsimd
            def _(e):
                for f in self.ops["gpsimd"]:
                    f(e)

            @block.sync
            def _(e):
                for f in self.ops["sync"]:
                    f(e)


def host_consts():
    bf = ml_dtypes.bfloat16
    c = {}
    c["ident"] = np.eye(128, dtype=np.float32).astype(bf)
    i = np.arange(128)
    same = (i[:, None] // 64) == (i[None, :] // 64)
    mU = (same & (i[:, None] < i[None, :])).astype(np.float32)
    mL = (same & (i[:, None] > i[None, :])).astype(np.float32)
    c["mskUL"] = np.concatenate([mU, mL], axis=1).astype(bf)
    j = np.arange(64)
    inc = (j[:, None] <= j[None, :]).astype(np.float32)
    st = (j[:, None] < j[None, :]).astype(np.float32)
    c["mskC"] = np.concatenate([inc, st, inc], axis=1).astype(bf)
    rm = np.ones((128, 512), np.float32)
    rm[:, ::64] = 0.0
    c["rmask"] = rm.astype(bf)
    c["onesBD"] = same.astype(np.float32).astype(bf)
    c["ones64"] = np.full((64, 64), 1.0 / 64, np.float32).astype(bf)
    c["ones64f"] = np.full((128, 64), 1.0 / 64, np.float32).astype(bf)
    c["ones128"] = np.full((128, 128), 1.0 / 128, np.float32).astype(bf)
    cb = np.zeros((128, 2, 256), np.float32)
    q = np.arange(128)
    kpos = np.arange(256)
    cb[:, 0, :] = np.where(kpos[None, :] <= q[:, None], 0.0, NEG)
    cb[:, 1, :] = np.where(kpos[None, :] <= q[:, None] + 128, 0.0, NEG)
    c["cbias"] = cb.astype(bf)
    pb = np.zeros((128, 8, 8), np.float32)
    for b in range(8):
        pb[:, b, b:] = NEG
    c["pbias"] = pb
    return c


def ilv(*gens):
    gens = list(gens)
    while gens:
        for g in list(gens):
            if g not in gens:
                continue
            try:
                next(g)
            except StopIteration:
                gens = [x for x in gens if x is not g]
        yield


def interleave(*gens):
    for _ in ilv(*gens):
        pass


class Arena:
    def __init__(self, base, nwords):
        self.base = base
        self.n = nwords
        self.off = 0
        self.bufs = {}

    def reset(self):
        self.off = 0
        self.bufs = {}

    def get(self, name, shape, dt=F32):
        if name in self.bufs:
            return self.bufs[name]
        nel = int(np.prod(shape[1:]))
        nby = nel * (4 if dt == F32 else 2)
        words = ((nby + 31) // 32) * 8
        assert self.off + words <= self.n, ("arena overflow", name, self.off, words, self.n)
        ap = self.base[0:shape[0], self.off:self.off + words]
        self.off += words
        if dt != F32:
            ap = ap.bitcast(dt)
        ap = ap[:, 0:nel]
        if len(shape) == 3:
            ap = ap.rearrange("p (a b) -> p a b", b=shape[2])
        self.bufs[name] = ap
        return ap


def build(nlayers=2, dbg=False, phases="ARMGD", rl=5, lim=(8, 4, 4)):
    nc = bass.Bass("TRN2", target_bir_lowering=False)
    DI = lambda n, s, dt=F32: nc.dram_tensor(n, s, dt, kind="ExternalInput").ap()
    x_d = DI("x", [S_, D_])
    win_d = DI("w_in", [2, D_, DIN])
    wout_d = DI("w_out", [2, D_, D_])
    pc_d = DI("pc", [2, 128, NPC])
    ph_d = DI("ph", [2, 64, 32])
    nw_d = DI("norm_w", [2, D_])
    fnw_d = DI("final_norm_w", [1, D_])
    w2_d = DI("rwkv_w2", [2, 64, 1024])
    a2_d = DI("rwkv_a2", [2, 64, 1024])
    aup_d = DI("gla_a_up", [2, 16, 256])
    cd = {}
    for n, a in host_consts().items():
        cd[n] = DI("c_" + n, list(a.shape), BF16 if a.dtype == ml_dtypes.bfloat16 else F32)
    out_d = nc.dram_tensor("out", [S_, D_], F32, kind="ExternalOutput").ap()
    xs_d = nc.dram_tensor("xs", [S_, D_], F32, kind="Internal").ap()
    mix_d = nc.dram_tensor("mixd", [16, 128, 16, 128], BF16, kind="Internal").ap()
    dbg_d = nc.dram_tensor("dbgmix", [128, 16, S_], BF16, kind="ExternalOutput").ap() if dbg else None

    with ExitStack() as es:
        S = Sched(nc, es)
        sb = lambda name, shape, dt=F32: es.enter_context(nc.sbuf_tensor("sb_" + name, shape, dt))
        ps = lambda name, shape, dt=F32: es.enter_context(nc.psum_tensor("ps_" + name, shape, dt))
        V = lambda fn, r=(), w=(): S.op("vector", fn, r, w)
        A = lambda fn, r=(), w=(): S.op("scalar", fn, r, w)
        G = lambda fn, r=(), w=(): S.op("gpsimd", fn, r, w)
        T = lambda fn, r=(), w=(): S.op("tensor", fn, r, w)

        hT = sb("hT", [128, 16, S_], BF16)
        ident = sb("ident", [128, 128], BF16)
        mskUL = sb("mskUL", [128, 256], BF16)
        mskC = sb("mskC", [64, 192], BF16)
        rmask = sb("rmask", [128, 512], BF16)
        onesBD = sb("onesBD", [128, 128], BF16)
        ones64 = sb("ones64", [64, 64], BF16)
        ones64f = sb("ones64f", [128, 64], BF16)
        ones128 = sb("ones128", [128, 128], BF16)
        cbias = sb("cbias", [128, 2, 256], BF16)
        pbias = sb("pbias", [128, 8, 8], F32)
        pc = sb("pc", [128, NPC], F32)
        omu = sb("omu", [128, 25], F32)
        ph = sb("ph", [64, 32], F32)
        w2a2 = sb("w2a2", [128, 1024], BF16)
        aupb = sb("aupb", [16, 256], BF16)
        for n, t in (("ident", ident), ("mskUL", mskUL), ("mskC", mskC), ("rmask", rmask), ("onesBD", onesBD),
                     ("ones64", ones64), ("ones64f", ones64f), ("ones128", ones128), ("cbias", cbias), ("pbias", pbias)):
            S.dma("sync", t[:], cd[n], writes=[n])
        epsc = sb("epsc", [128, 2], F32)
        G(lambda e: e.memset(epsc[:, 0:1], 64e-5), [], ["epsc"])
        G(lambda e: e.memset(epsc[:, 1:2], 1e-6), ["epsc"], ["epsc"])
        NW = 10
        wbf = [sb("wbf%d" % i, [128, 16, 128], BF16) for i in range(NW)]
        wctr = [0]
        ARW = ((nc.sbuf_bytes_remaining - 256) // 32) * 8
        print("arena words", ARW, "bytes", ARW * 4)
        arena_t = sb("arena", [128, ARW], F32)
        AR = Arena(arena_t, ARW)

        def B_(name, shape, dt=F32):
            return AR.get(name, shape, dt)

        def phase():
            S.barrier()
            AR.reset()

        def mix_store(src, skey, fc, q):
            S.dma("sync", mix_d[q * 4:(q + 1) * 4, :, fc, :].rearrange("tt p t -> p tt t"),
                  src[:, :].rearrange("p (tt t) -> p tt t", t=128), reads=[skey], writes=["mixd"])

        psb = ps("psb", [128, 2048], F32)
        pproj = [ps("pproj%d" % i, [128, 512], F32) for i in range(2)]
        ptr = ps("ptr", [128, 1024], BF16)
        pseq = ps("pseq", [128, 512], F32)
        pjc = [0]

        def load_w(src_ap, ncols):
            wi = wctr[0] % NW
            wctr[0] += 1
            S.dma("gpsimd", wbf[wi][:, :, 0:ncols], src_ap.rearrange("(c p) n -> p c n", p=128),
                  writes=["wbf%d" % wi])
            return wi

        def load_w_at(src_ap, ncols, wi):
            S.dma("gpsimd", wbf[wi][:, :, 0:ncols], src_ap.rearrange("(c p) n -> p c n", p=128),
                  writes=["wbf%d" % wi])
            return wi

        def proj_fm(wi, ncols, q):
            pi = pjc[0] % 2
            pjc[0] += 1
            pt = pproj[pi]
            for c in range(16):
                T(lambda e, c=c: e.matmul(pt[0:ncols, :], lhsT=wbf[wi][:, c, 0:ncols],
                                          rhs=hT[:, c, q * 512:(q + 1) * 512], start=(c == 0), stop=(c == 15)),
                  ["wbf%d" % wi, "hT"], ["pproj%d" % pi])
            return pt, "pproj%d" % pi

        def proj_gen(wi, ncols, q, out):
            pi = pjc[0] % 2
            pjc[0] += 1
            pt = pproj[pi]
            for c in range(16):
                T(lambda e, c=c: e.matmul(pt[0:ncols, :], lhsT=wbf[wi][:, c, 0:ncols],
                                          rhs=hT[:, c, q * 512:(q + 1) * 512], start=(c == 0), stop=(c == 15)),
                  ["wbf%d" % wi, "hT"], ["pproj%d" % pi])
                if c % FINE_P == FINE_P - 1:
                    yield
            out.append((pt, "pproj%d" % pi))

        def rms_stats(xin, h16, st4, xk="xin", hk="h16", sk="st4"):
            A(lambda e: e.activation(h16[:], xin[:], AF.Square, accum_out=st4[:, 0:1]), [xk], [hk, sk])
            V(lambda e: e.tensor_scalar(st4[:, 1:2], st4[:, 0:1], 1.0 / D_, 1e-6, op0=ALU.mult, op1=ALU.add),
              [sk], [sk])
            A(lambda e: e.activation(st4[:, 2:3], st4[:, 1:2], AF.Sqrt), [sk], [sk])
            V(lambda e: e.reciprocal(st4[:, 3:4], st4[:, 2:3]), [sk], [sk])

        def shift_evac(pt, pk_, mi, q, n, dst, dkey):
            Bb = B_("Bb", [128, 513])
            carry = B_("carry", [128, 4])
            if q == 0:
                G(lambda e: e.memset(Bb[:, 0:1], 0.0), [], ["Bb"])
            else:
                G(lambda e: e.tensor_copy(Bb[:, 0:1], carry[:, n:n + 1]), ["carry"], ["Bb"])
            A(lambda e: e.mul(Bb[:, 1:513], pt[:], pc[:, MU + mi:MU + mi + 1]), [pk_, "pc"], ["Bb"])
            G(lambda e: e.tensor_copy(carry[:, n:n + 1], Bb[:, 512:513]), ["Bb"], ["carry"])
            V(lambda e: e.scalar_tensor_tensor(dst, pt[:], omu[:, mi:mi + 1], Bb[:, 0:512], op0=ALU.mult, op1=ALU.add),
              [pk_, "omu", "Bb"], [dkey])

        sN = lambda i: B_("s%d" % i, [128, 512])

        for l in range(nlayers):
            phase()
            S.dma("sync", pc[:], pc_d[l], writes=["pc"])
            S.dma("sync", ph[:], ph_d[l], writes=["ph"])
            V(lambda e: e.tensor_scalar(omu[:], pc[:, MU:MU + 25], -1.0, 1.0, op0=ALU.mult, op1=ALU.add),
              ["pc"], ["omu"])
            S.dma("gpsimd", w2a2[0:64, :], w2_d[l], writes=["w2a2"])
            S.dma("gpsimd", w2a2[64:128, :], a2_d[l], writes=["w2a2"])
            S.dma("gpsimd", aupb[:], aup_d[l], writes=["aupb"])
            nwb = B_("nwb", [128, D_])
            xin = B_("xin", [128, D_])
            h16 = B_("h16", [128, D_], BF16)
            st4 = B_("st4", [128, 8])
            S.dma("sync", nwb[:], nw_d[l:l + 1, :].partition_broadcast(128), writes=["nwb"])
            xsrc = x_d if l == 0 else xs_d
            for tt in range(16 if "A" in phases else 0):
                xk_, hk_, sk_ = "xin%d" % (tt % 2), "h16%d" % (tt % 2), "st4%d" % (tt % 2)
                xin = B_(xk_, [128, D_])
                h16 = B_(hk_, [128, D_], BF16)
                st4 = B_(sk_, [128, 8])
                S.dma("sync", xin[:], xsrc[tt * 128:(tt + 1) * 128, :], reads=["xs%d" % tt], writes=[xk_])
                rms_stats(xin, h16, st4, xk_, hk_, sk_)
                V(lambda e: e.scalar_tensor_tensor(h16[:], xin[:], st4[:, 3:4], nwb[:], op0=ALU.mult, op1=ALU.mult),
                  [xk_, sk_, "nwb", hk_], [hk_])
                for half in range(2):
                    for c8 in range(8):
                        c = half * 8 + c8
                        T(lambda e, c=c, c8=c8: e.transpose(ptr[:, c8 * 128:(c8 + 1) * 128], h16[:, c * 128:(c + 1) * 128],
                                                            ident[:]),
                          [hk_, "ident"], ["ptr"])
                    dst = hT[:, half * 8:(half + 1) * 8, tt * 128:(tt + 1) * 128]
                    src = ptr[:, :].rearrange("p (c t) -> p c t", t=128)
                    if half == 0:
                        A(lambda e, dst=dst, src=src: e.copy(dst, src), ["ptr"], ["hT"])
                    else:
                        V(lambda e, dst=dst, src=src: e.tensor_copy(dst, src), ["ptr"], ["hT"])

            wl = win_d[l]

            phase()
            if "R" in phases:
                twad = B_("twad", [128, S_], BF16)
                wi = load_w(wl[:, 3072:3200], 128)
                for q in range(4):
                    sl = slice(q * 512, (q + 1) * 512)
                    pt, pk_ = proj_fm(wi, 128, q)
                    xw = B_("s1", [128, 512])
                    shift_evac(pt, pk_, 24, q, 3, xw, "s1")
                    A(lambda e, sl=sl: e.activation(twad[0:64, sl], xw[0:64, :], AF.Tanh), ["s1"], ["twad"])
                    G(lambda e, sl=sl: e.tensor_copy(twad[64:128, sl], xw[64:128, :]), ["s1"], ["twad"])

                QB = []
                for par in range(2):
                    d = {}
                    for n in ("AT", "RT", "BT", "KT", "BH", "KH", "V16", "SG", "BON", "MST"):
                        d[n] = (B_("%s_%d" % (n, par), [128, 512], BF16), "%s_%d" % (n, par))
                    for n in ("ATo", "RTo", "BTo", "KTo"):
                        d[n] = (B_("%s_%d" % (n, par), [64, 512], BF16), "%s_%d" % (n, par))
                    d["YS"] = (B_("YS_%d" % par, [64, 2, 512], BF16), "YS_%d" % par)
                    d["GAM"] = (B_("GAM_%d" % par, [128, 8]), "GAM_%d" % par)
                    d["GAMo"] = (B_("GAMo_%d" % par, [64, 8]), "GAMo_%d" % par)
                    QB.append(d)
                Tst = [B_("Tst%d" % j, [64, 64]) for j in range(2)]
                Tb = [B_("Tb%d" % j, [64, 64], BF16) for j in range(2)]
                PQg = [B_("PQg%d" % i, [128, 4, 256], BF16) for i in range(2)]
                Wtg = [B_("Wtg%d" % i, [128, 4, 128], BF16) for i in range(2)]
                Q5g = B_("Q5g", [128, 4, 128], BF16)
                Wfg = [B_("Wfg%d" % gs, [64, 4, 128], BF16) for gs in range(2)]
                AMg = [[B_("AMg%d_%d" % (gs, e_), [64, 4, 192], BF16) for e_ in range(2)] for gs in range(2)]
                TMg = [B_("TMg%d" % gs, [64, 4, 384], BF16) for gs in range(2)]
                zxs = [B_("zx%d" % j, [64, 64], BF16) for j in range(2)]
                uus = [B_("uu%d" % j, [64, 64], BF16) for j in range(2)]
                eN = lambda i: B_("e%d" % i, [64, 512])
                wsl = {}

                def prep_w(hp, q):
                    if hp == 0 and q == 0:
                        wsl[0] = [load_w(wl[:, o + 0 * 128:o + 1 * 128], 128) for o in (0, 1024, 2048, 3200)]
                    if q == 0 and hp + 1 < lim[0]:
                        h1 = hp + 1
                        wsl[h1] = [load_w(wl[:, o + h1 * 128:o + (h1 + 1) * 128], 128) for o in (0, 1024, 2048, 3200)]

                def prepA1_gen(hp, q, par):
                    prep_w(hp, q)
                    wr, wk, wv, wg = wsl[hp]
                    col = lambda base: pc[:, base + hp:base + hp + 1]
                    o_ = []
                    yield from proj_gen(wk, 128, q, o_)
                    pt, pk_ = o_.pop()
                    xk = sN(2)
                    shift_evac(pt, pk_, 8 + hp, q, 1, xk[:], "s2")
                    yield
                    kk = sN(10)
                    A(lambda e: e.mul(kk[:], xk[:], col(KK)), ["s2", "pc"], ["s10"])
                    s11 = sN(11)
                    kk2b = B_("kk2b", [128, 512], BF16)
                    A(lambda e: e.activation(kk2b[:], kk[:], AF.Square), ["s10"], ["kk2b"])
                    pi_ = pjc[0] % 2
                    pjc[0] += 1
                    pss = pproj[pi_][:, :]
                    T(lambda e: e.matmul(pss, lhsT=onesBD[:], rhs=kk2b[:], start=True, stop=True), ["onesBD", "kk2b"], ["pproj%d" % pi_])
                    V(lambda e: e.tensor_scalar_max(s11[:], pss, 1e-19), ["pproj%d" % pi_], ["s11"])
                    yield
                    yield from proj_gen(wr, 128, q, o_)
                    pt, pk_ = o_.pop()
                    xr = sN(1)
                    shift_evac(pt, pk_, hp, q, 0, xr[:], "s1")
                    yield
                    A(lambda e: e.activation(s11[:], s11[:], AF.Ln), ["s11"], ["s11"])
                    A(lambda e: e.activation(s11[:], s11[:], AF.Exp, scale=-0.5), ["s11"], ["s11"])
                    kkn = sN(12)
                    V(lambda e: e.tensor_tensor(kkn[:], kk[:], s11[:], ALU.mult), ["s10", "s11"], ["s12"])
                    yield

                def prepA2_gen(hp, q, par):
                    col = lambda base: pc[:, base + hp:base + hp + 1]
                    gsl = slice(q * 512, (q + 1) * 512)
                    pw = pseq[:, 0:512]
                    T(lambda e: e.matmul(pw, lhsT=w2a2[0:64, hp * 128:(hp + 1) * 128], rhs=twad[0:64, gsl],
                                         start=True, stop=True), ["w2a2", "twad"], ["pseq"])
                    sig = sN(3)
                    A(lambda e: e.activation(sig[:], pw, AF.Sigmoid, bias=col(W0)), ["pseq", "pc"], ["s3"])
                    yield
                    pa = pseq[:, 0:512]
                    T(lambda e: e.matmul(pa, lhsT=w2a2[64:128, hp * 128:(hp + 1) * 128], rhs=twad[64:128, gsl],
                                         start=True, stop=True), ["w2a2", "twad"], ["pseq"])
                    ai = sN(7)
                    A(lambda e: e.activation(ai[:], pa, AF.Sigmoid, bias=col(A0)), ["pseq", "pc"], ["s7"])
                    yield
                    cs = sN(4)
                    V(lambda e: e.tensor_tensor_scan(cs[:], rmask[:], sig[:], 0.0, ALU.mult, ALU.add), ["rmask", "s3"], ["s4"])
                    cs3 = cs[:].rearrange("p (c t) -> p c t", t=64)
                    eM = sN(6)
                    A(lambda e: e.activation(eM[:], cs[:], AF.Exp, scale=C0), ["s4"], ["s6"])
                    yield
                    s5 = sN(5)
                    V(lambda e: e.tensor_tensor(s5[:].rearrange("p (c t) -> p c t", t=64),
                                                cs3[:, :, 63:64].to_broadcast([128, 8, 64]), cs3, ALU.subtract), ["s4"], ["s5"])
                    eC = sN(8)
                    A(lambda e: e.activation(eC[:], s5[:], AF.Exp, scale=-C0), ["s5"], ["s8"])
                    yield
                    V(lambda e: e.tensor_tensor(s5[:], cs[:], sig[:], ALU.subtract), ["s4", "s3"], ["s5"])
                    A(lambda e: e.activation(s5[:], s5[:], AF.Exp, scale=-C0), ["s5"], ["s5"])
                    yield

                def prepB1_gen(hp, q, par):
                    qb = QB[par]
                    AT, ATk = qb["AT"]; RT, RTk = qb["RT"]; BT, BTk = qb["BT"]; KT, KTk = qb["KT"]
                    BH, BHk = qb["BH"]; KH, KHk = qb["KH"]
                    BON, BONk = qb["BON"]; ATo, ATok = qb["ATo"]; RTo, RTok = qb["RTo"]
                    BTo, BTok = qb["BTo"]; KTo, KTok = qb["KTo"]; GAM, GAMk = qb["GAM"]; GAMo, GAMok = qb["GAMo"]
                    col = lambda base: pc[:, base + hp:base + hp + 1]
                    xr, xk, cs, ePp, eM, eC, kkn, ai = sN(1), sN(2), sN(4), sN(5), sN(6), sN(8), sN(12), sN(7)
                    cs3 = cs[:].rearrange("p (c t) -> p c t", t=64)
                    bv = sN(10)
                    V(lambda e: e.tensor_tensor(bv[:], kkn[:], ai[:], ALU.mult), ["s12", "s7"], ["s10"])
                    t1 = sN(3)
                    V(lambda e: e.tensor_scalar(t1[:], ai[:], -1.0, col(KA), op0=ALU.add, op1=ALU.mult), ["s7", "pc"], ["s3"])
                    k2 = sN(9)
                    V(lambda e: e.scalar_tensor_tensor(k2[:], t1[:], 1.0, xk[:], op0=ALU.add, op1=ALU.mult), ["s3", "s2"], ["s9"])
                    yield
                    ex = sN(13)
                    A(lambda e: e.activation(ex[:], cs[:], AF.Exp, scale=-C0), ["s4"], ["s13"])
                    A(lambda e: e.activation(GAM[:, :], cs3[:, :, 63], AF.Exp, scale=-C0), ["s4"], [GAMk])
                    G(lambda e: e.tensor_copy(GAMo[:, :], GAM[64:128, :]), [GAMk], [GAMok])
                    yield
                    V(lambda e: e.tensor_tensor(RT[:], xr[:], ex[:], ALU.mult), ["s13", "s1"], [RTk])
                    G(lambda e: e.tensor_copy(RTo[:], RT[64:128, :]), [RTk], [RTok])
                    yield
                    V(lambda e: e.scalar_tensor_tensor(AT[:], kkn[:], -1.0, ePp[:], op0=ALU.mult, op1=ALU.mult), ["s12", "s5"], [ATk])
                    G(lambda e: e.tensor_copy(ATo[:], AT[64:128, :]), [ATk], [ATok])
                    yield
                    V(lambda e: e.tensor_tensor(BT[:], bv[:], eM[:], ALU.mult), ["s10", "s6"], [BTk])
                    G(lambda e: e.tensor_copy(BTo[:], BT[64:128, :]), [BTk], [BTok])
                    yield
                    V(lambda e: e.tensor_tensor(BH[:], bv[:], eC[:], ALU.mult), ["s10", "s8"], [BHk])
                    yield
                    V(lambda e: e.tensor_tensor(KT[:], k2[:], eM[:], ALU.mult), ["s9", "s6"], [KTk])
                    G(lambda e: e.tensor_copy(KTo[:], KT[64:128, :]), [KTk], [KTok])
                    yield
                    V(lambda e: e.tensor_tensor(KH[:], k2[:], eC[:], ALU.mult), ["s9", "s8"], [KHk])
                    rk = B_("rkb", [128, 512], BF16)
                    V(lambda e: e.scalar_tensor_tensor(rk[:], xr[:], col(RK), k2[:], op0=ALU.mult, op1=ALU.mult), ["s1", "s9", "pc"], ["rkb"])
                    yield
                    pbs = pseq[:, 0:512]
                    T(lambda e: e.matmul(pbs, lhsT=onesBD[:], rhs=rk[:], start=True, stop=True), ["onesBD", "rkb"], ["pseq"])
                    A(lambda e: e.copy(BON[:], pbs), ["pseq"], [BONk])
                    yield

                def prepB2_gen(hp, q, par):
                    qb = QB[par]
                    V16, V16k = qb["V16"]; SG, SGk = qb["SG"]
                    wr, wk, wv, wg = wsl[hp]
                    o_ = []
                    yield from proj_gen(wv, 128, q, o_)
                    pt, pk_ = o_.pop()
                    shift_evac(pt, pk_, 16 + hp, q, 2, V16[:], V16k)
                    yield
                    yield from proj_gen(wg, 128, q, o_)
                    pt, pk_ = o_.pop()
                    A(lambda e: e.activation(SG[:], pt[:], AF.Silu), [pk_], [SGk])
                    yield


                def ngroup_gen(g, gs, qb):
                    bc4 = lambda ap, n: ap.rearrange("p (o n) -> p o n", o=1).to_broadcast([ap.shape[0], 4 if n is None else n, ap.shape[1]])
                    TM = TMg[gs]
                    tmk = "TMg%d" % gs
                    for half in range(2):
                        for ee in range(2):
                            ci = half * 2 + ee
                            c0 = (2 * g) * 128 + ci * 64
                            for k_, nm in enumerate(("V16", "BH", "KH")):
                                src, kn_ = qb[nm]
                                T(lambda e: e.transpose(ptr[0:64, ee * 384 + k_ * 128: ee * 384 + (k_ + 1) * 128],
                                                        src[:, c0:c0 + 64], ident[:]), [kn_, "ident"], ["ptr"])
                        srcp = ptr[0:64, 0:768].rearrange("p (c n) -> p c n", n=384)
                        if half == 0:
                            A(lambda e: e.copy(TM[:, 0:2, :], srcp), ["ptr"], [tmk])
                        else:
                            V(lambda e: e.tensor_copy(TM[:, 2:4, :], srcp), ["ptr"], [tmk])
                        yield
                    ops = []
                    for c in range(4):
                        p, j = 2 * g + c // 2, c % 2
                        Ab, Abk = qb["AT"] if j == 0 else qb["ATo"]
                        Rb, Rbk = qb["RT"] if j == 0 else qb["RTo"]
                        Bb_, Bbk = qb["BT"] if j == 0 else qb["BTo"]
                        Kb, Kbk = qb["KT"] if j == 0 else qb["KTo"]
                        ops.append((p * 128, Ab, Abk, Rb, Rbk, Bb_, Bbk, Kb, Kbk))
                    bkc = lambda c: "psb%d" % (c // 2)
                    pn = psb[:, 0:1024].rearrange("p (c n) -> p c n", n=256)
                    pq4 = ptr[:, :].bitcast(F32).rearrange("p (c n) -> p c n", n=128)
                    id4 = ident[:, :].rearrange("p (o n) -> p o n", o=1).to_broadcast([128, 4, 128])
                    for c in range(4):
                        t0, Ab, Abk, Rb, Rbk, Bb_, Bbk, Kb, Kbk = ops[c]
                        T(lambda e: e.matmul(pn[:, c, 0:128], lhsT=Bb_[0:64, t0:t0 + 128], rhs=Ab[0:64, t0:t0 + 128],
                                             start=True, stop=True), [Bbk, Abk], [bkc(c)])
                        T(lambda e: e.matmul(pn[:, c, 128:256], lhsT=Ab[0:64, t0:t0 + 128], rhs=Bb_[0:64, t0:t0 + 128],
                                             start=True, stop=True), [Bbk, Abk], [bkc(c)])
                    yield
                    mk2 = mskUL[:, :].rearrange("p (o n) -> p o n", o=1).to_broadcast([128, 2, 256])
                    V(lambda e: e.tensor_tensor(PQg[0][:, 0:2, :], pn[:, 0:2, :], mk2, ALU.mult), ["psb0", "mskUL"], ["PWa0"])
                    V(lambda e: e.tensor_tensor(PQg[0][:, 2:4, :], pn[:, 2:4, :], mk2, ALU.mult), ["psb1", "mskUL"], ["PWb0"])
                    A(lambda e: e.copy(Wtg[0][:], PQg[0][:, :, 128:256]), ["PWa0", "PWb0"], ["Wtg0"])
                    V(lambda e: e.tensor_copy(PQg[0][:, :, 128:256], id4), ["Wtg0", "ident", "PWa0", "PWb0"], ["PWa0", "PWb0"])
                    yield
                    cur = 0
                    for lev in range(5):
                        nxt = 1 - cur
                        PW, Qc = PQg[cur], Wtg[cur]
                        ka, kb, kq = "PWa%d" % cur, "PWb%d" % cur, "Wtg%d" % cur
                        na, nb, nq = "PWa%d" % nxt, "PWb%d" % nxt, "Wtg%d" % nxt
                        for c in range(4):
                            kk_ = ka if c < 2 else kb
                            T(lambda e: e.matmul(pn[:, c, :], lhsT=Qc[:, c, :], rhs=PW[:, c, :], start=True, stop=True),
                              [kk_, kq], [bkc(c)])
                            T(lambda e: e.matmul(pq4[:, c, :], lhsT=PW[:, c, 0:128], rhs=Qc[:, c, :], start=True, stop=True),
                              [kk_, kq], ["ptr"])
                        yield
                        A(lambda e: e.copy(Wtg[nxt][:], pq4), ["ptr"], [nq])
                        V(lambda e: e.tensor_tensor(PQg[nxt][:, 0:2, 128:256], pn[:, 0:2, 128:256], PW[:, 0:2, 128:256], ALU.add),
                          ["psb0", ka], [na])
                        V(lambda e: e.tensor_tensor(PQg[nxt][:, 2:4, 128:256], pn[:, 2:4, 128:256], PW[:, 2:4, 128:256], ALU.add),
                          ["psb1", kb], [nb])
                        A(lambda e: e.copy(PQg[nxt][:, 0:2, 0:128], pn[:, 0:2, 0:128]), ["psb0", na], [na])
                        A(lambda e: e.copy(PQg[nxt][:, 2:4, 0:128], pn[:, 2:4, 0:128]), ["psb1", nb], [nb])
                        yield
                        cur = nxt
                    PW, Qc = PQg[cur], Wtg[cur]
                    ka, kb, kq = "PWa%d" % cur, "PWb%d" % cur, "Wtg%d" % cur
                    pw4 = psb[:, 0:512].rearrange("p (c n) -> p c n", n=128)
                    for c in range(4):
                        T(lambda e: e.matmul(pw4[:, c, :], lhsT=Qc[:, c, :], rhs=PW[:, c, 128:256], start=True, stop=True),
                          [ka, kb, kq], ["psb0"])
                    yield
                    Wf = Wfg[gs]
                    wfk = "Wfg%d" % gs
                    V(lambda e: e.tensor_tensor(Wf[:, :, 0:64], pw4[0:64, :, 0:64], PW[0:64, :, 128:192], ALU.add),
                      ["psb0", ka, kb], [wfk])
                    V(lambda e: e.tensor_tensor(Wf[:, :, 64:128], pw4[64:128, :, 64:128], PW[64:128, :, 192:256], ALU.add),
                      ["psb0", ka, kb, wfk], [wfk])
                    yield

                    mc2 = mskC[:, :].rearrange("p (o n) -> p o n", o=1).to_broadcast([64, 2, 192])
                    for e_ in range(2):
                        for c in range(4):
                            t0, Ab, Abk, Rb, Rbk, Bb_, Bbk, Kb, Kbk = ops[c]
                            c0 = t0 + e_ * 64
                            bnk = 1 if c < 2 else 0
                            pam = psb[0:64, bnk * 512 + (c % 2) * 192: bnk * 512 + (c % 2 + 1) * 192]
                            T(lambda e: e.matmul(pam[:, 0:64], lhsT=Bb_[0:64, c0:c0 + 64], rhs=Rb[0:64, c0:c0 + 64],
                                                 start=True, stop=True), [Bbk, Rbk], ["psb%d" % bnk])
                            T(lambda e: e.matmul(pam[:, 64:128], lhsT=Kb[0:64, c0:c0 + 64], rhs=Ab[0:64, c0:c0 + 64],
                                                 start=True, stop=True), [Kbk, Abk], ["psb%d" % bnk])
                            T(lambda e: e.matmul(pam[:, 128:192], lhsT=Kb[0:64, c0:c0 + 64], rhs=Rb[0:64, c0:c0 + 64],
                                                 start=True, stop=True), [Kbk, Rbk], ["psb%d" % bnk])
                            if FINE_N:
                                yield
                        if not FINE_N:
                            yield
                        am = AMg[gs][e_]
                        amk = "AMg%d_%d" % (gs, e_)
                        V(lambda e: e.tensor_tensor(am[:, 0:2, :], psb[0:64, 512:896].rearrange("p (c n) -> p c n", n=192), mc2, ALU.mult),
                          ["psb1", "mskC"], [amk])
                        V(lambda e: e.tensor_tensor(am[:, 2:4, :], psb[0:64, 0:384].rearrange("p (c n) -> p c n", n=192), mc2, ALU.mult),
                          ["psb0", "mskC", amk], [amk])
                        yield

                def seq_gen(j, p, gs, qb):
                    c = (p % 2) * 2 + j
                    Ab, Abk = qb["AT"] if j == 0 else qb["ATo"]
                    Rb, Rbk = qb["RT"] if j == 0 else qb["RTo"]
                    Gm, Gmk = qb["GAM"] if j == 0 else qb["GAMo"]
                    YS, YSk = qb["YS"]
                    sbank = psb[:, (2 + j) * 512:(3 + j) * 512]
                    sk_ = "psb%d" % (2 + j)
                    zx, uu = zxs[j], uus[j]
                    Wf = Wfg[gs][:, c, :]
                    wfk = "Wfg%d" % gs
                    tmk = "TMg%d" % gs
                    for e_ in range(2):
                        cl = 2 * p + e_
                        c0 = cl * 64
                        am = AMg[gs][e_][:, c, :]
                        amk = "AMg%d_%d" % (gs, e_)
                        tm = TMg[gs][:, (p % 2) * 2 + e_, :]
                        vj = tm[:, 64 * j:64 * j + 64]
                        bhj = tm[:, 128 + 64 * j:128 + 64 * j + 64]
                        khj = tm[:, 256 + 64 * j:256 + 64 * j + 64]
                        pzx = sbank[0:64, 0:64]
                        T(lambda e: e.matmul(pzx, lhsT=am[:, 64:128], rhs=vj, start=True, stop=False), [amk, tmk], [sk_])
                        T(lambda e: e.matmul(pzx, lhsT=Ab[0:64, c0:c0 + 64], rhs=Tb[j][:], start=False, stop=True),
                          [Abk, "Tb%d" % j], [sk_])
                        yield
                        A(lambda e: e.copy(zx[:], pzx), [sk_], ["zx%d" % j])
                        yield
                        pu = sbank[0:64, 64:128]
                        T(lambda e: e.matmul(pu, lhsT=Wf[:, e_ * 64:(e_ + 1) * 64], rhs=zx[:], start=True, stop=True),
                          [wfk, "zx%d" % j], [sk_])
                        yield
                        V(lambda e: e.tensor_copy(uu[:], pu), [sk_], ["uu%d" % j])
                        yield
                        ptn = sbank[0:64, 192:256]
                        T(lambda e: e.matmul(ptn, lhsT=bhj, rhs=uu[:], start=True, stop=False), [tmk, "uu%d" % j], [sk_])
                        T(lambda e: e.matmul(ptn, lhsT=khj, rhs=vj, start=False, stop=True), [tmk], [sk_])
                        py = sbank[0:64, 128:192]
                        T(lambda e: e.matmul(py, lhsT=Tb[j][:], rhs=Rb[0:64, c0:c0 + 64], start=True, stop=False),
                          ["Tb%d" % j, Rbk], [sk_])
                        T(lambda e: e.matmul(py, lhsT=uu[:], rhs=am[:, 0:64], start=False, stop=False), ["uu%d" % j, amk], [sk_])
                        T(lambda e: e.matmul(py, lhsT=vj, rhs=am[:, 128:192], start=False, stop=True), [tmk, amk], [sk_])
                        yield
                        V(lambda e: e.scalar_tensor_tensor(Tb[j][:], Tst[j][:], Gm[0:64, cl:cl + 1], ptn,
                                                           op0=ALU.mult, op1=ALU.add),
                          ["Tst%d" % j, Gmk, sk_], ["Tb%d" % j])
                        V(lambda e: e.scalar_tensor_tensor(Tst[j][:], Tst[j][:], Gm[0:64, cl:cl + 1], ptn,
                                                           op0=ALU.mult, op1=ALU.add),
                          ["Tst%d" % j, Gmk, sk_], ["Tst%d" % j])
                        A(lambda e: e.copy(YS[:, j, c0:c0 + 64], py), [sk_], [YSk])
                        yield


                def epi_gen(hp, q, j, qb):
                    h = 2 * hp + j
                    js = slice(64 * j, 64 * j + 64)
                    YS, YSk = qb["YS"]; BON, BONk = qb["BON"]; V16, V16k = qb["V16"]; SG, SGk = qb["SG"]; MST, MSTk = qb["MST"]
                    pe_ = psb[0:64, 1536 + 256:2048]
                    dd = eN(0)
                    d2 = B_("d2b", [64, 512], BF16)
                    sd = eN(2)
                    for hh in range(2):
                        hs = slice(hh * 256, (hh + 1) * 256)
                        T(lambda e: e.matmul(pe_, lhsT=ones64[:], rhs=YS[:, j, hs], start=True, stop=True), ["ones64", YSk], ["psb3"])
                        V(lambda e: e.scalar_tensor_tensor(dd[:, hs], pe_, -1.0, YS[:, j, hs], op0=ALU.mult, op1=ALU.add), ["psb3", YSk], ["e0"])
                        yield
                    A(lambda e: e.activation(d2[:], dd[:], AF.Square), ["e0"], ["d2b"])
                    for hh in range(2):
                        hs = slice(hh * 256, (hh + 1) * 256)
                        T(lambda e: e.matmul(pe_, lhsT=ones64f[0:64, :], rhs=d2[:, hs], start=True, stop=True), ["ones64f", "d2b"], ["psb3"])
                        A(lambda e: e.activation(sd[:, hs], pe_, AF.Ln, bias=epsc[0:64, 0:1]), ["psb3", "epsc"], ["e2"])
                        yield
                    A(lambda e: e.activation(sd[:], sd[:], AF.Exp, scale=-0.5), ["e2"], ["e2"])
                    V(lambda e: e.tensor_tensor(dd[:], dd[:], sd[:], ALU.mult), ["e0", "e2"], ["e0"])
                    yield
                    A(lambda e: e.activation(dd[:], dd[:], AF.Identity, bias=ph[:, 16 + h:17 + h], scale=ph[:, h:h + 1]),
                      ["e0", "ph"], ["e0"])
                    bvv = eN(1)
                    V(lambda e: e.tensor_tensor(bvv[:], BON[js, :], V16[js, :], ALU.mult), [BONk, V16k], ["e1"])
                    V(lambda e: e.tensor_tensor(dd[:], dd[:], bvv[:], ALU.add), ["e0", "e1"], ["e0"])
                    yield
                    sgs = eN(2)
                    G(lambda e: e.tensor_copy(sgs[:], SG[js, :]), [SGk], ["e2"])
                    V(lambda e: e.tensor_tensor(MST[js, :], dd[:], sgs[:], ALU.mult), ["e0", "e2"], [MSTk])
                    yield
                    if j == 1:
                        mix_store(MST, MSTk, hp, q)


                def sgroup_gen(hp, q, g, gs, par):
                    qb = QB[par]
                    if q == 0 and g == 0:
                        for j in range(2):
                            G(lambda e: e.memset(Tst[j][:], 0.0), [], ["Tst%d" % j])
                            G(lambda e: e.memset(Tb[j][:], 0.0), [], ["Tb%d" % j])
                    for p in (2 * g, 2 * g + 1):
                        yield from ilv(seq_gen(0, p, gs, qb), seq_gen(1, p, gs, qb))
                    if g == 1 and rl >= 5:
                        yield from epi_gen(hp, q, 0, qb)
                        yield from epi_gen(hp, q, 1, qb)

                quarters = [(hp, q) for hp in range(lim[0] if rl >= 2 else 0) for q in range(lim[1])]
                items = [(k, g) for k in range(len(quarters)) for g in range(2)]
                if quarters:
                    S.merge([prepA1_gen(quarters[0][0], quarters[0][1], 0), prepA2_gen(quarters[0][0], quarters[0][1], 0)])
                    S.merge([prepB1_gen(quarters[0][0], quarters[0][1], 0), prepB2_gen(quarters[0][0], quarters[0][1], 0)])
                for i in range(len(items) + 1):
                    gl = []
                    if i < len(items):
                        k, g = items[i]
                        if not os.environ.get("NO_N"):
                            gl.append(ngroup_gen(g, i % 2, QB[k % 2]))
                        if k + 1 < len(quarters):
                            for pg in ((prepA1_gen, prepA2_gen) if g == 0 else (prepB1_gen, prepB2_gen)):
                                gl.append(pg(quarters[k + 1][0], quarters[k + 1][1], (k + 1) % 2))
                    if i >= 1:
                        k, g = items[i - 1]
                        if not os.environ.get("NO_S"):
                            sg = sgroup_gen(quarters[k][0], quarters[k][1], g, (i - 1) % 2, k % 2)
                            gl.append(sg)
                    if MERGE:
                        S.merge(gl)
                    else:
                        interleave(*gl)

                print("rwkv arena used words", AR.off)


            phase()
            def moba_gen():
                if "M" not in phases:
                    return
                yield
                vtok = B_("vtok", [128, 16, 512], BF16)
                wvs = [load_w_at(wl[:, 5248 + i * 128:5248 + (i + 1) * 128], 128, i) for i in range(4)]
                mslots = {0: (4, 5, 0), 1: (1, 2, 3), 2: (4, 5, 0), 3: (1, 2, 3)}
                msrc = lambda h: (wl[:, 4224 + h * 128:4224 + (h + 1) * 128], wl[:, 4736 + h * 128:4736 + (h + 1) * 128],
                                  wl[:, 5760 + h * 128:5760 + (h + 1) * 128])
                load_w_at(msrc(0)[0], 128, 4)
                load_w_at(msrc(0)[1], 128, 5)
                for tt in range(16):
                    pi = pjc[0] % 2
                    pjc[0] += 1
                    for i in range(4):
                        for c in range(16):
                            T(lambda e, c=c, i=i, pi=pi, tt=tt: e.matmul(pproj[pi][:, i * 128:(i + 1) * 128],
                                                                       lhsT=hT[:, c, tt * 128:(tt + 1) * 128],
                                                                       rhs=wbf[wvs[i]][:, c, :], start=(c == 0), stop=(c == 15)),
                              ["hT", "wbf%d" % wvs[i]], ["pproj%d" % pi])
                    A(lambda e, pi=pi, tt=tt: e.copy(vtok[:, tt, :], pproj[pi][:]), ["pproj%d" % pi], ["vtok"])
                    yield
                QT = B_("QT", [128, S_], BF16)
                KTm = B_("KTm", [128, S_], BF16)
                SGm = B_("SGm", [128, S_], BF16)
                ksum = B_("ksum", [128, 8])
                ksb = B_("ksb", [128, 8], BF16)
                Sm = B_("Sm", [128, S_])
                Ex = B_("Ex", [128, S_], BF16)
                ET = B_("ET", [128, 16, 128], BF16)
                gm = B_("gm", [128, 8])
                t8 = B_("t8", [128, 8])
                mx = B_("mx", [128, 4])
                on = B_("on", [128, 128], BF16)
                sc = float(128 ** -0.5)
                load_w_at(msrc(0)[2], 128, 0)
                for k_ in range(3):
                    load_w_at(msrc(1)[k_], 128, mslots[1][k_])
                for h in range(4):
                    wq, wk, wg = mslots[h]
                    for q in range(4):
                        sl = slice(q * 512, (q + 1) * 512)
                        pt, pk_ = proj_fm(wq, 128, q)
                        A(lambda e, pt=pt, sl=sl: e.copy(QT[:, sl], pt[:]), [pk_], ["QT"])
                        pt, pk_ = proj_fm(wk, 128, q)
                        for hb in range(2):
                            n = q * 2 + hb
                            A(lambda e, pt=pt, n=n, hb=hb: e.activation(KTm[:, n * 256:(n + 1) * 256], pt[:, hb * 256:(hb + 1) * 256],
                                                                        AF.Copy, accum_out=ksum[:, n:n + 1]),
                              [pk_], ["KTm", "ksum"])
                        pt, pk_ = proj_fm(wg, 128, q)
                        A(lambda e, pt=pt, sl=sl: e.activation(SGm[:, sl], pt[:], AF.Silu), [pk_], ["SGm"])
                        yield
                    V(lambda e: e.tensor_copy(ksb[:], ksum[:]), ["ksum"], ["ksb"])
                    if h + 2 < 4:
                        for k_ in range(3):
                            load_w_at(msrc(h + 2)[k_], 128, mslots[h + 2][k_])
                    for i in range(16):
                        b = i // 2
                        nk = (b + 1) * 256
                        qs = slice(i * 128, (i + 1) * 128)
                        nkb = (nk + 511) // 512
                        for kb in range(nkb):
                            w_ = min(512, nk - kb * 512)
                            T(lambda e, kb=kb, w_=w_, qs=qs: e.matmul(psb[:, kb * 512:kb * 512 + w_], lhsT=QT[:, qs],
                                                                     rhs=KTm[:, kb * 512:kb * 512 + w_], start=True, stop=True),
                              ["QT", "KTm"], ["psb%d" % kb])
                        allb = ["psb%d" % kb for kb in range(nkb)]
                        if b > 0:
                            pg_ = pseq[:, 384:392]
                            T(lambda e, qs=qs: e.matmul(pg_, lhsT=QT[:, qs], rhs=ksb[:], start=True, stop=True),
                              ["QT", "ksb"], ["pseq"])
                            V(lambda e, b=b: e.tensor_tensor(gm[:], pg_, pbias[:, b, :], ALU.add), ["pseq", "pbias"], ["gm"])
                            V(lambda e: e.max(t8[:], gm[:]), ["gm"], ["t8"])
                            V(lambda e: e.tensor_scalar_max(t8[:, 2:3], t8[:, 2:3], -1e29), ["t8"], ["t8"])
                            V(lambda e: e.tensor_scalar(gm[:], gm[:], t8[:, 2:3], None, op0=ALU.is_ge), ["gm", "t8"], ["gm"])
                            V(lambda e: e.tensor_scalar(gm[:], gm[:], -1.0, 1.0e30, op0=ALU.add, op1=ALU.mult), ["gm"], ["gm"])
                            V(lambda e, b=b: e.tensor_tensor(
                                Sm[:, 0:b * 256].rearrange("p (n t) -> p n t", t=256),
                                psb[:, 0:b * 256].rearrange("p (n t) -> p n t", t=256),
                                gm[:, 0:b].rearrange("p (n o) -> p n o", o=1).to_broadcast([128, b, 256]), ALU.add),
                              allb + ["gm"], ["Sm"])
                        V(lambda e, b=b, i=i: e.tensor_tensor(Sm[:, b * 256:(b + 1) * 256], psb[:, b * 256:(b + 1) * 256],
                                                              cbias[:, i % 2, :], ALU.add), allb + ["cbias"], ["Sm"])
                        yield
                        V(lambda e, nk=nk: e.reduce_max(mx[:, 0:1], Sm[:, 0:nk], AX.X), ["Sm"], ["mx"])
                        V(lambda e: e.tensor_scalar(mx[:, 1:2], mx[:, 0:1], -sc, None, op0=ALU.mult), ["mx"], ["mx"])
                        A(lambda e, nk=nk: e.activation(Ex[:, 0:nk], Sm[:, 0:nk], AF.Exp, bias=mx[:, 1:2], scale=sc,
                                                        accum_out=mx[:, 2:3]), ["Sm", "mx"], ["Ex", "mx"])
                        V(lambda e: e.reciprocal(mx[:, 3:4], mx[:, 2:3]), ["mx"], ["mx"])
                        njt = nk // 128
                        yield
                        for g0 in range(0, njt, 8):
                            g1 = min(njt, g0 + 8)
                            for jt in range(g0, g1):
                                T(lambda e, jt=jt, g0=g0: e.transpose(ptr[:, (jt - g0) * 128:(jt - g0 + 1) * 128],
                                                                      Ex[:, jt * 128:(jt + 1) * 128], ident[:]),
                                  ["Ex", "ident"], ["ptr"])
                            srcp = ptr[:, 0:(g1 - g0) * 128].rearrange("p (j q) -> p j q", q=128)
                            if (g0 // 8) % 2 == 0:
                                V(lambda e, g0=g0, g1=g1, srcp=srcp: e.tensor_copy(ET[:, g0:g1, :], srcp), ["ptr"], ["ET"])
                            else:
                                A(lambda e, g0=g0, g1=g1, srcp=srcp: e.copy(ET[:, g0:g1, :], srcp), ["ptr"], ["ET"])
                            yield
                        po = pseq[:, 128:256]
                        for jt in range(njt):
                            T(lambda e, jt=jt, h=h, njt=njt: e.matmul(po, lhsT=ET[:, jt, :], rhs=vtok[:, jt, h * 128:(h + 1) * 128],
                                                                      start=(jt == 0), stop=(jt == njt - 1)),
                              ["ET", "vtok"], ["pseq"])
                        A(lambda e: e.mul(on[:], po, mx[:, 3:4]), ["pseq", "mx"], ["on"])
                        yield
                        T(lambda e: e.transpose(ptr[:, 0:128], on[:], ident[:]), ["on", "ident"], ["ptr"])
                        mst = B_("mst%d" % (i % 2), [128, 128], BF16)
                        V(lambda e, qs=qs, mst=mst: e.tensor_tensor(mst[:], ptr[:, 0:128], SGm[:, qs], ALU.mult),
                          ["ptr", "SGm"], ["mst%d" % (i % 2)])
                        S.dma("sync", mix_d[i, :, 8 + h, :], mst[:], reads=["mst%d" % (i % 2)], writes=["mixd"])
                        yield


            def gla_gen():
                if "G" not in phases:
                    return
                yield
                gq_, gk_ = {0: 7, 1: 9}, {0: 8, 1: 6}
                gv_, gg_ = {0: 9, 1: 7, 2: 7, 3: 9}, {0: 6, 1: 8, 2: 8, 3: 6}
                srcq = lambda ft: wl[:, 6272 + ft * 128:6272 + (ft + 1) * 128]
                srck = lambda ft: wl[:, 6528 + ft * 128:6528 + (ft + 1) * 128]
                srcv = lambda h: wl[:, 6784 + h * 128:6784 + (h + 1) * 128]
                srcg = lambda h: wl[:, 7296 + h * 128:7296 + (h + 1) * 128]
                wlow = load_w_at(wl[:, 7808:7824], 16, 6)
                load_w_at(srcq(0), 128, gq_[0])
                load_w_at(srck(0), 128, gk_[0])
                load_w_at(srcv(0), 128, gv_[0])
                lowT = B_("lowT", [16, S_], BF16)
                for q in range(4):
                    pt, pk_ = proj_fm(wlow, 16, q)
                    A(lambda e, pt=pt, q=q: e.copy(lowT[:, q * 512:(q + 1) * 512], pt[0:16, :]), [pk_], ["lowT"])
                load_w_at(srcg(0), 128, gg_[0])
                GQ = B_("GQ", [128, S_], BF16)
                GQo = B_("GQo", [64, S_], BF16)
                GKe = B_("GKe", [128, S_], BF16)
                GKd = B_("GKd", [128, S_], BF16)
                gdec = B_("gdec", [128, 32])
                gdeco = B_("gdeco", [64, 32])
                VT = B_("VT", [128, S_], BF16)
                OT = B_("OT", [128, S_])
                SGg = B_("SGg", [128, S_], BF16)
                Sst = B_("Sst", [64, 128])
                Sbf = B_("Sbf", [64, 128], BF16)
                tmg = B_("tmg", [64, 256], BF16)
                att = B_("att", [64, 64], BF16)
                for ft in range(2):
                    wq, wk = gq_[ft], gk_[ft]
                    for q in range(4):
                        sl = slice(q * 512, (q + 1) * 512)
                        pi_ = pjc[0] % 2
                        pjc[0] += 1
                        pla = pproj[pi_][:, :]
                        T(lambda e, sl=sl, ft=ft: e.matmul(pla, lhsT=aupb[:, ft * 128:(ft + 1) * 128], rhs=lowT[:, sl],
                                                           start=True, stop=True), ["aupb", "lowT"], ["pproj%d" % pi_])
                        la = sN(3)
                        A(lambda e, ft=ft: e.activation(la[:], pla, AF.Sigmoid, bias=pc[:, GAB + ft:GAB + ft + 1]),
                          ["pproj%d" % pi_, "pc"], ["s3"])
                        A(lambda e: e.activation(la[:], la[:], AF.Ln), ["s3"], ["s3"])
                        cs = sN(4)
                        V(lambda e: e.tensor_tensor_scan(cs[:], rmask[:], la[:], 0.0, ALU.mult, ALU.add),
                          ["rmask", "s3"], ["s4"])
                        cs3 = cs[:].rearrange("p (c t) -> p c t", t=64)
                        ex = sN(5)
                        A(lambda e: e.activation(ex[:], cs[:], AF.Exp, scale=1.0 / 16), ["s4"], ["s5"])
                        pt, pk_ = proj_fm(wq, 128, q)
                        V(lambda e, pt=pt, sl=sl: e.scalar_tensor_tensor(GQ[:, sl], pt[:], 0.125, ex[:], op0=ALU.mult, op1=ALU.mult),
                          [pk_, "s5"], ["GQ"])
                        G(lambda e, sl=sl: e.tensor_copy(GQo[:, sl], GQ[64:128, sl]), ["GQ"], ["GQo"])
                        eM = sN(6)
                        A(lambda e: e.activation(eM[:], cs[:], AF.Exp, scale=-1.0 / 16), ["s4"], ["s6"])
                        dC = sN(5)
                        V(lambda e: e.tensor_tensor(dC[:].rearrange("p (c t) -> p c t", t=64),
                                                    cs3[:, :, 63:64].to_broadcast([128, 8, 64]), cs3, ALU.subtract),
                          ["s4"], ["s5"])
                        eC = sN(5)
                        A(lambda e: e.activation(eC[:], dC[:], AF.Exp, scale=1.0 / 16), ["s5"], ["s5"])
                        A(lambda e, q=q: e.activation(gdec[:, q * 8:(q + 1) * 8], cs3[:, :, 63], AF.Exp, scale=1.0 / 16),
                          ["s4"], ["gdec"])
                        pt, pk_ = proj_fm(wk, 128, q)
                        kf = sN(9)
                        A(lambda e, pt=pt: e.copy(kf[:], pt[:]), [pk_], ["s9"])
                        V(lambda e, sl=sl: e.tensor_tensor(GKe[:, sl], kf[:], eM[:], ALU.mult), ["s9", "s6"], ["GKe"])
                        V(lambda e, sl=sl: e.tensor_tensor(GKd[:, sl], kf[:], eC[:], ALU.mult), ["s9", "s5"], ["GKd"])
                        yield
                    G(lambda e: e.tensor_copy(gdeco[:, :], gdec[64:128, :]), ["gdec"], ["gdeco"])
                    load_w_at(srcv(2 * ft + 1), 128, gv_[2 * ft + 1])
                    load_w_at(srcg(2 * ft + 1), 128, gg_[2 * ft + 1])
                    for j in range(2):
                        h = ft * 2 + j
                        js = slice(64 * j, 64 * j + 64)
                        wv, wg = gv_[h], gg_[h]
                        for q in range(4):
                            sl = slice(q * 512, (q + 1) * 512)
                            pt, pk_ = proj_fm(wv, 128, q)
                            A(lambda e, pt=pt, sl=sl: e.copy(VT[:, sl], pt[:]), [pk_], ["VT"])
                            pt, pk_ = proj_fm(wg, 128, q)
                            A(lambda e, pt=pt, sl=sl: e.activation(SGg[:, sl], pt[:], AF.Silu), [pk_], ["SGg"])
                            yield
                        if h == 0:
                            load_w_at(srcq(1), 128, gq_[1])
                            load_w_at(srck(1), 128, gk_[1])
                        elif h == 1:
                            load_w_at(srcv(2), 128, gv_[2])
                            load_w_at(srcg(2), 128, gg_[2])
                        G(lambda e: e.memset(Sst[:], 0.0), [], ["Sst"])
                        G(lambda e: e.memset(Sbf[:], 0.0), [], ["Sbf"])
                        Qb, Qbk = (GQ, "GQ") if j == 0 else (GQo, "GQo")
                        Gd, Gdk = (gdec, "gdec") if j == 0 else (gdeco, "gdeco")
                        tmgs = [tmg, B_("tmg1", [64, 256], BF16)]
                        atts = [att, B_("att1", [64, 64], BF16)]
                        tkeys = ["tmg", "tmg1"]
                        akeys = ["att", "att1"]

                        def gla_pre(c):
                            c0 = c * 64
                            tm_, at_ = tmgs[c % 2], atts[c % 2]
                            T(lambda e: e.transpose(ptr[0:64, 0:128], VT[:, c0:c0 + 64], ident[:]), ["VT", "ident"], ["ptr"])
                            T(lambda e: e.transpose(ptr[0:64, 128:256], GKd[:, c0:c0 + 64], ident[:]), ["GKd", "ident"], ["ptr"])
                            A(lambda e: e.copy(tm_[:], ptr[0:64, 0:256]), ["ptr"], [tkeys[c % 2]])
                            pat = pseq[0:64, 0:64]
                            T(lambda e: e.matmul(pat, lhsT=GKe[js, c0:c0 + 64], rhs=GQ[js, c0:c0 + 64], start=True, stop=True),
                              ["GKe", "GQ"], ["pseq"])
                            V(lambda e: e.tensor_tensor(at_[:], pat, mskC[:, 0:64], ALU.mult), ["pseq", "mskC"], [akeys[c % 2]])

                        gla_pre(0)
                        yield
                        for c in range(32):
                            c0 = c * 64
                            if c + 1 < 32:
                                gla_pre(c + 1)
                                yield
                            tm_, at_ = tmgs[c % 2], atts[c % 2]
                            tk_, ak_ = tkeys[c % 2], akeys[c % 2]
                            po = pseq[:, 64:128]
                            T(lambda e: e.matmul(po, lhsT=tm_[:, 0:128], rhs=at_[:], start=True, stop=False),
                              [tk_, ak_], ["pseq"])
                            T(lambda e: e.matmul(po, lhsT=Sbf[:], rhs=Qb[0:64, c0:c0 + 64], start=False, stop=True),
                              ["Sbf", Qbk], ["pseq"])
                            A(lambda e: e.copy(OT[:, c0:c0 + 64], po), ["pseq"], ["OT"])
                            pst = pseq[0:64, 256:384]
                            T(lambda e: e.matmul(pst, lhsT=tm_[:, 128 + 64 * j:128 + 64 * j + 64], rhs=tm_[:, 0:128],
                                                 start=True, stop=True), [tk_], ["pseq"])
                            V(lambda e: e.scalar_tensor_tensor(Sst[:], Sst[:], Gd[0:64, c:c + 1], pst, op0=ALU.mult,
                                                               op1=ALU.add), ["Sst", Gdk, "pseq"], ["Sst"])
                            A(lambda e: e.copy(Sbf[:], Sst[:]), ["Sst"], ["Sbf"])
                            yield
                        for q in range(4):
                            sl = slice(q * 512, (q + 1) * 512)
                            o2 = B_("o2b", [128, 512], BF16)
                            A(lambda e, sl=sl: e.activation(o2[:], OT[:, sl], AF.Square), ["OT"], ["o2b"])
                            pi_ = pjc[0] % 2
                            pjc[0] += 1
                            pms = pproj[pi_][:, :]
                            T(lambda e: e.matmul(pms, lhsT=ones128[:], rhs=o2[:], start=True, stop=True),
                              ["ones128", "o2b"], ["pproj%d" % pi_])
                            sd = sN(4)
                            A(lambda e: e.activation(sd[:], pms, AF.Ln, bias=epsc[:, 1:2]), ["pproj%d" % pi_, "epsc"], ["s4"])
                            A(lambda e: e.activation(sd[:], sd[:], AF.Exp, scale=-0.5), ["s4"], ["s4"])
                            V(lambda e, sl=sl: e.scalar_tensor_tensor(sd[:], OT[:, sl], pc[:, GNW:GNW + 1], sd[:], op0=ALU.mult,
                                                                     op1=ALU.mult), ["OT", "pc", "s4"], ["s4"])
                            gms = B_("gms%d" % (q % 2), [128, 512], BF16)
                            V(lambda e, sl=sl, gms=gms: e.tensor_tensor(gms[:], sd[:], SGg[:, sl], ALU.mult),
                              ["s4", "SGg"], ["gms%d" % (q % 2)])
                            mix_store(gms, "gms%d" % (q % 2), 12 + h, q)
                            yield


            interleave(moba_gen(), gla_gen())

            if dbg and l == 0:
                for tt in range(16):
                    S.dma("sync", dbg_d[:, :, tt * 128:(tt + 1) * 128], mix_d[tt], reads=["mixd"], writes=["dbgout"])

            phase()
            wo = wout_d[l]
            for fb in range(16 if "D" in phases else 0):
                S.dma("gpsimd", hT[:, :, fb * 128:(fb + 1) * 128],
                      wo[:, fb * 128:(fb + 1) * 128].rearrange("(c p) n -> p c n", p=128), writes=["hT"])
            last = (l == nlayers - 1)
            nwb = B_("nwb", [128, D_])
            xin = B_("xin", [128, D_])
            h16 = B_("h16", [128, D_], BF16)
            st4 = B_("st4", [128, 8])
            if last:
                S.dma("sync", nwb[:], fnw_d[0:1, :].partition_broadcast(128), writes=["nwb"])
            for tt in range(16 if "D" in phases else 0):
                ts_ = slice(tt * 128, (tt + 1) * 128)
                xk_, hk_, sk_ = "xin%d" % (tt % 2), "h16%d" % (tt % 2), "st4%d" % (tt % 2)
                xin = B_(xk_, [128, D_])
                h16 = B_(hk_, [128, D_], BF16)
                st4 = B_(sk_, [128, 8])
                S.dma("sync", xin[:], xsrc[ts_, :], reads=["xs%d" % tt], writes=[xk_])
                mxt = B_("mxt%d" % (tt % 2), [128, 16, 128], BF16)
                mxk = "mxt%d" % (tt % 2)
                S.dma("sync", mxt[:], mix_d[tt], reads=["mixd"], writes=[mxk])
                for dq in range(4):
                    pi = pjc[0] % 2
                    pjc[0] += 1
                    for fc in range(16):
                        T(lambda e, fc=fc, pi=pi, dq=dq, ts_=ts_: e.matmul(pproj[pi][:], lhsT=mxt[:, fc, :],
                                                                         rhs=hT[:, fc, dq * 512:(dq + 1) * 512],
                                                                         start=(fc == 0), stop=(fc == 15)),
                          [mxk, "hT"], ["pproj%d" % pi])
                    V(lambda e, pi=pi, dq=dq: e.tensor_tensor(xin[:, dq * 512:(dq + 1) * 512], pproj[pi][:],
                                                              xin[:, dq * 512:(dq + 1) * 512], ALU.add),
                      ["pproj%d" % pi, xk_], [xk_])
                if not last:
                    S.dma("sync", xs_d[ts_, :], xin[:], reads=[xk_], writes=["xs%d" % tt])
                else:
                    rms_stats(xin, h16, st4, xk_, hk_, sk_)
                    V(lambda e: e.scalar_tensor_tensor(xin[:], xin[:], st4[:, 3:4], nwb[:], op0=ALU.mult, op1=ALU.mult),
                      [xk_, sk_, "nwb"], [xk_])
                    S.dma("sync", out_d[ts_, :], xin[:], reads=[xk_], writes=["out%d" % tt])

        fin = [S.lastw[k] for k in S.lastw if k.startswith("out") or k == "dbgout"]
        S.finish(fin, "sync")
        S.run()
    return nc


def host_inputs(inp):
    f = lambda a: np.ascontiguousarray(np.asarray(a, dtype=np.float32))
    pcs = np.zeros((2, 128, NPC), np.float32)
    phs = np.zeros((2, 64, 32), np.float32)
    for l in range(2):
        tp = lambda v: f(v).reshape(-1, 128).T
        pcs[l, :, MU:MU + 25] = tp(inp["rwkv_mu"][l])
        pcs[l, :, W0:W0 + 8] = tp(inp["rwkv_w0"][l])
        pcs[l, :, A0:A0 + 8] = tp(inp["rwkv_a0"][l])
        pcs[l, :, KK:KK + 8] = tp(inp["rwkv_k_k"][l])
        pcs[l, :, KA:KA + 8] = tp(inp["rwkv_k_a"][l])
        pcs[l, :, RK:RK + 8] = tp(f(inp["rwkv_r_k"][l]).reshape(-1))
        pcs[l, :, GAB:GAB + 2] = tp(inp["gla_a_b"][l])
        pcs[l, :, GNW:GNW + 1] = tp(inp["gla_norm_w"][l])
        pcs[l, :, LNW:LNW + 8] = tp(inp["rwkv_ln_w"][l])
        pcs[l, :, LNB:LNB + 8] = tp(inp["rwkv_ln_b"][l])
        phs[l, :, 0:16] = f(inp["rwkv_ln_w"][l]).reshape(16, 64).T
        phs[l, :, 16:32] = f(inp["rwkv_ln_b"][l]).reshape(16, 64).T
    common = {
        "w_in": f(inp["w_in"]), "w_out": f(inp["w_out"]), "pc": pcs, "ph": phs,
        "norm_w": f(inp["norm_w"]), "final_norm_w": f(inp["final_norm_w"]).reshape(1, D_),
        "rwkv_w2": f(inp["rwkv_w2"]), "rwkv_a2": f(inp["rwkv_a2"]), "gla_a_up": f(inp["gla_a_up"]),
    }
    for n, a in host_consts().items():
        common["c_" + n] = a
    x = f(inp["x"])
    return [dict(common, x=x[b]) for b in range(x.shape[0])]


_NC = {}


def kernel(**inputs):
    in_maps = host_inputs(inputs)
    if "nc" not in _NC:
        _NC["nc"] = build()
    res = run_bass_kernel_spmd(_NC["nc"], in_maps, core_ids=list(range(8)))
    return np.stack([np.asarray(r["out"], dtype=np.float32) for r in res.results], axis=0)
```

```python
import os
import numpy as np
import ml_dtypes
from contextlib import ExitStack
import concourse.bass as bass
import concourse.mybir as mybir
from concourse.bass_utils import run_bass_kernel_spmd

F32 = mybir.dt.float32
BF16 = mybir.dt.bfloat16
ALU = mybir.AluOpType
AF = mybir.ActivationFunctionType
AX = mybir.AxisListType

NDSEM = 6
FINE_P = int(os.environ.get("FINE_P", "16"))
FINE_N = int(os.environ.get("FINE_N", "0"))
SW = int(os.environ.get("SW", "1"))
MERGE = int(os.environ.get("MERGE", "1"))
S_ = 2048
D_ = 2048
DIN = 7824
C0 = float(np.exp(-0.5))
NEG = -1.0e30

MU, W0, A0, KK, KA, RK, GAB, GNW, LNW, LNB, NPC = 0, 25, 33, 41, 49, 57, 65, 67, 68, 76, 84


class _Rec:
    def __getattr__(self, name):
        def f(*a, **k):
            self.call = (name, a, k)
            return self
        return f


class Sched:
    ENGS = ("tensor", "vector", "scalar", "gpsimd", "sync")

    def __init__(self, nc, es):
        self.nc = nc
        self.ops = {e: [] for e in self.ENGS}
        self.sem = {e: es.enter_context(nc.semaphore("s_" + e)) for e in self.ENGS}
        self.cnt = {e: 0 for e in self.ENGS}
        self.dsem = {e: [es.enter_context(nc.semaphore("d_%s%d" % (e, i))) for i in range(NDSEM)]
                     for e in ("sync", "gpsimd")}
        self.dcnt = {e: 0 for e in self.dsem}
        self.clock = {e: {} for e in self.ENGS}
        self.lastw = {}
        self.readers = {}
        self.semobj = {}
        self.bar = []
        self.bar_pending = set()
        self.collect = None
        self.vt = {}

    def barrier(self):
        toks = [("c_" + e, self.cnt[e], e) for e in self.ENGS if self.cnt[e] > 0]
        for q in self.dsem:
            n = self.dcnt[q]
            for j in range(NDSEM):
                k = (n - j + NDSEM - 1) // NDSEM if n > j else 0
                if k > 0:
                    toks.append(("d_%s%d" % (q, j), 16 * k, None))
        self.bar = toks
        self.bar_pending = set(self.ENGS)

    def _bar(self, eng, deps):
        if eng in self.bar_pending:
            self.bar_pending.discard(eng)
            deps.extend(self.bar)
        return deps

    def _deps(self, reads, writes):
        deps = []
        for k in reads:
            t = self.lastw.get(k)
            if t is not None:
                deps.append(t)
        for k in writes:
            t = self.lastw.get(k)
            if t is not None:
                deps.append(t)
            deps.extend(self.readers.get(k, ()))
        return deps

    def _commit(self, tok, reads, writes):
        for k in reads:
            self.readers.setdefault(k, []).append(tok)
        for k in writes:
            self.lastw[k] = tok
            self.readers[k] = []

    def _waits(self, eng, deps):
        need = {}
        for (sk, val, src) in deps:
            if src == eng and eng == "tensor":
                continue
            if self.clock[eng].get(sk, 0) >= val:
                continue
            if need.get(sk, 0) < val:
                need[sk] = val
        out = []
        for sk, val in need.items():
            self.clock[eng][sk] = val
            out.append((self.semobj[sk], val))
        return out

    def op(self, eng, fn, reads=(), writes=()):
        if self.collect is not None:
            rec = _Rec()
            fn(rec)
            self.collect.append(("op", eng, rec.call, tuple(reads), tuple(writes)))
            return None
        return self._op(eng, fn, reads, writes)

    def _op(self, eng, fn, reads=(), writes=(), call=None):
        pk = [k for k in reads if k.startswith(("psb", "pproj", "ptr", "pseq"))]
        if pk:
            reads = [k for k in reads if k not in pk]
            writes = list(writes) + pk
        deps = self._bar(eng, self._deps(reads, writes))
        waits = self._waits(eng, deps)
        self.cnt[eng] += 1
        n = self.cnt[eng]
        sem = self.sem[eng]
        sk = "c_" + eng
        self.semobj[sk] = sem
        tok = (sk, n, eng)
        self._commit(tok, reads, writes)
        if call is None:
            rec = _Rec()
            fn(rec)
            call = rec.call

        def emit(e, call=call, waits=waits, sem=sem):
            for (s, v) in waits:
                e.wait_ge(s, v)
            getattr(e, call[0])(*call[1], **call[2]).then_inc(sem, 1)
        self.ops[eng].append(emit)
        return tok

    def dma(self, q, out, in_, reads=(), writes=()):
        if self.collect is not None:
            self.collect.append(("dma", q, (out, in_), tuple(reads), tuple(writes)))
            return None
        deps = self._bar(q, self._deps(reads, writes))
        i = self.dcnt[q]
        self.dcnt[q] += 1
        j = i % NDSEM
        m = i // NDSEM
        sem = self.dsem[q][j]
        sk = "d_%s%d" % (q, j)
        self.semobj[sk] = sem
        if m > 0:
            deps.append((sk, 16 * m, None))
        waits = self._waits(q, deps)
        tok = (sk, 16 * (m + 1), None)
        self._commit(tok, reads, writes)

        def emit(e, waits=waits, sem=sem, out=out, in_=in_):
            for (s, v) in waits:
                e.wait_ge(s, v)
            e.dma_start(out=out, in_=in_).then_inc(sem, 16)
        self.ops[q].append(emit)
        return tok

    @staticmethod
    def _cost(kind, eng, call):
        if kind == "dma":
            return 0.3 if eng == "gpsimd" else 0.08
        args = call[1]
        out = call[2].get("out", args[0] if args else None)
        try:
            n = int(np.prod(out.shape[1:]))
        except Exception:
            n = 128
        if eng == "tensor":
            return 0.14 + n / 2400.0
        if eng == "gpsimd":
            return 0.25 + n / 700.0
        return 0.13 + n / 1000.0

    def merge(self, gens, lat=float(os.environ.get("LAT", "1.5"))):
        streams = []
        for g in gens:
            self.collect = []
            for _ in g:
                pass
            streams.append(self.collect)
            self.collect = None
        ptr = [0] * len(streams)
        ready = [0.0] * len(streams)
        prev_eng = [None] * len(streams)
        free = {e: self.vt.get(e, 0.0) for e in self.ENGS}
        while True:
            best, bt = None, None
            for i, st in enumerate(streams):
                if ptr[i] >= len(st):
                    continue
                kind, eng, call, rd, wr = st[ptr[i]]
                r = ready[i] + (lat if (prev_eng[i] is not None and prev_eng[i] != eng) else 0.0)
                t = max(free[eng], r)
                if bt is None or t < bt:
                    best, bt = i, t
            if best is None:
                break
            kind, eng, call, rd, wr = streams[best][ptr[best]]
            ptr[best] += 1
            c = self._cost(kind, eng, call)
            free[eng] = bt + c
            ready[best] = bt + c
            prev_eng[best] = eng
            if kind == "op":
                self._op(eng, None, rd, wr, call=call)
            else:
                self.dma(eng, call[0], call[1], rd, wr)
        base = max(free.values())
        self.vt = {e: base for e in self.ENGS}

    def finish(self, tokens, eng="sync"):
        waits = self._waits(eng, list(tokens))

        def emit(e, waits=waits):
            for (s, v) in waits:
                e.wait_ge(s, v)
        self.ops[eng].append(emit)

    def run(self):
        nc = self.nc
        with nc.Block() as block:
            @block.tensor
            def _(e):
                for f in self.ops["tensor"]:
                    f(e)

            @block.vector
            def _(e):
                for f in self.ops["vector"]:
                    f(e)

            @block.scalar
            def _(e):
                for f in self.ops["scalar"]:
                    f(e)

            @block.gpsimd
            def _(e):
                for f in self.ops["gpsimd"]:
                    f(e)

            @block.sync
            def _(e):
                for f in self.ops["sync"]:
                    f(e)


def host_consts():
    bf = ml_dtypes.bfloat16
    c = {}
    c["ident"] = np.eye(128, dtype=np.float32).astype(bf)
    i = np.arange(128)
    same = (i[:, None] // 64) == (i[None, :] // 64)
    mU = (same & (i[:, None] < i[None, :])).astype(np.float32)
    mL = (same & (i[:, None] > i[None, :])).astype(np.float32)
    c["mskUL"] = np.concatenate([mU, mL], axis=1).astype(bf)
    j = np.arange(64)
    inc = (j[:, None] <= j[None, :]).astype(np.float32)
    st = (j[:, None] < j[None, :]).astype(np.float32)
    c["mskC"] = np.concatenate([inc, st, inc], axis=1).astype(bf)
    rm = np.ones((128, 512), np.float32)
    rm[:, ::64] = 0.0
    c["rmask"] = rm.astype(bf)
    c["onesBD"] = same.astype(np.float32).astype(bf)
    c["ones64"] = np.full((64, 64), 1.0 / 64, np.float32).astype(bf)
    c["ones64f"] = np.full((128, 64), 1.0 / 64, np.float32).astype(bf)
    c["ones128"] = np.full((128, 128), 1.0 / 128, np.float32).astype(bf)
    cb = np.zeros((128, 2, 256), np.float32)
    q = np.arange(128)
    kpos = np.arange(256)
    cb[:, 0, :] = np.where(kpos[None, :] <= q[:, None], 0.0, NEG)
    cb[:, 1, :] = np.where(kpos[None, :] <= q[:, None] + 128, 0.0, NEG)
    c["cbias"] = cb.astype(bf)
    pb = np.zeros((128, 8, 8), np.float32)
    for b in range(8):
        pb[:, b, b:] = NEG
    c["pbias"] = pb
    return c


def ilv(*gens):
    gens = list(gens)
    while gens:
        for g in list(gens):
            if g not in gens:
                continue
            try:
                next(g)
            except StopIteration:
                gens = [x for x in gens if x is not g]
        yield


def interleave(*gens):
    for _ in ilv(*gens):
        pass


class Arena:
    def __init__(self, base, nwords):
        self.base = base
        self.n = nwords
        self.off = 0
        self.bufs = {}

    def reset(self):
        self.off = 0
        self.bufs = {}

    def get(self, name, shape, dt=F32):
        if name in self.bufs:
            return self.bufs[name]
        nel = int(np.prod(shape[1:]))
        nby = nel * (4 if dt == F32 else 2)
        words = ((nby + 31) // 32) * 8
        assert self.off + words <= self.n, ("arena overflow", name, self.off, words, self.n)
        ap = self.base[0:shape[0], self.off:self.off + words]
        self.off += words
        if dt != F32:
            ap = ap.bitcast(dt)
        ap = ap[:, 0:nel]
        if len(shape) == 3:
            ap = ap.rearrange("p (a b) -> p a b", b=shape[2])
        self.bufs[name] = ap
        return ap


def build(nlayers=2, dbg=False, phases="ARMGD", rl=5, lim=(8, 4, 4)):
    nc = bass.Bass("TRN2", target_bir_lowering=False)
    DI = lambda n, s, dt=F32: nc.dram_tensor(n, s, dt, kind="ExternalInput").ap()
    x_d = DI("x", [S_, D_])
    win_d = DI("w_in", [2, D_, DIN])
    wout_d = DI("w_out", [2, D_, D_])
    pc_d = DI("pc", [2, 128, NPC])
    ph_d = DI("ph", [2, 64, 32])
    nw_d = DI("norm_w", [2, D_])
    fnw_d = DI("final_norm_w", [1, D_])
    w2_d = DI("rwkv_w2", [2, 64, 1024])
    a2_d = DI("rwkv_a2", [2, 64, 1024])
    aup_d = DI("gla_a_up", [2, 16, 256])
    cd = {}
    for n, a in host_consts().items():
        cd[n] = DI("c_" + n, list(a.shape), BF16 if a.dtype == ml_dtypes.bfloat16 else F32)
    out_d = nc.dram_tensor("out", [S_, D_], F32, kind="ExternalOutput").ap()
    xs_d = nc.dram_tensor("xs", [S_, D_], F32, kind="Internal").ap()
    mix_d = nc.dram_tensor("mixd", [16, 128, 16, 128], BF16, kind="Internal").ap()
    dbg_d = nc.dram_tensor("dbgmix", [128, 16, S_], BF16, kind="ExternalOutput").ap() if dbg else None

    with ExitStack() as es:
        S = Sched(nc, es)
        sb = lambda name, shape, dt=F32: es.enter_context(nc.sbuf_tensor("sb_" + name, shape, dt))
        ps = lambda name, shape, dt=F32: es.enter_context(nc.psum_tensor("ps_" + name, shape, dt))
        V = lambda fn, r=(), w=(): S.op("vector", fn, r, w)
        A = lambda fn, r=(), w=(): S.op("scalar", fn, r, w)
        G = lambda fn, r=(), w=(): S.op("gpsimd", fn, r, w)
        T = lambda fn, r=(), w=(): S.op("tensor", fn, r, w)

        hT = sb("hT", [128, 16, S_], BF16)
        ident = sb("ident", [128, 128], BF16)
        mskUL = sb("mskUL", [128, 256], BF16)
        mskC = sb("mskC", [64, 192], BF16)
        rmask = sb("rmask", [128, 512], BF16)
        onesBD = sb("onesBD", [128, 128], BF16)
        ones64 = sb("ones64", [64, 64], BF16)
        ones64f = sb("ones64f", [128, 64], BF16)
        ones128 = sb("ones128", [128, 128], BF16)
        cbias = sb("cbias", [128, 2, 256], BF16)
        pbias = sb("pbias", [128, 8, 8], F32)
        pc = sb("pc", [128, NPC], F32)
        omu = sb("omu", [128, 25], F32)
        ph = sb("ph", [64, 32], F32)
        w2a2 = sb("w2a2", [128, 1024], BF16)
        aupb = sb("aupb", [16, 256], BF16)
        for n, t in (("ident", ident), ("mskUL", mskUL), ("mskC", mskC), ("rmask", rmask), ("onesBD", onesBD),
                     ("ones64", ones64), ("ones64f", ones64f), ("ones128", ones128), ("cbias", cbias), ("pbias", pbias)):
            S.dma("sync", t[:], cd[n], writes=[n])
        epsc = sb("epsc", [128, 2], F32)
        G(lambda e: e.memset(epsc[:, 0:1], 64e-5), [], ["epsc"])
        G(lambda e: e.memset(epsc[:, 1:2], 1e-6), ["epsc"], ["epsc"])
        NW = 10
        wbf = [sb("wbf%d" % i, [128, 16, 128], BF16) for i in range(NW)]
        wctr = [0]
        ARW = ((nc.sbuf_bytes_remaining - 256) // 32) * 8
        print("arena words", ARW, "bytes", ARW * 4)
        arena_t = sb("arena", [128, ARW], F32)
        AR = Arena(arena_t, ARW)

        def B_(name, shape, dt=F32):
            return AR.get(name, shape, dt)

        def phase():
            S.barrier()
            AR.reset()

        def mix_store(src, skey, fc, q):
            S.dma("sync", mix_d[q * 4:(q + 1) * 4, :, fc, :].rearrange("tt p t -> p tt t"),
                  src[:, :].rearrange("p (tt t) -> p tt t", t=128), reads=[skey], writes=["mixd"])

        psb = ps("psb", [128, 2048], F32)
        pproj = [ps("pproj%d" % i, [128, 512], F32) for i in range(2)]
        ptr = ps("ptr", [128, 1024], BF16)
        pseq = ps("pseq", [128, 512], F32)
        pjc = [0]

        def load_w(src_ap, ncols):
            wi = wctr[0] % NW
            wctr[0] += 1
            S.dma("gpsimd", wbf[wi][:, :, 0:ncols], src_ap.rearrange("(c p) n -> p c n", p=128),
                  writes=["wbf%d" % wi])
            return wi

        def load_w_at(src_ap, ncols, wi):
            S.dma("gpsimd", wbf[wi][:, :, 0:ncols], src_ap.rearrange("(c p) n -> p c n", p=128),
                  writes=["wbf%d" % wi])
            return wi

        def proj_fm(wi, ncols, q):
            pi = pjc[0] % 2
            pjc[0] += 1
            pt = pproj[pi]
            for c in range(16):
                T(lambda e, c=c: e.matmul(pt[0:ncols, :], lhsT=wbf[wi][:, c, 0:ncols],
                                          rhs=hT[:, c, q * 512:(q + 1) * 512], start=(c == 0), stop=(c == 15)),
                  ["wbf%d" % wi, "hT"], ["pproj%d" % pi])
            return pt, "pproj%d" % pi

        def proj_gen(wi, ncols, q, out):
            pi = pjc[0] % 2
            pjc[0] += 1
            pt = pproj[pi]
            for c in range(16):
                T(lambda e, c=c: e.matmul(pt[0:ncols, :], lhsT=wbf[wi][:, c, 0:ncols],
                                          rhs=hT[:, c, q * 512:(q + 1) * 512], start=(c == 0), stop=(c == 15)),
                  ["wbf%d" % wi, "hT"], ["pproj%d" % pi])
                if c % FINE_P == FINE_P - 1:
                    yield
            out.append((pt, "pproj%d" % pi))

        def rms_stats(xin, h16, st4, xk="xin", hk="h16", sk="st4"):
            A(lambda e: e.activation(h16[:], xin[:], AF.Square, accum_out=st4[:, 0:1]), [xk], [hk, sk])
            V(lambda e: e.tensor_scalar(st4[:, 1:2], st4[:, 0:1], 1.0 / D_, 1e-6, op0=ALU.mult, op1=ALU.add),
              [sk], [sk])
            A(lambda e: e.activation(st4[:, 2:3], st4[:, 1:2], AF.Sqrt), [sk], [sk])
            V(lambda e: e.reciprocal(st4[:, 3:4], st4[:, 2:3]), [sk], [sk])

        def shift_evac(pt, pk_, mi, q, n, dst, dkey):
            Bb = B_("Bb", [128, 513])
            carry = B_("carry", [128, 4])
            if q == 0:
                G(lambda e: e.memset(Bb[:, 0:1], 0.0), [], ["Bb"])
            else:
                G(lambda e: e.tensor_copy(Bb[:, 0:1], carry[:, n:n + 1]), ["carry"], ["Bb"])
            A(lambda e: e.mul(Bb[:, 1:513], pt[:], pc[:, MU + mi:MU + mi + 1]), [pk_, "pc"], ["Bb"])
            G(lambda e: e.tensor_copy(carry[:, n:n + 1], Bb[:, 512:513]), ["Bb"], ["carry"])
            V(lambda e: e.scalar_tensor_tensor(dst, pt[:], omu[:, mi:mi + 1], Bb[:, 0:512], op0=ALU.mult, op1=ALU.add),
              [pk_, "omu", "Bb"], [dkey])

        sN = lambda i: B_("s%d" % i, [128, 512])

        for l in range(nlayers):
            phase()
            S.dma("sync", pc[:], pc_d[l], writes=["pc"])
            S.dma("sync", ph[:], ph_d[l], writes=["ph"])
            V(lambda e: e.tensor_scalar(omu[:], pc[:, MU:MU + 25], -1.0, 1.0, op0=ALU.mult, op1=ALU.add),
              ["pc"], ["omu"])
            S.dma("gpsimd", w2a2[0:64, :], w2_d[l], writes=["w2a2"])
            S.dma("gpsimd", w2a2[64:128, :], a2_d[l], writes=["w2a2"])
            S.dma("gpsimd", aupb[:], aup_d[l], writes=["aupb"])
            nwb = B_("nwb", [128, D_])
            xin = B_("xin", [128, D_])
            h16 = B_("h16", [128, D_], BF16)
            st4 = B_("st4", [128, 8])
            S.dma("sync", nwb[:], nw_d[l:l + 1, :].partition_broadcast(128), writes=["nwb"])
            xsrc = x_d if l == 0 else xs_d
            for tt in range(16 if "A" in phases else 0):
                xk_, hk_, sk_ = "xin%d" % (tt % 2), "h16%d" % (tt % 2), "st4%d" % (tt % 2)
                xin = B_(xk_, [128, D_])
                h16 = B_(hk_, [128, D_], BF16)
                st4 = B_(sk_, [128, 8])
                S.dma("sync", xin[:], xsrc[tt * 128:(tt + 1) * 128, :], reads=["xs%d" % tt], writes=[xk_])
                rms_stats(xin, h16, st4, xk_, hk_, sk_)
                V(lambda e: e.scalar_tensor_tensor(h16[:], xin[:], st4[:, 3:4], nwb[:], op0=ALU.mult, op1=ALU.mult),
                  [xk_, sk_, "nwb", hk_], [hk_])
                for half in range(2):
                    for c8 in range(8):
                        c = half * 8 + c8
                        T(lambda e, c=c, c8=c8: e.transpose(ptr[:, c8 * 128:(c8 + 1) * 128], h16[:, c * 128:(c + 1) * 128],
                                                            ident[:]),
                          [hk_, "ident"], ["ptr"])
                    dst = hT[:, half * 8:(half + 1) * 8, tt * 128:(tt + 1) * 128]
                    src = ptr[:, :].rearrange("p (c t) -> p c t", t=128)
                    if half == 0:
                        A(lambda e, dst=dst, src=src: e.copy(dst, src), ["ptr"], ["hT"])
                    else:
                        V(lambda e, dst=dst, src=src: e.tensor_copy(dst, src), ["ptr"], ["hT"])

            wl = win_d[l]

            phase()
            if "R" in phases:
                twad = B_("twad", [128, S_], BF16)
                wi = load_w(wl[:, 3072:3200], 128)
                for q in range(4):
                    sl = slice(q * 512, (q + 1) * 512)
                    pt, pk_ = proj_fm(wi, 128, q)
                    xw = B_("s1", [128, 512])
                    shift_evac(pt, pk_, 24, q, 3, xw, "s1")
                    A(lambda e, sl=sl: e.activation(twad[0:64, sl], xw[0:64, :], AF.Tanh), ["s1"], ["twad"])
                    G(lambda e, sl=sl: e.tensor_copy(twad[64:128, sl], xw[64:128, :]), ["s1"], ["twad"])

                QB = []
                for par in range(2):
                    d = {}
                    for n in ("AT", "RT", "BT", "KT", "BH", "KH", "V16", "SG", "BON", "MST"):
                        d[n] = (B_("%s_%d" % (n, par), [128, 512], BF16), "%s_%d" % (n, par))
                    for n in ("ATo", "RTo", "BTo", "KTo"):
                        d[n] = (B_("%s_%d" % (n, par), [64, 512], BF16), "%s_%d" % (n, par))
                    d["YS"] = (B_("YS_%d" % par, [64, 2, 512], BF16), "YS_%d" % par)
                    d["GAM"] = (B_("GAM_%d" % par, [128, 8]), "GAM_%d" % par)
                    d["GAMo"] = (B_("GAMo_%d" % par, [64, 8]), "GAMo_%d" % par)
                    QB.append(d)
                Tst = [B_("Tst%d" % j, [64, 64]) for j in range(2)]
                Tb = [B_("Tb%d" % j, [64, 64], BF16) for j in range(2)]
                PQg = [B_("PQg%d" % i, [128, 4, 256], BF16) for i in range(2)]
                Wtg = [B_("Wtg%d" % i, [128, 4, 128], BF16) for i in range(2)]
                Q5g = B_("Q5g", [128, 4, 128], BF16)
                Wfg = [B_("Wfg%d" % gs, [64, 4, 128], BF16) for gs in range(2)]
                AMg = [[B_("AMg%d_%d" % (gs, e_), [64, 4, 192], BF16) for e_ in range(2)] for gs in range(2)]
                TMg = [B_("TMg%d" % gs, [64, 4, 384], BF16) for gs in range(2)]
                zxs = [B_("zx%d" % j, [64, 64], BF16) for j in range(2)]
                uus = [B_("uu%d" % j, [64, 64], BF16) for j in range(2)]
                eN = lambda i: B_("e%d" % i, [64, 512])
                wsl = {}

                def prep_w(hp, q):
                    if hp == 0 and q == 0:
                        wsl[0] = [load_w(wl[:, o + 0 * 128:o + 1 * 128], 128) for o in (0, 1024, 2048, 3200)]
                    if q == 0 and hp + 1 < lim[0]:
                        h1 = hp + 1
                        wsl[h1] = [load_w(wl[:, o + h1 * 128:o + (h1 + 1) * 128], 128) for o in (0, 1024, 2048, 3200)]

                def prepA1_gen(hp, q, par):
                    prep_w(hp, q)
                    wr, wk, wv, wg = wsl[hp]
                    col = lambda base: pc[:, base + hp:base + hp + 1]
                    o_ = []
                    yield from proj_gen(wk, 128, q, o_)
                    pt, pk_ = o_.pop()
                    xk = sN(2)
                    shift_evac(pt, pk_, 8 + hp, q, 1, xk[:], "s2")
                    yield
                    kk = sN(10)
                    A(lambda e: e.mul(kk[:], xk[:], col(KK)), ["s2", "pc"], ["s10"])
                    s11 = sN(11)
                    kk2b = B_("kk2b", [128, 512], BF16)
                    A(lambda e: e.activation(kk2b[:], kk[:], AF.Square), ["s10"], ["kk2b"])
                    pi_ = pjc[0] % 2
                    pjc[0] += 1
                    pss = pproj[pi_][:, :]
                    T(lambda e: e.matmul(pss, lhsT=onesBD[:], rhs=kk2b[:], start=True, stop=True), ["onesBD", "kk2b"], ["pproj%d" % pi_])
                    V(lambda e: e.tensor_scalar_max(s11[:], pss, 1e-19), ["pproj%d" % pi_], ["s11"])
                    yield
                    yield from proj_gen(wr, 128, q, o_)
                    pt, pk_ = o_.pop()
                    xr = sN(1)
                    shift_evac(pt, pk_, hp, q, 0, xr[:], "s1")
                    yield
                    A(lambda e: e.activation(s11[:], s11[:], AF.Ln), ["s11"], ["s11"])
                    A(lambda e: e.activation(s11[:], s11[:], AF.Exp, scale=-0.5), ["s11"], ["s11"])
                    kkn = sN(12)
                    V(lambda e: e.tensor_tensor(kkn[:], kk[:], s11[:], ALU.mult), ["s10", "s11"], ["s12"])
                    yield

                def prepA2_gen(hp, q, par):
                    col = lambda base: pc[:, base + hp:base + hp + 1]
                    gsl = slice(q * 512, (q + 1) * 512)
                    pw = pseq[:, 0:512]
                    T(lambda e: e.matmul(pw, lhsT=w2a2[0:64, hp * 128:(hp + 1) * 128], rhs=twad[0:64, gsl],
                                         start=True, stop=True), ["w2a2", "twad"], ["pseq"])
                    sig = sN(3)
                    A(lambda e: e.activation(sig[:], pw, AF.Sigmoid, bias=col(W0)), ["pseq", "pc"], ["s3"])
                    yield
                    pa = pseq[:, 0:512]
                    T(lambda e: e.matmul(pa, lhsT=w2a2[64:128, hp * 128:(hp + 1) * 128], rhs=twad[64:128, gsl],
                                         start=True, stop=True), ["w2a2", "twad"], ["pseq"])
                    ai = sN(7)
                    A(lambda e: e.activation(ai[:], pa, AF.Sigmoid, bias=col(A0)), ["pseq", "pc"], ["s7"])
                    yield
                    cs = sN(4)
                    V(lambda e: e.tensor_tensor_scan(cs[:], rmask[:], sig[:], 0.0, ALU.mult, ALU.add), ["rmask", "s3"], ["s4"])
                    cs3 = cs[:].rearrange("p (c t) -> p c t", t=64)
                    eM = sN(6)
                    A(lambda e: e.activation(eM[:], cs[:], AF.Exp, scale=C0), ["s4"], ["s6"])
                    yield
                    s5 = sN(5)
                    V(lambda e: e.tensor_tensor(s5[:].rearrange("p (c t) -> p c t", t=64),
                                                cs3[:, :, 63:64].to_broadcast([128, 8, 64]), cs3, ALU.subtract), ["s4"], ["s5"])
                    eC = sN(8)
                    A(lambda e: e.activation(eC[:], s5[:], AF.Exp, scale=-C0), ["s5"], ["s8"])
                    yield
                    V(lambda e: e.tensor_tensor(s5[:], cs[:], sig[:], ALU.subtract), ["s4", "s3"], ["s5"])
                    A(lambda e: e.activation(s5[:], s5[:], AF.Exp, scale=-C0), ["s5"], ["s5"])
                    yield

                def prepB1_gen(hp, q, par):
                    qb = QB[par]
                    AT, ATk = qb["AT"]; RT, RTk = qb["RT"]; BT, BTk = qb["BT"]; KT, KTk = qb["KT"]
                    BH, BHk = qb["BH"]; KH, KHk = qb["KH"]
                    BON, BONk = qb["BON"]; ATo, ATok = qb["ATo"]; RTo, RTok = qb["RTo"]
                    BTo, BTok = qb["BTo"]; KTo, KTok = qb["KTo"]; GAM, GAMk = qb["GAM"]; GAMo, GAMok = qb["GAMo"]
                    col = lambda base: pc[:, base + hp:base + hp + 1]
                    xr, xk, cs, ePp, eM, eC, kkn, ai = sN(1), sN(2), sN(4), sN(5), sN(6), sN(8), sN(12), sN(7)
                    cs3 = cs[:].rearrange("p (c t) -> p c t", t=64)
                    bv = sN(10)
                    V(lambda e: e.tensor_tensor(bv[:], kkn[:], ai[:], ALU.mult), ["s12", "s7"], ["s10"])
                    t1 = sN(3)
                    V(lambda e: e.tensor_scalar(t1[:], ai[:], -1.0, col(KA), op0=ALU.add, op1=ALU.mult), ["s7", "pc"], ["s3"])
                    k2 = sN(9)
                    V(lambda e: e.scalar_tensor_tensor(k2[:], t1[:], 1.0, xk[:], op0=ALU.add, op1=ALU.mult), ["s3", "s2"], ["s9"])
                    yield
                    ex = sN(13)
                    A(lambda e: e.activation(ex[:], cs[:], AF.Exp, scale=-C0), ["s4"], ["s13"])
                    A(lambda e: e.activation(GAM[:, :], cs3[:, :, 63], AF.Exp, scale=-C0), ["s4"], [GAMk])
                    G(lambda e: e.tensor_copy(GAMo[:, :], GAM[64:128, :]), [GAMk], [GAMok])
                    yield
                    V(lambda e: e.tensor_tensor(RT[:], xr[:], ex[:], ALU.mult), ["s13", "s1"], [RTk])
                    G(lambda e: e.tensor_copy(RTo[:], RT[64:128, :]), [RTk], [RTok])
                    yield
                    V(lambda e: e.scalar_tensor_tensor(AT[:], kkn[:], -1.0, ePp[:], op0=ALU.mult, op1=ALU.mult), ["s12", "s5"], [ATk])
                    G(lambda e: e.tensor_copy(ATo[:], AT[64:128, :]), [ATk], [ATok])
                    yield
                    V(lambda e: e.tensor_tensor(BT[:], bv[:], eM[:], ALU.mult), ["s10", "s6"], [BTk])
                    G(lambda e: e.tensor_copy(BTo[:], BT[64:128, :]), [BTk], [BTok])
                    yield
                    V(lambda e: e.tensor_tensor(BH[:], bv[:], eC[:], ALU.mult), ["s10", "s8"], [BHk])
                    yield
                    V(lambda e: e.tensor_tensor(KT[:], k2[:], eM[:], ALU.mult), ["s9", "s6"], [KTk])
                    G(lambda e: e.tensor_copy(KTo[:], KT[64:128, :]), [KTk], [KTok])
                    yield
                    V(lambda e: e.tensor_tensor(KH[:], k2[:], eC[:], ALU.mult), ["s9", "s8"], [KHk])
                    rk = B_("rkb", [128, 512], BF16)
                    V(lambda e: e.scalar_tensor_tensor(rk[:], xr[:], col(RK), k2[:], op0=ALU.mult, op1=ALU.mult), ["s1", "s9", "pc"], ["rkb"])
                    yield
                    pbs = pseq[:, 0:512]
                    T(lambda e: e.matmul(pbs, lhsT=onesBD[:], rhs=rk[:], start=True, stop=True), ["onesBD", "rkb"], ["pseq"])
                    A(lambda e: e.copy(BON[:], pbs), ["pseq"], [BONk])
                    yield

                def prepB2_gen(hp, q, par):
                    qb = QB[par]
                    V16, V16k = qb["V16"]; SG, SGk = qb["SG"]
                    wr, wk, wv, wg = wsl[hp]
                    o_ = []
                    yield from proj_gen(wv, 128, q, o_)
                    pt, pk_ = o_.pop()
                    shift_evac(pt, pk_, 16 + hp, q, 2, V16[:], V16k)
                    yield
                    yield from proj_gen(wg, 128, q, o_)
                    pt, pk_ = o_.pop()
                    A(lambda e: e.activation(SG[:], pt[:], AF.Silu), [pk_], [SGk])
                    yield


                def ngroup_gen(g, gs, qb):
                    bc4 = lambda ap, n: ap.rearrange("p (o n) -> p o n", o=1).to_broadcast([ap.shape[0], 4 if n is None else n, ap.shape[1]])
                    TM = TMg[gs]
                    tmk = "TMg%d" % gs
                    for half in range(2):
                        for ee in range(2):
                            ci = half * 2 + ee
                            c0 = (2 * g) * 128 + ci * 64
                            for k_, nm in enumerate(("V16", "BH", "KH")):
                                src, kn_ = qb[nm]
                                T(lambda e: e.transpose(ptr[0:64, ee * 384 + k_ * 128: ee * 384 + (k_ + 1) * 128],
                                                        src[:, c0:c0 + 64], ident[:]), [kn_, "ident"], ["ptr"])
                        srcp = ptr[0:64, 0:768].rearrange("p (c n) -> p c n", n=384)
                        if half == 0:
                            A(lambda e: e.copy(TM[:, 0:2, :], srcp), ["ptr"], [tmk])
                        else:
                            V(lambda e: e.tensor_copy(TM[:, 2:4, :], srcp), ["ptr"], [tmk])
                        yield
                    ops = []
                    for c in range(4):
                        p, j = 2 * g + c // 2, c % 2
                        Ab, Abk = qb["AT"] if j == 0 else qb["ATo"]
                        Rb, Rbk = qb["RT"] if j == 0 else qb["RTo"]
                        Bb_, Bbk = qb["BT"] if j == 0 else qb["BTo"]
                        Kb, Kbk = qb["KT"] if j == 0 else qb["KTo"]
                        ops.append((p * 128, Ab, Abk, Rb, Rbk, Bb_, Bbk, Kb, Kbk))
                    bkc = lambda c: "psb%d" % (c // 2)
                    pn = psb[:, 0:1024].rearrange("p (c n) -> p c n", n=256)
                    pq4 = ptr[:, :].bitcast(F32).rearrange("p (c n) -> p c n", n=128)
                    id4 = ident[:, :].rearrange("p (o n) -> p o n", o=1).to_broadcast([128, 4, 128])
                    for c in range(4):
                        t0, Ab, Abk, Rb, Rbk, Bb_, Bbk, Kb, Kbk = ops[c]
                        T(lambda e: e.matmul(pn[:, c, 0:128], lhsT=Bb_[0:64, t0:t0 + 128], rhs=Ab[0:64, t0:t0 + 128],
                                             start=True, stop=True), [Bbk, Abk], [bkc(c)])
                        T(lambda e: e.matmul(pn[:, c, 128:256], lhsT=Ab[0:64, t0:t0 + 128], rhs=Bb_[0:64, t0:t0 + 128],
                                             start=True, stop=True), [Bbk, Abk], [bkc(c)])
                    yield
                    mk2 = mskUL[:, :].rearrange("p (o n) -> p o n", o=1).to_broadcast([128, 2, 256])
                    V(lambda e: e.tensor_tensor(PQg[0][:, 0:2, :], pn[:, 0:2, :], mk2, ALU.mult), ["psb0", "mskUL"], ["PWa0"])
                    V(lambda e: e.tensor_tensor(PQg[0][:, 2:4, :], pn[:, 2:4, :], mk2, ALU.mult), ["psb1", "mskUL"], ["PWb0"])
                    A(lambda e: e.copy(Wtg[0][:], PQg[0][:, :, 128:256]), ["PWa0", "PWb0"], ["Wtg0"])
                    V(lambda e: e.tensor_copy(PQg[0][:, :, 128:256], id4), ["Wtg0", "ident", "PWa0", "PWb0"], ["PWa0", "PWb0"])
                    yield
                    cur = 0
                    for lev in range(5):
                        nxt = 1 - cur
                        PW, Qc = PQg[cur], Wtg[cur]
                        ka, kb, kq = "PWa%d" % cur, "PWb%d" % cur, "Wtg%d" % cur
                        na, nb, nq = "PWa%d" % nxt, "PWb%d" % nxt, "Wtg%d" % nxt
                        for c in range(4):
                            kk_ = ka if c < 2 else kb
                            T(lambda e: e.matmul(pn[:, c, :], lhsT=Qc[:, c, :], rhs=PW[:, c, :], start=True, stop=True),
                              [kk_, kq], [bkc(c)])
                            T(lambda e: e.matmul(pq4[:, c, :], lhsT=PW[:, c, 0:128], rhs=Qc[:, c, :], start=True, stop=True),
                              [kk_, kq], ["ptr"])
                        yield
                        A(lambda e: e.copy(Wtg[nxt][:], pq4), ["ptr"], [nq])
                        V(lambda e: e.tensor_tensor(PQg[nxt][:, 0:2, 128:256], pn[:, 0:2, 128:256], PW[:, 0:2, 128:256], ALU.add),
                          ["psb0", ka], [na])
                        V(lambda e: e.tensor_tensor(PQg[nxt][:, 2:4, 128:256], pn[:, 2:4, 128:256], PW[:, 2:4, 128:256], ALU.add),
                          ["psb1", kb], [nb])
                        A(lambda e: e.copy(PQg[nxt][:, 0:2, 0:128], pn[:, 0:2, 0:128]), ["psb0", na], [na])
                        A(lambda e: e.copy(PQg[nxt][:, 2:4, 0:128], pn[:, 2:4, 0:128]), ["psb1", nb], [nb])
                        yield
                        cur = nxt
                    PW, Qc = PQg[cur], Wtg[cur]
                    ka, kb, kq = "PWa%d" % cur, "PWb%d" % cur, "Wtg%d" % cur
                    pw4 = psb[:, 0:512].rearrange("p (c n) -> p c n", n=128)
                    for c in range(4):
                        T(lambda e: e.matmul(pw4[:, c, :], lhsT=Qc[:, c, :], rhs=PW[:, c, 128:256], start=True, stop=True),
                          [ka, kb, kq], ["psb0"])
                    yield
                    Wf = Wfg[gs]
                    wfk = "Wfg%d" % gs
                    V(lambda e: e.tensor_tensor(Wf[:, :, 0:64], pw4[0:64, :, 0:64], PW[0:64, :, 128:192], ALU.add),
                      ["psb0", ka, kb], [wfk])
                    V(lambda e: e.tensor_tensor(Wf[:, :, 64:128], pw4[64:128, :, 64:128], PW[64:128, :, 192:256], ALU.add),
                      ["psb0", ka, kb, wfk], [wfk])
                    yield

                    mc2 = mskC[:, :].rearrange("p (o n) -> p o n", o=1).to_broadcast([64, 2, 192])
                    for e_ in range(2):
                        for c in range(4):
                            t0, Ab, Abk, Rb, Rbk, Bb_, Bbk, Kb, Kbk = ops[c]
                            c0 = t0 + e_ * 64
                            bnk = 1 if c < 2 else 0
                            pam = psb[0:64, bnk * 512 + (c % 2) * 192: bnk * 512 + (c % 2 + 1) * 192]
                            T(lambda e: e.matmul(pam[:, 0:64], lhsT=Bb_[0:64, c0:c0 + 64], rhs=Rb[0:64, c0:c0 + 64],
                                                 start=True, stop=True), [Bbk, Rbk], ["psb%d" % bnk])
                            T(lambda e: e.matmul(pam[:, 64:128], lhsT=Kb[0:64, c0:c0 + 64], rhs=Ab[0:64, c0:c0 + 64],
                                                 start=True, stop=True), [Kbk, Abk], ["psb%d" % bnk])
                            T(lambda e: e.matmul(pam[:, 128:192], lhsT=Kb[0:64, c0:c0 + 64], rhs=Rb[0:64, c0:c0 + 64],
                                                 start=True, stop=True), [Kbk, Rbk], ["psb%d" % bnk])
                            if FINE_N:
                                yield
                        if not FINE_N:
                            yield
                        am = AMg[gs][e_]
                        amk = "AMg%d_%d" % (gs, e_)
                        V(lambda e: e.tensor_tensor(am[:, 0:2, :], psb[0:64, 512:896].rearrange("p (c n) -> p c n", n=192), mc2, ALU.mult),
                          ["psb1", "mskC"], [amk])
                        V(lambda e: e.tensor_tensor(am[:, 2:4, :], psb[0:64, 0:384].rearrange("p (c n) -> p c n", n=192), mc2, ALU.mult),
                          ["psb0", "mskC", amk], [amk])
                        yield

                def seq_gen(j, p, gs, qb):
                    c = (p % 2) * 2 + j
                    Ab, Abk = qb["AT"] if j == 0 else qb["ATo"]
                    Rb, Rbk = qb["RT"] if j == 0 else qb["RTo"]
                    Gm, Gmk = qb["GAM"] if j == 0 else qb["GAMo"]
                    YS, YSk = qb["YS"]
                    sbank = psb[:, (2 + j) * 512:(3 + j) * 512]
                    sk_ = "psb%d" % (2 + j)
                    zx, uu = zxs[j], uus[j]
                    Wf = Wfg[gs][:, c, :]
                    wfk = "Wfg%d" % gs
                    tmk = "TMg%d" % gs
                    for e_ in range(2):
                        cl = 2 * p + e_
                        c0 = cl * 64
                        am = AMg[gs][e_][:, c, :]
                        amk = "AMg%d_%d" % (gs, e_)
                        tm = TMg[gs][:, (p % 2) * 2 + e_, :]
                        vj = tm[:, 64 * j:64 * j + 64]
                        bhj = tm[:, 128 + 64 * j:128 + 64 * j + 64]
                        khj = tm[:, 256 + 64 * j:256 + 64 * j + 64]
                        pzx = sbank[0:64, 0:64]
                        T(lambda e: e.matmul(pzx, lhsT=am[:, 64:128], rhs=vj, start=True, stop=False), [amk, tmk], [sk_])
                        T(lambda e: e.matmul(pzx, lhsT=Ab[0:64, c0:c0 + 64], rhs=Tb[j][:], start=False, stop=True),
                          [Abk, "Tb%d" % j], [sk_])
                        yield
                        A(lambda e: e.copy(zx[:], pzx), [sk_], ["zx%d" % j])
                        yield
                        pu = sbank[0:64, 64:128]
                        T(lambda e: e.matmul(pu, lhsT=Wf[:, e_ * 64:(e_ + 1) * 64], rhs=zx[:], start=True, stop=True),
                          [wfk, "zx%d" % j], [sk_])
                        yield
                        V(lambda e: e.tensor_copy(uu[:], pu), [sk_], ["uu%d" % j])
                        yield
                        ptn = sbank[0:64, 192:256]
                        T(lambda e: e.matmul(ptn, lhsT=bhj, rhs=uu[:], start=True, stop=False), [tmk, "uu%d" % j], [sk_])
                        T(lambda e: e.matmul(ptn, lhsT=khj, rhs=vj, start=False, stop=True), [tmk], [sk_])
                        py = sbank[0:64, 128:192]
                        T(lambda e: e.matmul(py, lhsT=Tb[j][:], rhs=Rb[0:64, c0:c0 + 64], start=True, stop=False),
                          ["Tb%d" % j, Rbk], [sk_])
                        T(lambda e: e.matmul(py, lhsT=uu[:], rhs=am[:, 0:64], start=False, stop=False), ["uu%d" % j, amk], [sk_])
                        T(lambda e: e.matmul(py, lhsT=vj, rhs=am[:, 128:192], start=False, stop=True), [tmk, amk], [sk_])
                        yield
                        V(lambda e: e.scalar_tensor_tensor(Tb[j][:], Tst[j][:], Gm[0:64, cl:cl + 1], ptn,
                                                           op0=ALU.mult, op1=ALU.add),
                          ["Tst%d" % j, Gmk, sk_], ["Tb%d" % j])
                        V(lambda e: e.scalar_tensor_tensor(Tst[j][:], Tst[j][:], Gm[0:64, cl:cl + 1], ptn,
                                                           op0=ALU.mult, op1=ALU.add),
                          ["Tst%d" % j, Gmk, sk_], ["Tst%d" % j])
                        A(lambda e: e.copy(YS[:, j, c0:c0 + 64], py), [sk_], [YSk])
                        yield


                def epi_gen(hp, q, j, qb):
                    h = 2 * hp + j
                    js = slice(64 * j, 64 * j + 64)
                    YS, YSk = qb["YS"]; BON, BONk = qb["BON"]; V16, V16k = qb["V16"]; SG, SGk = qb["SG"]; MST, MSTk = qb["MST"]
                    pe_ = psb[0:64, 1536 + 256:2048]
                    dd = eN(0)
                    d2 = B_("d2b", [64, 512], BF16)
                    sd = eN(2)
                    for hh in range(2):
                        hs = slice(hh * 256, (hh + 1) * 256)
                        T(lambda e: e.matmul(pe_, lhsT=ones64[:], rhs=YS[:, j, hs], start=True, stop=True), ["ones64", YSk], ["psb3"])
                        V(lambda e: e.scalar_tensor_tensor(dd[:, hs], pe_, -1.0, YS[:, j, hs], op0=ALU.mult, op1=ALU.add), ["psb3", YSk], ["e0"])
                        yield
                    A(lambda e: e.activation(d2[:], dd[:], AF.Square), ["e0"], ["d2b"])
                    for hh in range(2):
                        hs = slice(hh * 256, (hh + 1) * 256)
                        T(lambda e: e.matmul(pe_, lhsT=ones64f[0:64, :], rhs=d2[:, hs], start=True, stop=True), ["ones64f", "d2b"], ["psb3"])
                        A(lambda e: e.activation(sd[:, hs], pe_, AF.Ln, bias=epsc[0:64, 0:1]), ["psb3", "epsc"], ["e2"])
                        yield
                    A(lambda e: e.activation(sd[:], sd[:], AF.Exp, scale=-0.5), ["e2"], ["e2"])
                    V(lambda e: e.tensor_tensor(dd[:], dd[:], sd[:], ALU.mult), ["e0", "e2"], ["e0"])
                    yield
                    A(lambda e: e.activation(dd[:], dd[:], AF.Identity, bias=ph[:, 16 + h:17 + h], scale=ph[:, h:h + 1]),
                      ["e0", "ph"], ["e0"])
                    bvv = eN(1)
                    V(lambda e: e.tensor_tensor(bvv[:], BON[js, :], V16[js, :], ALU.mult), [BONk, V16k], ["e1"])
                    V(lambda e: e.tensor_tensor(dd[:], dd[:], bvv[:], ALU.add), ["e0", "e1"], ["e0"])
                    yield
                    sgs = eN(2)
                    G(lambda e: e.tensor_copy(sgs[:], SG[js, :]), [SGk], ["e2"])
                    V(lambda e: e.tensor_tensor(MST[js, :], dd[:], sgs[:], ALU.mult), ["e0", "e2"], [MSTk])
                    yield
                    if j == 1:
                        mix_store(MST, MSTk, hp, q)


                def sgroup_gen(hp, q, g, gs, par):
                    qb = QB[par]
                    if q == 0 and g == 0:
                        for j in range(2):
                            G(lambda e: e.memset(Tst[j][:], 0.0), [], ["Tst%d" % j])
                            G(lambda e: e.memset(Tb[j][:], 0.0), [], ["Tb%d" % j])
                    for p in (2 * g, 2 * g + 1):
                        yield from ilv(seq_gen(0, p, gs, qb), seq_gen(1, p, gs, qb))
                    if g == 1 and rl >= 5:
                        yield from epi_gen(hp, q, 0, qb)
                        yield from epi_gen(hp, q, 1, qb)

                quarters = [(hp, q) for hp in range(lim[0] if rl >= 2 else 0) for q in range(lim[1])]
                items = [(k, g) for k in range(len(quarters)) for g in range(2)]
                if quarters:
                    S.merge([prepA1_gen(quarters[0][0], quarters[0][1], 0), prepA2_gen(quarters[0][0], quarters[0][1], 0)])
                    S.merge([prepB1_gen(quarters[0][0], quarters[0][1], 0), prepB2_gen(quarters[0][0], quarters[0][1], 0)])
                for i in range(len(items) + 1):
                    gl = []
                    if i < len(items):
                        k, g = items[i]
                        if not os.environ.get("NO_N"):
                            gl.append(ngroup_gen(g, i % 2, QB[k % 2]))
                        if k + 1 < len(quarters):
                            for pg in ((prepA1_gen, prepA2_gen) if g == 0 else (prepB1_gen, prepB2_gen)):
                                gl.append(pg(quarters[k + 1][0], quarters[k + 1][1], (k + 1) % 2))
                    if i >= 1:
                        k, g = items[i - 1]
                        if not os.environ.get("NO_S"):
                            sg = sgroup_gen(quarters[k][0], quarters[k][1], g, (i - 1) % 2, k % 2)
                            gl.append(sg)
                    if MERGE:
                        S.merge(gl)
                    else:
                        interleave(*gl)

                print("rwkv arena used words", AR.off)


            phase()
            def moba_gen():
                if "M" not in phases:
                    return
                yield
                vtok = B_("vtok", [128, 16, 512], BF16)
                wvs = [load_w_at(wl[:, 5248 + i * 128:5248 + (i + 1) * 128], 128, i) for i in range(4)]
                mslots = {0: (4, 5, 0), 1: (1, 2, 3), 2: (4, 5, 0), 3: (1, 2, 3)}
                msrc = lambda h: (wl[:, 4224 + h * 128:4224 + (h + 1) * 128], wl[:, 4736 + h * 128:4736 + (h + 1) * 128],
                                  wl[:, 5760 + h * 128:5760 + (h + 1) * 128])
                load_w_at(msrc(0)[0], 128, 4)
                load_w_at(msrc(0)[1], 128, 5)
                for tt in range(16):
                    pi = pjc[0] % 2
                    pjc[0] += 1
                    for i in range(4):
                        for c in range(16):
                            T(lambda e, c=c, i=i, pi=pi, tt=tt: e.matmul(pproj[pi][:, i * 128:(i + 1) * 128],
                                                                       lhsT=hT[:, c, tt * 128:(tt + 1) * 128],
                                                                       rhs=wbf[wvs[i]][:, c, :], start=(c == 0), stop=(c == 15)),
                              ["hT", "wbf%d" % wvs[i]], ["pproj%d" % pi])
                    A(lambda e, pi=pi, tt=tt: e.copy(vtok[:, tt, :], pproj[pi][:]), ["pproj%d" % pi], ["vtok"])
                    yield
                QT = B_("QT", [128, S_], BF16)
                KTm = B_("KTm", [128, S_], BF16)
                SGm = B_("SGm", [128, S_], BF16)
                ksum = B_("ksum", [128, 8])
                ksb = B_("ksb", [128, 8], BF16)
                Sm = B_("Sm", [128, S_])
                Ex = B_("Ex", [128, S_], BF16)
                ET = B_("ET", [128, 16, 128], BF16)
                gm = B_("gm", [128, 8])
                t8 = B_("t8", [128, 8])
                mx = B_("mx", [128, 4])
                on = B_("on", [128, 128], BF16)
                sc = float(128 ** -0.5)
                load_w_at(msrc(0)[2], 128, 0)
                for k_ in range(3):
                    load_w_at(msrc(1)[k_], 128, mslots[1][k_])
                for h in range(4):
                    wq, wk, wg = mslots[h]
                    for q in range(4):
                        sl = slice(q * 512, (q + 1) * 512)
                        pt, pk_ = proj_fm(wq, 128, q)
                        A(lambda e, pt=pt, sl=sl: e.copy(QT[:, sl], pt[:]), [pk_], ["QT"])
                        pt, pk_ = proj_fm(wk, 128, q)
                        for hb in range(2):
                            n = q * 2 + hb
                            A(lambda e, pt=pt, n=n, hb=hb: e.activation(KTm[:, n * 256:(n + 1) * 256], pt[:, hb * 256:(hb + 1) * 256],
                                                                        AF.Copy, accum_out=ksum[:, n:n + 1]),
                              [pk_], ["KTm", "ksum"])
                        pt, pk_ = proj_fm(wg, 128, q)
                        A(lambda e, pt=pt, sl=sl: e.activation(SGm[:, sl], pt[:], AF.Silu), [pk_], ["SGm"])
                        yield
                    V(lambda e: e.tensor_copy(ksb[:], ksum[:]), ["ksum"], ["ksb"])
                    if h + 2 < 4:
                        for k_ in range(3):
                            load_w_at(msrc(h + 2)[k_], 128, mslots[h + 2][k_])
                    for i in range(16):
                        b = i // 2
                        nk = (b + 1) * 256
                        qs = slice(i * 128, (i + 1) * 128)
                        nkb = (nk + 511) // 512
                        for kb in range(nkb):
                            w_ = min(512, nk - kb * 512)
                            T(lambda e, kb=kb, w_=w_, qs=qs: e.matmul(psb[:, kb * 512:kb * 512 + w_], lhsT=QT[:, qs],
                                                                     rhs=KTm[:, kb * 512:kb * 512 + w_], start=True, stop=True),
                              ["QT", "KTm"], ["psb%d" % kb])
                        allb = ["psb%d" % kb for kb in range(nkb)]
                        if b > 0:
                            pg_ = pseq[:, 384:392]
                            T(lambda e, qs=qs: e.matmul(pg_, lhsT=QT[:, qs], rhs=ksb[:], start=True, stop=True),
                              ["QT", "ksb"], ["pseq"])
                            V(lambda e, b=b: e.tensor_tensor(gm[:], pg_, pbias[:, b, :], ALU.add), ["pseq", "pbias"], ["gm"])
                            V(lambda e: e.max(t8[:], gm[:]), ["gm"], ["t8"])
                            V(lambda e: e.tensor_scalar_max(t8[:, 2:3], t8[:, 2:3], -1e29), ["t8"], ["t8"])
                            V(lambda e: e.tensor_scalar(gm[:], gm[:], t8[:, 2:3], None, op0=ALU.is_ge), ["gm", "t8"], ["gm"])
                            V(lambda e: e.tensor_scalar(gm[:], gm[:], -1.0, 1.0e30, op0=ALU.add, op1=ALU.mult), ["gm"], ["gm"])
                            V(lambda e, b=b: e.tensor_tensor(
                                Sm[:, 0:b * 256].rearrange("p (n t) -> p n t", t=256),
                                psb[:, 0:b * 256].rearrange("p (n t) -> p n t", t=256),
                                gm[:, 0:b].rearrange("p (n o) -> p n o", o=1).to_broadcast([128, b, 256]), ALU.add),
                              allb + ["gm"], ["Sm"])
                        V(lambda e, b=b, i=i: e.tensor_tensor(Sm[:, b * 256:(b + 1) * 256], psb[:, b * 256:(b + 1) * 256],
                                                              cbias[:, i % 2, :], ALU.add), allb + ["cbias"], ["Sm"])
                        yield
                        V(lambda e, nk=nk: e.reduce_max(mx[:, 0:1], Sm[:, 0:nk], AX.X), ["Sm"], ["mx"])
                        V(lambda e: e.tensor_scalar(mx[:, 1:2], mx[:, 0:1], -sc, None, op0=ALU.mult), ["mx"], ["mx"])
                        A(lambda e, nk=nk: e.activation(Ex[:, 0:nk], Sm[:, 0:nk], AF.Exp, bias=mx[:, 1:2], scale=sc,
                                                        accum_out=mx[:, 2:3]), ["Sm", "mx"], ["Ex", "mx"])
                        V(lambda e: e.reciprocal(mx[:, 3:4], mx[:, 2:3]), ["mx"], ["mx"])
                        njt = nk // 128
                        yield
                        for g0 in range(0, njt, 8):
                            g1 = min(njt, g0 + 8)
                            for jt in range(g0, g1):
                                T(lambda e, jt=jt, g0=g0: e.transpose(ptr[:, (jt - g0) * 128:(jt - g0 + 1) * 128],
                                                                      Ex[:, jt * 128:(jt + 1) * 128], ident[:]),
                                  ["Ex", "ident"], ["ptr"])
                            srcp = ptr[:, 0:(g1 - g0) * 128].rearrange("p (j q) -> p j q", q=128)
                            if (g0 // 8) % 2 == 0:
                                V(lambda e, g0=g0, g1=g1, srcp=srcp: e.tensor_copy(ET[:, g0:g1, :], srcp), ["ptr"], ["ET"])
                            else:
                                A(lambda e, g0=g0, g1=g1, srcp=srcp: e.copy(ET[:, g0:g1, :], srcp), ["ptr"], ["ET"])
                            yield
                        po = pseq[:, 128:256]
                        for jt in range(njt):
                            T(lambda e, jt=jt, h=h, njt=njt: e.matmul(po, lhsT=ET[:, jt, :], rhs=vtok[:, jt, h * 128:(h + 1) * 128],
                                                                      start=(jt == 0), stop=(jt == njt - 1)),
                              ["ET", "vtok"], ["pseq"])
                        A(lambda e: e.mul(on[:], po, mx[:, 3:4]), ["pseq", "mx"], ["on"])
                        yield
                        T(lambda e: e.transpose(ptr[:, 0:128], on[:], ident[:]), ["on", "ident"], ["ptr"])
                        mst = B_("mst%d" % (i % 2), [128, 128], BF16)
                        V(lambda e, qs=qs, mst=mst: e.tensor_tensor(mst[:], ptr[:, 0:128], SGm[:, qs], ALU.mult),
                          ["ptr", "SGm"], ["mst%d" % (i % 2)])
                        S.dma("sync", mix_d[i, :, 8 + h, :], mst[:], reads=["mst%d" % (i % 2)], writes=["mixd"])
                        yield


            def gla_gen():
                if "G" not in phases:
                    return
                yield
                gq_, gk_ = {0: 7, 1: 9}, {0: 8, 1: 6}
                gv_, gg_ = {0: 9, 1: 7, 2: 7, 3: 9}, {0: 6, 1: 8, 2: 8, 3: 6}
                srcq = lambda ft: wl[:, 6272 + ft * 128:6272 + (ft + 1) * 128]
                srck = lambda ft: wl[:, 6528 + ft * 128:6528 + (ft + 1) * 128]
                srcv = lambda h: wl[:, 6784 + h * 128:6784 + (h + 1) * 128]
                srcg = lambda h: wl[:, 7296 + h * 128:7296 + (h + 1) * 128]
                wlow = load_w_at(wl[:, 7808:7824], 16, 6)
                load_w_at(srcq(0), 128, gq_[0])
                load_w_at(srck(0), 128, gk_[0])
                load_w_at(srcv(0), 128, gv_[0])
                lowT = B_("lowT", [16, S_], BF16)
                for q in range(4):
                    pt, pk_ = proj_fm(wlow, 16, q)
                    A(lambda e, pt=pt, q=q: e.copy(lowT[:, q * 512:(q + 1) * 512], pt[0:16, :]), [pk_], ["lowT"])
                load_w_at(srcg(0), 128, gg_[0])
                GQ = B_("GQ", [128, S_], BF16)
                GQo = B_("GQo", [64, S_], BF16)
                GKe = B_("GKe", [128, S_], BF16)
                GKd = B_("GKd", [128, S_], BF16)
                gdec = B_("gdec", [128, 32])
                gdeco = B_("gdeco", [64, 32])
                VT = B_("VT", [128, S_], BF16)
                OT = B_("OT", [128, S_])
                SGg = B_("SGg", [128, S_], BF16)
                Sst = B_("Sst", [64, 128])
                Sbf = B_("Sbf", [64, 128], BF16)
                tmg = B_("tmg", [64, 256], BF16)
                att = B_("att", [64, 64], BF16)
                for ft in range(2):
                    wq, wk = gq_[ft], gk_[ft]
                    for q in range(4):
                        sl = slice(q * 512, (q + 1) * 512)
                        pi_ = pjc[0] % 2
                        pjc[0] += 1
                        pla = pproj[pi_][:, :]
                        T(lambda e, sl=sl, ft=ft: e.matmul(pla, lhsT=aupb[:, ft * 128:(ft + 1) * 128], rhs=lowT[:, sl],
                                                           start=True, stop=True), ["aupb", "lowT"], ["pproj%d" % pi_])
                        la = sN(3)
                        A(lambda e, ft=ft: e.activation(la[:], pla, AF.Sigmoid, bias=pc[:, GAB + ft:GAB + ft + 1]),
                          ["pproj%d" % pi_, "pc"], ["s3"])
                        A(lambda e: e.activation(la[:], la[:], AF.Ln), ["s3"], ["s3"])
                        cs = sN(4)
                        V(lambda e: e.tensor_tensor_scan(cs[:], rmask[:], la[:], 0.0, ALU.mult, ALU.add),
                          ["rmask", "s3"], ["s4"])
                        cs3 = cs[:].rearrange("p (c t) -> p c t", t=64)
                        ex = sN(5)
                        A(lambda e: e.activation(ex[:], cs[:], AF.Exp, scale=1.0 / 16), ["s4"], ["s5"])
                        pt, pk_ = proj_fm(wq, 128, q)
                        V(lambda e, pt=pt, sl=sl: e.scalar_tensor_tensor(GQ[:, sl], pt[:], 0.125, ex[:], op0=ALU.mult, op1=ALU.mult),
                          [pk_, "s5"], ["GQ"])
                        G(lambda e, sl=sl: e.tensor_copy(GQo[:, sl], GQ[64:128, sl]), ["GQ"], ["GQo"])
                        eM = sN(6)
                        A(lambda e: e.activation(eM[:], cs[:], AF.Exp, scale=-1.0 / 16), ["s4"], ["s6"])
                        dC = sN(5)
                        V(lambda e: e.tensor_tensor(dC[:].rearrange("p (c t) -> p c t", t=64),
                                                    cs3[:, :, 63:64].to_broadcast([128, 8, 64]), cs3, ALU.subtract),
                          ["s4"], ["s5"])
                        eC = sN(5)
                        A(lambda e: e.activation(eC[:], dC[:], AF.Exp, scale=1.0 / 16), ["s5"], ["s5"])
                        A(lambda e, q=q: e.activation(gdec[:, q * 8:(q + 1) * 8], cs3[:, :, 63], AF.Exp, scale=1.0 / 16),
                          ["s4"], ["gdec"])
                        pt, pk_ = proj_fm(wk, 128, q)
                        kf = sN(9)
                        A(lambda e, pt=pt: e.copy(kf[:], pt[:]), [pk_], ["s9"])
                        V(lambda e, sl=sl: e.tensor_tensor(GKe[:, sl], kf[:], eM[:], ALU.mult), ["s9", "s6"], ["GKe"])
                        V(lambda e, sl=sl: e.tensor_tensor(GKd[:, sl], kf[:], eC[:], ALU.mult), ["s9", "s5"], ["GKd"])
                        yield
                    G(lambda e: e.tensor_copy(gdeco[:, :], gdec[64:128, :]), ["gdec"], ["gdeco"])
                    load_w_at(srcv(2 * ft + 1), 128, gv_[2 * ft + 1])
                    load_w_at(srcg(2 * ft + 1), 128, gg_[2 * ft + 1])
                    for j in range(2):
                        h = ft * 2 + j
                        js = slice(64 * j, 64 * j + 64)
                        wv, wg = gv_[h], gg_[h]
                        for q in range(4):
                            sl = slice(q * 512, (q + 1) * 512)
                            pt, pk_ = proj_fm(wv, 128, q)
                            A(lambda e, pt=pt, sl=sl: e.copy(VT[:, sl], pt[:]), [pk_], ["VT"])
                            pt, pk_ = proj_fm(wg, 128, q)
                            A(lambda e, pt=pt, sl=sl: e.activation(SGg[:, sl], pt[:], AF.Silu), [pk_], ["SGg"])
                            yield
                        if h == 0:
                            load_w_at(srcq(1), 128, gq_[1])
                            load_w_at(srck(1), 128, gk_[1])
                        elif h == 1:
                            load_w_at(srcv(2), 128, gv_[2])
                            load_w_at(srcg(2), 128, gg_[2])
                        G(lambda e: e.memset(Sst[:], 0.0), [], ["Sst"])
                        G(lambda e: e.memset(Sbf[:], 0.0), [], ["Sbf"])
                        Qb, Qbk = (GQ, "GQ") if j == 0 else (GQo, "GQo")
                        Gd, Gdk = (gdec, "gdec") if j == 0 else (gdeco, "gdeco")
                        tmgs = [tmg, B_("tmg1", [64, 256], BF16), B_("tmg2", [64, 256], BF16)]
                        atts = [att, B_("att1", [64, 64], BF16), B_("att2", [64, 64], BF16)]
                        tkeys = ["tmg", "tmg1", "tmg2"]
                        akeys = ["att", "att1", "att2"]

                        def gla_pre(c):
                            c0 = c * 64
                            tm_, at_ = tmgs[c % 3], atts[c % 3]
                            T(lambda e: e.transpose(ptr[0:64, 0:128], VT[:, c0:c0 + 64], ident[:]), ["VT", "ident"], ["ptr"])
                            T(lambda e: e.transpose(ptr[0:64, 128:256], GKd[:, c0:c0 + 64], ident[:]), ["GKd", "ident"], ["ptr"])
                            A(lambda e: e.copy(tm_[:], ptr[0:64, 0:256]), ["ptr"], [tkeys[c % 3]])
                            pat = pseq[0:64, 0:64]
                            T(lambda e: e.matmul(pat, lhsT=GKe[js, c0:c0 + 64], rhs=GQ[js, c0:c0 + 64], start=True, stop=True),
                              ["GKe", "GQ"], ["pseq"])
                            V(lambda e: e.tensor_tensor(at_[:], pat, mskC[:, 0:64], ALU.mult), ["pseq", "mskC"], [akeys[c % 3]])

                        gla_pre(0)
                        gla_pre(1)
                        yield
                        for c in range(32):
                            c0 = c * 64
                            if c + 2 < 32:
                                gla_pre(c + 2)
                                yield
                            tm_, at_ = tmgs[c % 3], atts[c % 3]
                            tk_, ak_ = tkeys[c % 3], akeys[c % 3]
                            po = pseq[:, 64:128]
                            T(lambda e: e.matmul(po, lhsT=tm_[:, 0:128], rhs=at_[:], start=True, stop=False),
                              [tk_, ak_], ["pseq"])
                            T(lambda e: e.matmul(po, lhsT=Sbf[:], rhs=Qb[0:64, c0:c0 + 64], start=False, stop=True),
                              ["Sbf", Qbk], ["pseq"])
                            A(lambda e: e.copy(OT[:, c0:c0 + 64], po), ["pseq"], ["OT"])
                            pst = pseq[0:64, 256:384]
                            T(lambda e: e.matmul(pst, lhsT=tm_[:, 128 + 64 * j:128 + 64 * j + 64], rhs=tm_[:, 0:128],
                                                 start=True, stop=True), [tk_], ["pseq"])
                            V(lambda e: e.scalar_tensor_tensor(Sst[:], Sst[:], Gd[0:64, c:c + 1], pst, op0=ALU.mult,
                                                               op1=ALU.add), ["Sst", Gdk, "pseq"], ["Sst"])
                            A(lambda e: e.copy(Sbf[:], Sst[:]), ["Sst"], ["Sbf"])
                            yield
                        for q in range(4):
                            sl = slice(q * 512, (q + 1) * 512)
                            o2 = B_("o2b", [128, 512], BF16)
                            A(lambda e, sl=sl: e.activation(o2[:], OT[:, sl], AF.Square), ["OT"], ["o2b"])
                            pi_ = pjc[0] % 2
                            pjc[0] += 1
                            pms = pproj[pi_][:, :]
                            T(lambda e: e.matmul(pms, lhsT=ones128[:], rhs=o2[:], start=True, stop=True),
                              ["ones128", "o2b"], ["pproj%d" % pi_])
                            sd = sN(4)
                            A(lambda e: e.activation(sd[:], pms, AF.Ln, bias=epsc[:, 1:2]), ["pproj%d" % pi_, "epsc"], ["s4"])
                            A(lambda e: e.activation(sd[:], sd[:], AF.Exp, scale=-0.5), ["s4"], ["s4"])
                            V(lambda e, sl=sl: e.scalar_tensor_tensor(sd[:], OT[:, sl], pc[:, GNW:GNW + 1], sd[:], op0=ALU.mult,
                                                                     op1=ALU.mult), ["OT", "pc", "s4"], ["s4"])
                            gms = B_("gms0", [128, 512], BF16)
                            V(lambda e, sl=sl, gms=gms: e.tensor_tensor(gms[:], sd[:], SGg[:, sl], ALU.mult),
                              ["s4", "SGg"], ["gms0"])
                            mix_store(gms, "gms0", 12 + h, q)
                            yield


            interleave(moba_gen(), gla_gen())

            if dbg and l == 0:
                for tt in range(16):
                    S.dma("sync", dbg_d[:, :, tt * 128:(tt + 1) * 128], mix_d[tt], reads=["mixd"], writes=["dbgout"])

            phase()
            wo = wout_d[l]
            for fb in range(16 if "D" in phases else 0):
                S.dma("gpsimd", hT[:, :, fb * 128:(fb + 1) * 128],
                      wo[:, fb * 128:(fb + 1) * 128].rearrange("(c p) n -> p c n", p=128), writes=["hT"])
            last = (l == nlayers - 1)
            nwb = B_("nwb", [128, D_])
            xin = B_("xin", [128, D_])
            h16 = B_("h16", [128, D_], BF16)
            st4 = B_("st4", [128, 8])
            if last:
                S.dma("sync", nwb[:], fnw_d[0:1, :].partition_broadcast(128), writes=["nwb"])
            for tt in range(16 if "D" in phases else 0):
                ts_ = slice(tt * 128, (tt + 1) * 128)
                xk_, hk_, sk_ = "xin%d" % (tt % 2), "h16%d" % (tt % 2), "st4%d" % (tt % 2)
                xin = B_(xk_, [128, D_])
                h16 = B_(hk_, [128, D_], BF16)
                st4 = B_(sk_, [128, 8])
                S.dma("sync", xin[:], xsrc[ts_, :], reads=["xs%d" % tt], writes=[xk_])
                mxt = B_("mxt%d" % (tt % 2), [128, 16, 128], BF16)
                mxk = "mxt%d" % (tt % 2)
                S.dma("sync", mxt[:], mix_d[tt], reads=["mixd"], writes=[mxk])
                for dq in range(4):
                    pi = pjc[0] % 2
                    pjc[0] += 1
                    for fc in range(16):
                        T(lambda e, fc=fc, pi=pi, dq=dq, ts_=ts_: e.matmul(pproj[pi][:], lhsT=mxt[:, fc, :],
                                                                         rhs=hT[:, fc, dq * 512:(dq + 1) * 512],
                                                                         start=(fc == 0), stop=(fc == 15)),
                          [mxk, "hT"], ["pproj%d" % pi])
                    V(lambda e, pi=pi, dq=dq: e.tensor_tensor(xin[:, dq * 512:(dq + 1) * 512], pproj[pi][:],
                                                              xin[:, dq * 512:(dq + 1) * 512], ALU.add),
                      ["pproj%d" % pi, xk_], [xk_])
                if not last:
                    S.dma("sync", xs_d[ts_, :], xin[:], reads=[xk_], writes=["xs%d" % tt])
                else:
                    rms_stats(xin, h16, st4, xk_, hk_, sk_)
                    V(lambda e: e.scalar_tensor_tensor(xin[:], xin[:], st4[:, 3:4], nwb[:], op0=ALU.mult, op1=ALU.mult),
                      [xk_, sk_, "nwb"], [xk_])
                    S.dma("sync", out_d[ts_, :], xin[:], reads=[xk_], writes=["out%d" % tt])

        fin = [S.lastw[k] for k in S.lastw if k.startswith("out") or k == "dbgout"]
        S.finish(fin, "sync")
        S.run()
    return nc


def host_inputs(inp):
    f = lambda a: np.ascontiguousarray(np.asarray(a, dtype=np.float32))
    pcs = np.zeros((2, 128, NPC), np.float32)
    phs = np.zeros((2, 64, 32), np.float32)
    for l in range(2):
        tp = lambda v: f(v).reshape(-1, 128).T
        pcs[l, :, MU:MU + 25] = tp(inp["rwkv_mu"][l])
        pcs[l, :, W0:W0 + 8] = tp(inp["rwkv_w0"][l])
        pcs[l, :, A0:A0 + 8] = tp(inp["rwkv_a0"][l])
        pcs[l, :, KK:KK + 8] = tp(inp["rwkv_k_k"][l])
        pcs[l, :, KA:KA + 8] = tp(inp["rwkv_k_a"][l])
        pcs[l, :, RK:RK + 8] = tp(f(inp["rwkv_r_k"][l]).reshape(-1))
        pcs[l, :, GAB:GAB + 2] = tp(inp["gla_a_b"][l])
        pcs[l, :, GNW:GNW + 1] = tp(inp["gla_norm_w"][l])
        pcs[l, :, LNW:LNW + 8] = tp(inp["rwkv_ln_w"][l])
        pcs[l, :, LNB:LNB + 8] = tp(inp["rwkv_ln_b"][l])
        phs[l, :, 0:16] = f(inp["rwkv_ln_w"][l]).reshape(16, 64).T
        phs[l, :, 16:32] = f(inp["rwkv_ln_b"][l]).reshape(16, 64).T
    common = {
        "w_in": f(inp["w_in"]), "w_out": f(inp["w_out"]), "pc": pcs, "ph": phs,
        "norm_w": f(inp["norm_w"]), "final_norm_w": f(inp["final_norm_w"]).reshape(1, D_),
        "rwkv_w2": f(inp["rwkv_w2"]), "rwkv_a2": f(inp["rwkv_a2"]), "gla_a_up": f(inp["gla_a_up"]),
    }
    for n, a in host_consts().items():
        common["c_" + n] = a
    x = f(inp["x"])
    return [dict(common, x=x[b]) for b in range(x.shape[0])]


_NC = {}


def kernel(**inputs):
    in_maps = host_inputs(inputs)
    if "nc" not in _NC:
        _NC["nc"] = build()
    res = run_bass_kernel_spmd(_NC["nc"], in_maps, core_ids=list(range(8)))
    return np.stack([np.asarray(r["out"], dtype=np.float32) for r in res.results], axis=0)
```
